# Optimizing a Trainium2 kernel written in Bass

```python
import jax, jax.numpy as jnp
from jax import lax
import numpy as np

D_MODEL = 2048
BATCH = 4
SEQ = 8192
DEPTH = 4

N_MIXERS = 3

D_FF = -(-8 * D_MODEL // (3 * 256)) * 256

POOL_WINDOWS = (2, 4, 8, 16)
POOL_GROUPS = len(POOL_WINDOWS)
POOL_DIM = D_MODEL
POOL_GROUP_DIM = POOL_DIM // POOL_GROUPS

SGU_CHUNK = 128
SGU_DIM = D_MODEL
SGU_HEADS = 8
SGU_HEAD_DIM = SGU_DIM // SGU_HEADS

GLA_HEADS = 4
GLA_KEY_DIM = D_MODEL // 2
GLA_VALUE_DIM = D_MODEL
GLA_HEAD_K = GLA_KEY_DIM // GLA_HEADS
GLA_HEAD_V = GLA_VALUE_DIM // GLA_HEADS
GLA_GATE_RANK = 16
GLA_GATE_TEMP = 16.0
GLA_CHUNK = 64
GLA_IN_DIM = 2 * GLA_KEY_DIM + 2 * GLA_VALUE_DIM + GLA_GATE_RANK

EPS = 1e-6

kernel_name = "hybrid_pool_sgu_gla_trunk"


def rms_norm(x, w):
    xf = x.astype(jnp.float32)
    y = xf * lax.rsqrt(jnp.mean(xf * xf, axis=-1, keepdims=True) + EPS)
    return y.astype(x.dtype) * w


def layer_norm(x, w, b):
    xf = x.astype(jnp.float32)
    mu = jnp.mean(xf, axis=-1, keepdims=True)
    xc = xf - mu
    y = xc * lax.rsqrt(jnp.mean(xc * xc, axis=-1, keepdims=True) + EPS)
    return y.astype(x.dtype) * w + b


def swiglu_ffn(h, w_gate, w_up, w_down):
    return (jax.nn.silu(h @ w_gate) * (h @ w_up)) @ w_down


def pool_mixer(h, w_in, w_group, scale, w_out):
    z = h @ w_in
    B, S, _ = z.shape
    zf = z.astype(jnp.float32)
    cs = jnp.cumsum(zf, axis=1)
    t = jnp.arange(S)
    outs = []
    for g, w in enumerate(POOL_WINDOWS):
        lo, hi = g * POOL_GROUP_DIM, (g + 1) * POOL_GROUP_DIM
        c = cs[..., lo:hi]
        c_prev = jnp.pad(c, ((0, 0), (w, 0), (0, 0)))[:, :S]
        count = jnp.minimum(t + 1, w).astype(jnp.float32)[None, :, None]
        outs.append((c - c_prev) / count - zf[..., lo:hi])
    p = jnp.stack(outs, axis=2).astype(z.dtype)
    y = jnp.einsum('bsgc,gcd->bsgd', p, w_group).reshape(B, S, POOL_DIM)
    return (y * scale) @ w_out


def sgu_mixer(h, w_in, v_norm_w, v_norm_b, w_spatial, b_spatial, w_out):
    z = jax.nn.gelu(h @ w_in, approximate=False)
    u, v = jnp.split(z, 2, axis=-1)
    v = layer_norm(v, v_norm_w, v_norm_b)
    B, S, _ = v.shape
    nc = S // SGU_CHUNK
    v = v.reshape(B, nc, SGU_CHUNK, SGU_HEADS, SGU_HEAD_DIM)
    mask = jnp.tril(jnp.ones((SGU_CHUNK, SGU_CHUNK), dtype=bool))
    w = jnp.where(mask[None], w_spatial, 0)
    mixed = jnp.einsum('hts,bnshd->bnthd', w, v) + b_spatial.T[None, None, :, :, None]
    gate = mixed.reshape(B, S, SGU_DIM)
    return (u * gate) @ w_out


def gla_mixer(h, w_in, w_gk_up, b_gk, out_norm_w, w_out):
    B, S, _ = h.shape
    z = h @ w_in
    q, k, v, r, gk_low = jnp.split(
        z, [GLA_KEY_DIM, 2 * GLA_KEY_DIM, 2 * GLA_KEY_DIM + GLA_VALUE_DIM,
            2 * GLA_KEY_DIM + 2 * GLA_VALUE_DIM], axis=-1)
    log_a = jax.nn.log_sigmoid((gk_low @ w_gk_up + b_gk).astype(jnp.float32)) / GLA_GATE_TEMP
    nc = S // GLA_CHUNK

    def heads(t, d):
        return t.astype(jnp.float32).reshape(B, nc, GLA_CHUNK, GLA_HEADS, d).transpose(0, 3, 1, 2, 4)

    q = heads(q, GLA_HEAD_K) * (GLA_HEAD_K ** -0.5)
    k = heads(k, GLA_HEAD_K)
    v = heads(v, GLA_HEAD_V)
    g = heads(log_a, GLA_HEAD_K)
    b = jnp.cumsum(g, axis=3)
    b_last = b[:, :, :, -1:, :]
    q_in = q * jnp.exp(b)
    k_in = k * jnp.exp(-b)
    k_dec = k * jnp.exp(b_last - b)

    mask = jnp.tril(jnp.ones((GLA_CHUNK, GLA_CHUNK), dtype=bool))
    scores = jnp.einsum('bhntd,bhnsd->bhnts', q_in, k_in)
    scores = jnp.where(mask, scores, 0.0)
    o_intra = jnp.einsum('bhnts,bhnsv->bhntv', scores, v)

    xs = (jnp.moveaxis(q_in, 2, 0), jnp.moveaxis(k_dec, 2, 0), jnp.moveaxis(v, 2, 0),
          jnp.moveaxis(jnp.exp(b_last[:, :, :, 0, :]), 2, 0))

    def step(state, inp):
        qc, kc, vc, dc = inp
        o = jnp.einsum('bhtd,bhdv->bhtv', qc, state)
        new_state = dc[..., None] * state + jnp.einsum('bhsd,bhsv->bhdv', kc, vc)
        return new_state, o

    s0 = jnp.zeros((B, GLA_HEADS, GLA_HEAD_K, GLA_HEAD_V), jnp.float32)
    _, o_inter = lax.scan(step, s0, xs)
    o = o_intra + jnp.moveaxis(o_inter, 0, 2)
    o = o.transpose(0, 2, 3, 1, 4).reshape(B, S, GLA_HEADS, GLA_HEAD_V)
    o = o * lax.rsqrt(jnp.mean(o * o, axis=-1, keepdims=True) + EPS)
    o = o.reshape(B, S, GLA_VALUE_DIM).astype(h.dtype) * out_norm_w
    o = o * jax.nn.silu(r)
    return o @ w_out


MIXERS = (pool_mixer, sgu_mixer, gla_mixer)


def setup_inputs(seed: int = 0) -> dict:
    key = jax.random.key(seed)
    keys = list(jax.random.split(key, 64))

    def nrm(shape, scale):
        return jax.random.normal(keys.pop(), shape, jnp.float32) * scale

    def gain(n):
        return 1.0 + nrm((n,), 0.02)

    p = {"x": nrm((BATCH, SEQ, D_MODEL), 1.0)}

    def ffn(prefix):
        p[prefix + "_norm2"] = gain(D_MODEL)
        p[prefix + "_ffn_w_gate"] = nrm((D_MODEL, D_FF), D_MODEL ** -0.5)
        p[prefix + "_ffn_w_up"] = nrm((D_MODEL, D_FF), D_MODEL ** -0.5)
        p[prefix + "_ffn_w_down"] = nrm((D_FF, D_MODEL), D_FF ** -0.5)

    def pool(prefix):
        p[prefix + "_norm1"] = gain(D_MODEL)
        p[prefix + "_pool_w_in"] = nrm((D_MODEL, POOL_DIM), D_MODEL ** -0.5)
        p[prefix + "_pool_w_group"] = nrm((POOL_GROUPS, POOL_GROUP_DIM, POOL_GROUP_DIM), POOL_GROUP_DIM ** -0.5)
        p[prefix + "_pool_scale"] = gain(POOL_DIM)
        p[prefix + "_pool_w_out"] = nrm((POOL_DIM, D_MODEL), POOL_DIM ** -0.5)
        ffn(prefix)

    def sgu(prefix):
        p[prefix + "_norm1"] = gain(D_MODEL)
        p[prefix + "_sgu_w_in"] = nrm((D_MODEL, 2 * SGU_DIM), D_MODEL ** -0.5)
        p[prefix + "_sgu_v_norm_w"] = gain(SGU_DIM)
        p[prefix + "_sgu_v_norm_b"] = nrm((SGU_DIM,), 0.02)
        p[prefix + "_sgu_w_spatial"] = nrm((SGU_HEADS, SGU_CHUNK, SGU_CHUNK), SGU_CHUNK ** -0.5)
        p[prefix + "_sgu_b_spatial"] = 1.0 + nrm((SGU_HEADS, SGU_CHUNK), 0.02)
        p[prefix + "_sgu_w_out"] = nrm((SGU_DIM, D_MODEL), SGU_DIM ** -0.5)
        ffn(prefix)

    def gla(prefix):
        p[prefix + "_norm1"] = gain(D_MODEL)
        p[prefix + "_gla_w_in"] = nrm((D_MODEL, GLA_IN_DIM), D_MODEL ** -0.5)
        p[prefix + "_gla_w_gk_up"] = nrm((GLA_GATE_RANK, GLA_KEY_DIM), GLA_GATE_RANK ** -0.5)
        p[prefix + "_gla_b_gk"] = nrm((GLA_KEY_DIM,), 0.1)
        p[prefix + "_gla_out_norm_w"] = gain(GLA_VALUE_DIM)
        p[prefix + "_gla_w_out"] = nrm((GLA_VALUE_DIM, D_MODEL), GLA_VALUE_DIM ** -0.5)
        ffn(prefix)

    pool("l0")
    sgu("l1")
    gla("l2")
    pool("l3")
    p["final_norm_w"] = gain(D_MODEL)
    return p


def reference(x,
              l0_norm1, l0_pool_w_in, l0_pool_w_group, l0_pool_scale, l0_pool_w_out,
              l0_norm2, l0_ffn_w_gate, l0_ffn_w_up, l0_ffn_w_down,
              l1_norm1, l1_sgu_w_in, l1_sgu_v_norm_w, l1_sgu_v_norm_b, l1_sgu_w_spatial,
              l1_sgu_b_spatial, l1_sgu_w_out,
              l1_norm2, l1_ffn_w_gate, l1_ffn_w_up, l1_ffn_w_down,
              l2_norm1, l2_gla_w_in, l2_gla_w_gk_up, l2_gla_b_gk, l2_gla_out_norm_w, l2_gla_w_out,
              l2_norm2, l2_ffn_w_gate, l2_ffn_w_up, l2_ffn_w_down,
              l3_norm1, l3_pool_w_in, l3_pool_w_group, l3_pool_scale, l3_pool_w_out,
              l3_norm2, l3_ffn_w_gate, l3_ffn_w_up, l3_ffn_w_down,
              final_norm_w):
    norm1 = (l0_norm1, l1_norm1, l2_norm1, l3_norm1)
    mixer_params = (
        (l0_pool_w_in, l0_pool_w_group, l0_pool_scale, l0_pool_w_out),
        (l1_sgu_w_in, l1_sgu_v_norm_w, l1_sgu_v_norm_b, l1_sgu_w_spatial, l1_sgu_b_spatial, l1_sgu_w_out),
        (l2_gla_w_in, l2_gla_w_gk_up, l2_gla_b_gk, l2_gla_out_norm_w, l2_gla_w_out),
        (l3_pool_w_in, l3_pool_w_group, l3_pool_scale, l3_pool_w_out),
    )
    norm2 = (l0_norm2, l1_norm2, l2_norm2, l3_norm2)
    ffn_params = (
        (l0_ffn_w_gate, l0_ffn_w_up, l0_ffn_w_down),
        (l1_ffn_w_gate, l1_ffn_w_up, l1_ffn_w_down),
        (l2_ffn_w_gate, l2_ffn_w_up, l2_ffn_w_down),
        (l3_ffn_w_gate, l3_ffn_w_up, l3_ffn_w_down),
    )
    for i in range(DEPTH):
        mixer = MIXERS[i % N_MIXERS]
        x = x + mixer(rms_norm(x, norm1[i]), *mixer_params[i])
        x = x + swiglu_ffn(rms_norm(x, norm2[i]), *ffn_params[i])
    return rms_norm(x, final_norm_w)
```

```python
import contextlib
import numpy as np
import concourse.bass as bass
import concourse.mybir as mybir
from concourse.bass_utils import run_bass_kernel_spmd

F32 = mybir.dt.float32
BF16 = mybir.dt.bfloat16
AF = mybir.ActivationFunctionType
ALU = mybir.AluOpType

D = 2048
DFF = 5632
NJ = DFF // 128
T = 512
NC16 = 16
EPS = 1e-6
ENGS = ("pe", "act", "dve", "pool", "sp")


class Tile:
    __slots__ = ("name", "writers", "readers", "sem", "dma_count")

    def __init__(self, name):
        self.name = name
        self.writers = []
        self.readers = []
        self.sem = None
        self.dma_count = 0


class Op:
    __slots__ = ("eng", "fn", "deps", "is_dma", "sem_tile", "sig_val", "needs_signal", "inc")

    def __init__(self, eng, fn, is_dma, sem_tile, inc=16):
        self.inc = inc
        self.eng = eng
        self.fn = fn
        self.deps = []
        self.is_dma = is_dma
        self.sem_tile = sem_tile
        self.sig_val = None
        self.needs_signal = False


class Prog:
    def __init__(self, nc):
        self.nc = nc
        self.ops = {e: [] for e in ENGS}
        self.stack = contextlib.ExitStack()
        self.all_tiles = []
        self.banks = []
        self.bank_tiles = []
        self.bank_reserved = [False] * 8
        self.bank_rr = 0
        self.last_op = {}
        self.bar_prev = []
        self.bar_done = {"pe": True, "act": True, "dve": True}

    def barrier(self):
        self.bar_prev = [self.last_op[e] for e in ("pe", "act", "dve") if e in self.last_op]
        self.bar_done = {"pe": False, "act": False, "dve": False}

    def sbuf(self, name, shape, dtype):
        return self.stack.enter_context(self.nc.sbuf_tensor(name, list(shape), dtype))

    def tile(self, name):
        t = Tile(name)
        self.all_tiles.append(t)
        return t

    def tiles(self, name, n):
        return [self.tile("%s%d" % (name, i)) for i in range(n)]

    def init_banks(self):
        for i in range(8):
            self.banks.append(self.stack.enter_context(self.nc.psum_tensor("bank%d" % i, [128, 512], F32)))
            self.bank_tiles.append(self.tile("bank%d" % i))

    def bank(self):
        for _ in range(16):
            i = self.bank_rr
            self.bank_rr = (self.bank_rr + 1) % 8
            if not self.bank_reserved[i]:
                return self.banks[i], self.bank_tiles[i]
        raise RuntimeError("no free psum bank")

    def reserve(self, n):
        out = []
        for _ in range(n):
            for _ in range(16):
                i = self.bank_rr
                self.bank_rr = (self.bank_rr + 1) % 8
                if not self.bank_reserved[i]:
                    break
            else:
                raise RuntimeError("no free psum bank")
            self.bank_reserved[i] = True
            out.append(i)
        return out

    def release(self, idxs):
        for i in idxs:
            self.bank_reserved[i] = False

    def add(self, eng, fn, reads=(), writes=(), partial=(), dma=False, sem_tile=None, inc=16):
        op = Op(eng, fn, dma, sem_tile, inc)
        deps = []
        for t in reads:
            deps.extend(t.writers)
        for t in writes:
            deps.extend(t.writers)
            deps.extend(t.readers)
        for t in partial:
            deps.extend(t.readers)
            if t.readers:
                deps.extend(t.writers)
            elif t.writers:
                deps.append(t.writers[0])
        if eng in self.bar_done and not self.bar_done[eng]:
            deps.extend(self.bar_prev)
            self.bar_done[eng] = True
        if eng in self.bar_done:
            self.last_op[eng] = op
        for t in reads:
            t.readers.append(op)
        for t in writes:
            t.writers = [op]
            t.readers = []
        for t in partial:
            if t.readers:
                t.writers = [op]
                t.readers = []
            else:
                t.writers.append(op)
        seen = set()
        for d in deps:
            if d is op or id(d) in seen:
                continue
            seen.add(id(d))
            if d.eng == "pe" and eng == "pe" and not d.is_dma:
                continue
            op.deps.append(d)
            d.needs_signal = True
        if dma:
            op.needs_signal = True
        self.ops[eng].append(op)
        return op

    def emit(self, final_wait_tiles=()):
        nc = self.nc
        stack = self.stack
        eng_sem = {e: stack.enter_context(nc.semaphore("s_" + e)) for e in ("pe", "act", "dve", "pool")}
        for e in ENGS:
            for op in self.ops[e]:
                if op.is_dma and op.sem_tile.sem is None:
                    op.sem_tile.sem = stack.enter_context(nc.semaphore("d_" + op.sem_tile.name))
        cnt = {e: 0 for e in ENGS}
        for e in ENGS:
            for op in self.ops[e]:
                if op.is_dma:
                    op.sem_tile.dma_count += op.inc
                    op.sig_val = (op.sem_tile.sem, op.sem_tile.dma_count)
                elif op.needs_signal:
                    cnt[e] += 1
                    op.sig_val = (eng_sem[e], cnt[e])
        final_waits = [(t.sem, t.dma_count) for t in final_wait_tiles if t.sem is not None]
        ops = self.ops

        def run_engine(e, engine):
            waited = {}
            for op in ops[e]:
                need = {}
                for d in op.deps:
                    s, v = d.sig_val
                    k = id(s)
                    if k not in need or need[k][1] < v:
                        need[k] = (s, v)
                for k, (s, v) in need.items():
                    if waited.get(k, 0) >= v:
                        continue
                    engine.wait_ge(s, v)
                    waited[k] = v
                ins = op.fn(engine)
                if op.sig_val is not None:
                    s, v = op.sig_val
                    ins.then_inc(s, op.inc if op.is_dma else 1)
            if e == "sp":
                for s, v in final_waits:
                    engine.wait_ge(s, v)

        with nc.Block() as block:
            @block.tensor
            def _(eng):
                run_engine("pe", eng)

            @block.scalar
            def _(eng):
                run_engine("act", eng)

            @block.vector
            def _(eng):
                run_engine("dve", eng)

            @block.gpsimd
            def _(eng):
                run_engine("pool", eng)

            @block.sync
            def _(eng):
                run_engine("sp", eng)
        stack.close()


POOL_PARAMS = ("norm1", "pool_w_in", "pool_w_group", "pool_scale", "pool_w_out")
SGU_PARAMS = ("norm1", "sgu_w_in", "sgu_v_norm_w", "sgu_v_norm_b", "sgu_w_spatial", "sgu_b_spatial", "sgu_w_out")
GLA_PARAMS = ("norm1", "gla_w_in", "gla_w_gk_up", "gla_b_gk", "gla_out_norm_w", "gla_w_out")
FFN_PARAMS = ("norm2", "ffn_w_gate", "ffn_w_up", "ffn_w_down")
LAYER_KIND = ("pool", "sgu", "gla", "pool")
NSLOT = 3


class Builder:
    def __init__(self, layers, n_tiles, plan, store_tiles, seq_start_tiles, final_norm=True, pair=False):
        self.pair = pair
        self.layers = layers
        self.n_tiles = n_tiles
        self.plan = plan
        self.store_tiles = store_tiles
        self.seq_start_tiles = seq_start_tiles
        self.final_norm = final_norm
        nc = self.nc = bass.Bass("TRN2", target_bir_lowering=False)
        self.P = Prog(nc)
        self.dram = {}

    def din(self, name, shape, dtype=F32):
        ap = self.nc.dram_tensor(name, list(shape), dtype, kind="ExternalInput").ap()
        self.dram[name] = ap
        return ap

    def wslot(self, src_ap, shape):
        P = self.P
        i = self.slot_rr
        self.slot_rr = (self.slot_rr + 1) % NSLOT
        sl, tl = self.slots[i], self.slot_tiles[i]
        n = 1
        for s in shape[1:]:
            n *= s
        assert shape[0] == 128
        if len(shape) == 3:
            view = sl[0:shape[0], 0:n].rearrange("p (a b) -> p a b", a=shape[1])
        else:
            view = sl[0:shape[0], 0:n]
        f = self.fill_id
        self.fill_id += 1
        if not self.use_scratch:
            P.add("pool", lambda e, view=view, src=src_ap: e.dma_start(out=view, in_=src), writes=[tl], dma=True, sem_tile=tl)
        elif f not in self.scr:
            P.add("pool", lambda e, view=view, src=src_ap: e.dma_start(out=view, in_=src), writes=[tl], dma=True, sem_tile=tl)
            scr = self.nc.dram_tensor("wscr%d" % f, [128, n], BF16).ap()
            st = P.tile("wscr%d" % f)
            self.scr[f] = (scr, st, n)
            P.add("sp", lambda e, scr=scr, sl=sl, n=n: e.dma_start(out=scr, in_=sl[:, 0:n]), reads=[tl], writes=[st], dma=True, sem_tile=self.scr_sem[i])
        else:
            scr, st, n2 = self.scr[f]
            assert n2 == n, (f, n, n2)
            P.add("pool", lambda e, scr=scr, sl=sl, n=n: e.dma_start(out=sl[:, 0:n], in_=scr), reads=[st], writes=[tl], dma=True, sem_tile=tl)
        return view, tl

    def setup(self):
        P = self.P
        nc = self.nc
        L = self.layers
        P.init_banks()
        self.warm_bank = P.reserve(1)[0]
        self.x_in = self.din("x_in", [self.n_tiles, 128, NC16, T])
        n_store = len(self.store_tiles)
        self.y_out = nc.dram_tensor("y_out", [max(n_store, 1), 128, NC16, T], F32, kind="ExternalOutput").ap()
        self.x = P.sbuf("x", [128, NC16, T], F32)
        self.xt = P.tiles("x", NC16)
        self.h = P.sbuf("h", [128, NC16, T], BF16)
        self.ht = P.tiles("h", NC16)
        self.slots = [P.sbuf("slot%d" % i, [128, 4096], BF16) for i in range(NSLOT)]
        self.slot_tiles = P.tiles("slot", NSLOT)
        self.slot_rr = 0
        self.fill_id = 0
        self.scr = {}
        self.scr_sem = P.tiles("scrsem", NSLOT)
        self.use_scratch = self.pair
        self.t_x_dma = P.tile("xdma")
        self.arena = P.sbuf("arena", [128, 44 * T], BF16)
        self.arena2 = P.sbuf("arena2", [128, 8448], F32)
        self.sq = [P.sbuf("sq%d" % i, [128, 2, T], BF16) for i in range(2)]
        self.sq_t = P.tiles("sq", 2)
        self.sq_rr = 0
        self.rs_a = P.sbuf("rs_a", [128, T], F32)
        self.rs_at = P.tile("rs_a")
        self.rstd = P.sbuf("rstd", [128, T], F32)
        self.rstd_t = P.tile("rstd")
        self.ones = P.sbuf("ones", [128, 128], BF16)
        self.ones_t = P.tile("ones")
        self.warm = P.sbuf("warm", [128, T], BF16)
        P.add("dve", lambda e: e.memset(self.warm[:], 1.0), writes=[self.ones_t])
        P.add("dve", lambda e: e.memset(self.ones[:], 1.0), reads=[self.ones_t], writes=[self.ones_t])
        self.vec = {}
        self.vec_t = P.tile("vecs")
        names = []
        for l in L:
            names += ["l%d_norm1" % l, "l%d_norm2" % l]
            k = LAYER_KIND[l]
            if k == "pool":
                names.append("l%d_pool_scale" % l)
            if k == "gla":
                names.append("l%d_gla_out_norm_w" % l)
            if k == "sgu":
                names += ["l%d_sgu_v_norm_w" % l, "l%d_sgu_v_norm_b" % l]
        if self.final_norm:
            names.append("final_norm_w")
        vb = P.sbuf("vecbuf", [128, len(names), NC16], F32)
        for i, nm in enumerate(names):
            src = self.din("v_" + nm, [128, NC16])
            self.vec[nm] = vb[:, i, :]
            P.add("sp", lambda e, i=i, src=src: e.dma_start(out=vb[:, i, :], in_=src), partial=[self.vec_t], dma=True, sem_tile=self.vec_t)
        for l in L:
            k = LAYER_KIND[l]
            pre = "l%d_" % l
            for nm in FFN_PARAMS[1:]:
                shp = [D, DFF] if nm != "ffn_w_down" else [DFF, D]
                self.din(pre + nm, shp)
            if k == "pool":
                self.din(pre + "pool_w_in", [D, D])
                self.din(pre + "pool_w_group", [4, 512, 512])
                self.din(pre + "pool_w_out", [D, D])
            elif k == "sgu":
                self.din(pre + "sgu_w_in", [D, 2 * D])
                self.din(pre + "sgu_w_out", [D, D])
            elif k == "gla":
                self.din(pre + "gla_w_in", [D, 6160])
                self.din(pre + "gla_w_out", [D, D])
        if self.pair:
            self.fl, self.fl_t = self.const_load("flags", [128, 2])
            self.xs_src = P.tiles("xsrc", 8)
            self.xs_cc = P.tiles("xcc", 8)
            self.xs_n = 0
            self.xbuf = {}
        self.setup_pool()
        self.setup_sgu()
        self.setup_gla()
        P.barrier()

    def const_load(self, name, shape, dtype_sb=F32, eng="sp"):
        P = self.P
        src = self.din(name, shape)
        buf = P.sbuf("c_" + name, shape, dtype_sb)
        t = P.tile("c_" + name)
        P.add(eng, lambda e: e.dma_start(out=buf[:], in_=src), writes=[t], dma=True, sem_tile=t)
        return buf, t

    def rmsnorm(self, wvec, dst, dst_tiles, nfeat=D):
        P = self.P
        x, xt = self.x, self.xt
        self.pe_warm(40)
        bk, bt = P.bank()
        for g4 in range(8):
            i = self.sq_rr
            self.sq_rr ^= 1
            sq, sqt = self.sq[i], self.sq_t[i]
            P.add("act", lambda e, g4=g4, sq=sq: e.activation(sq[:], x[:, 2 * g4:2 * g4 + 2, :], AF.Square),
                  reads=xt[2 * g4:2 * g4 + 2], writes=[sqt])

            def mm(e, g4=g4, sq=sq):
                ins = None
                for c in range(2):
                    ins = e.matmul(bk[:], self.ones[:], sq[:, c, :], start=(g4 == 0 and c == 0), stop=(g4 == 7 and c == 1))
                return ins
            P.add("pe", mm, reads=[sqt, self.ones_t], partial=[bt] if g4 else (), writes=() if g4 else [bt])
        P.add("act", lambda e: e.activation(self.rs_a[:], bk[:], AF.Sqrt, scale=1.0 / nfeat, bias=EPS),
              reads=[bt], writes=[self.rs_at])
        if dst is not self.x:
            self.pe_warm(100)
        P.add("dve", lambda e: e.reciprocal(self.rstd[:], self.rs_a[:]), reads=[self.rs_at], writes=[self.rstd_t])
        for c in range(NC16):
            P.add("dve", lambda e, c=c: e.scalar_tensor_tensor(dst[:, c, :], x[:, c, :], wvec[:, c:c + 1], self.rstd[:], ALU.mult, ALU.mult),
                  reads=[xt[c], self.rstd_t, self.vec_t], writes=[dst_tiles[c]])

    def pe_warm(self, n):
        P = self.P
        bk = P.banks[self.warm_bank]

        def mm(e, bk=bk):
            ins = None
            for _ in range(n):
                ins = e.matmul(bk[:], self.ones[:], self.warm[:], start=True, stop=True)
            return ins
        P.add("pe", mm, reads=[self.ones_t])

    def proj_fm(self, W, c0, ncols, src, src_tiles, nk=NC16):
        P = self.P
        Wv = W.rearrange("(k p) n -> p k n", p=128)
        for s0 in range(0, ncols, 256):
            w = min(256, ncols - s0)
            view, tl = self.wslot(Wv[:, :, c0 + s0:c0 + s0 + w], [128, nk, w])
            for mi in range(w // 128):
                bk, bt = P.bank()

                def mm(e, view=view, mi=mi, bk=bk):
                    ins = None
                    for k in range(nk):
                        ins = e.matmul(bk[:], view[:, k, mi * 128:(mi + 1) * 128], src[:, k, :], start=(k == 0), stop=(k == nk - 1))
                    return ins
                P.add("pe", mm, reads=[tl] + list(src_tiles), writes=[bt])
                yield (s0 // 128 + mi, bk, bt)

    def proj_tm(self, W, n0, w, src, src_tiles):
        P = self.P
        Wv = W.rearrange("(k p) n -> p k n", p=128)
        if w == 512:
            v0, t0 = self.wslot(Wv[:, 0:8, n0:n0 + w], [128, 8, w])
            v1, t1 = self.wslot(Wv[:, 8:16, n0:n0 + w], [128, 8, w])
            views = [(v0, t0), (v1, t1)]
            per = 8
        else:
            v0, t0 = self.wslot(Wv[:, :, n0:n0 + w], [128, 16, w])
            views = [(v0, t0)]
            per = 16
        for b in range(4):
            bk, bt = P.bank()

            def mm(e, b=b, bk=bk):
                ins = None
                for k in range(NC16):
                    vw = views[k // per][0]
                    ins = e.matmul(bk[:, 0:w], src[:, k, b * 128:(b + 1) * 128], vw[:, k % per, :], start=(k == 0), stop=(k == NC16 - 1))
                return ins
            P.add("pe", mm, reads=[v[1] for v in views] + list(src_tiles), writes=[bt])
            yield (b, bk, bt)

    def out_proj_residual(self, W, src, src_tiles):
        P = self.P
        for m, bk, bt in self.proj_fm(W, 0, D, src, src_tiles):
            P.add("dve", lambda e, m=m, bk=bk: e.tensor_tensor(self.x[:, m, :], self.x[:, m, :], bk[:], ALU.add),
                  reads=[bt, self.xt[m]], writes=[self.xt[m]])

    def ffn(self, l):
        P = self.P
        pre = "l%d_" % l
        P.barrier()
        self.rmsnorm(self.vec[pre + "norm2"], self.h, self.ht)
        Wg = self.dram[pre + "ffn_w_gate"].rearrange("(k p) n -> p k n", p=128)
        Wu = self.dram[pre + "ffn_w_up"].rearrange("(k p) n -> p k n", p=128)
        Wd = self.dram[pre + "ffn_w_down"].rearrange("(j p) n -> p j n", p=128)
        act = self.arena[:, :].rearrange("p (j t) -> p j t", j=NJ)
        act_t = self.act_t
        sg, sg_t = self.sg, self.sg_t
        for jp in range(NJ // 2):
            gv, gt = self.wslot(Wg[:, :, jp * 256:(jp + 1) * 256], [128, 16, 256])
            uv, ut = self.wslot(Wu[:, :, jp * 256:(jp + 1) * 256], [128, 16, 256])
            for ji in range(2):
                j = jp * 2 + ji
                bg, bgt = P.bank()
                bu, but = P.bank()

                def mmg(e, v=gv, ji=ji, bk=bg):
                    ins = None
                    for k in range(NC16):
                        ins = e.matmul(bk[:], v[:, k, ji * 128:(ji + 1) * 128], self.h[:, k, :], start=(k == 0), stop=(k == NC16 - 1))
                    return ins
                P.add("pe", mmg, reads=[gt] + self.ht, writes=[bgt])
                P.add("pe", lambda e, v=uv, ji=ji, bk=bu: mmg(e, v, ji, bk), reads=[ut] + self.ht, writes=[but])
                i = j % 2
                P.add("act", lambda e, i=i, bk=bg: e.activation(sg[i][:], bk[:], AF.Silu), reads=[bgt], writes=[sg_t[i]])
                P.add("dve", lambda e, i=i, j=j, bk=bu: e.tensor_tensor(act[:, j, :], sg[i][:], bk[:], ALU.mult),
                      reads=[sg_t[i], but], writes=[act_t[j]])
        groups = [(0, 8), (8, 8), (16, 8), (24, 8), (32, 8), (40, 4)]
        for q in range(4):
            bidx = P.reserve(4)
            for (j0, nj) in groups:
                dv, dt = self.wslot(Wd[:, j0:j0 + nj, q * 512:(q + 1) * 512], [128, nj, 512])

                def mmd(e, dv=dv, j0=j0, nj=nj, bidx=bidx):
                    ins = None
                    for jj in range(nj):
                        j = j0 + jj
                        for m in range(4):
                            ins = e.matmul(P.banks[bidx[m]][:], dv[:, jj, m * 128:(m + 1) * 128], act[:, j, :],
                                           start=(j == 0), stop=(j == NJ - 1))
                    return ins
                bts = [P.bank_tiles[i] for i in bidx]
                P.add("pe", mmd, reads=[dt] + act_t[j0:j0 + nj], writes=bts if j0 == 0 else (), partial=() if j0 == 0 else bts)
            for m in range(4):
                c = 4 * q + m
                P.add("dve", lambda e, c=c, bk=P.banks[bidx[m]]: e.tensor_tensor(self.x[:, c, :], self.x[:, c, :], bk[:], ALU.add),
                      reads=[P.bank_tiles[bidx[m]], self.xt[c]], writes=[self.xt[c]])
            P.release(bidx)

    def setup_pool(self):
        P = self.P
        self.act_t = P.tiles("act", NJ)
        self.sg = [P.sbuf("sg%d" % i, [128, T], F32) for i in range(2)]
        self.sg_t = P.tiles("sg", 2)
        if not any(LAYER_KIND[l] == "pool" for l in self.layers):
            return
        self.halo = {}
        self.halo_t = {}
        for l in self.layers:
            if LAYER_KIND[l] == "pool":
                self.halo[l] = P.sbuf("halo%d" % l, [128, NC16, 16], F32)
                self.halo_t[l] = P.tiles("halo%d_" % l, 4)
                for g in range(4):
                    P.add("dve", lambda e, l=l, g=g: e.memset(self.halo[l][:, 4 * g:4 * g + 4, :], 0.0), writes=[self.halo_t[l][g]])
        self.invc, self.invc_t = self.const_load("invc", [128, len(self.seq_start_tiles), 4, 16])
        self.pool_t = dict(z=P.tiles("pz", 2), a=P.tile("pa"), b=P.tile("pb"), p=P.tiles("pp", NC16), ys=P.tiles("pys", NC16))
        if self.pair:
            self.tail = P.sbuf("ptail", [128, NC16, 16], F32)
            self.tail_t = P.tiles("ptail", 4)
            self.zf = P.sbuf("pzf", [128, NC16, 16], F32)
            self.zf_t = P.tiles("pzf", 4)
            self.Hy = P.sbuf("pHy", [128, NC16, 16], F32)
            self.Hy_t = P.tile("pHy")
            self.pw = P.sbuf("ppatch", [128, 3, 4, 32], F32)
            self.pyg = P.sbuf("ppyg", [128, 4, 16], F32)
            self.patch_t = P.tile("ppatch")
            self.halo_dma_t = {l: P.tile("halodma%d" % l) for l in self.halo}

    def pool_mixer(self, l, ti, mode):
        P = self.P
        pre = "l%d_" % l
        P.barrier()
        self.rmsnorm(self.vec[pre + "norm1"], self.h, self.ht)
        W_in = self.dram[pre + "pool_w_in"]
        zb = [self.arena2[:, i * 2112:(i + 1) * 2112].rearrange("p (c t) -> p c t", c=4) for i in range(2)]
        A = self.arena2[:, 2 * 2112:3 * 2112].rearrange("p (c t) -> p c t", c=4)
        B = self.arena2[:, 3 * 2112:4 * 2112].rearrange("p (c t) -> p c t", c=4)
        pt = self.pool_t
        p_buf = self.arena[:, 0:NC16 * T].rearrange("p (c t) -> p c t", c=NC16)
        ys = self.arena[:, NC16 * T:2 * NC16 * T].rearrange("p (c t) -> p c t", c=NC16)
        halo, halo_t = self.halo[l], self.halo_t[l]
        gen = self.proj_fm(W_in, 0, D, self.h, self.ht)
        for g in range(4):
            z, zt = zb[g % 2], pt["z"][g % 2]
            first = True
            for _ in range(4):
                m, bk, bt = next(gen)
                mi = m % 4
                if first:
                    P.add("dve", lambda e, z=z, g=g: e.tensor_copy(z[:, :, 0:16], halo[:, 4 * g:4 * g + 4, :]),
                          reads=[halo_t[g]], writes=[zt])
                    first = False
                P.add("act", lambda e, z=z, mi=mi, bk=bk: e.activation(z[:, mi, 16:528], bk[:], AF.Copy), reads=[bt], partial=[zt])
            if self.pair:
                P.add("dve", lambda e, z=z, g=g: e.tensor_copy(self.tail[:, 4 * g:4 * g + 4, :], z[:, :, 512:528]), reads=[zt], writes=[self.tail_t[g]])
                P.add("dve", lambda e, z=z, g=g: e.tensor_copy(self.zf[:, 4 * g:4 * g + 4, :], z[:, :, 16:32]), reads=[zt], writes=[self.zf_t[g]])
            else:
                P.add("dve", lambda e, z=z, g=g: e.tensor_copy(halo[:, 4 * g:4 * g + 4, :], z[:, :, 512:528]), reads=[zt], writes=[halo_t[g]])
            if mode != "full":
                continue
            ta, tb = pt["a"], pt["b"]
            P.add("dve", lambda e, z=z: e.tensor_tensor(A[:, :, 1:528], z[:, :, 1:528], z[:, :, 0:527], ALU.add), reads=[zt], writes=[ta])
            cur, curt = A, ta
            if g >= 1:
                P.add("dve", lambda e: e.tensor_tensor(B[:, :, 3:528], A[:, :, 3:528], A[:, :, 1:526], ALU.add), reads=[ta], writes=[tb])
                cur, curt = B, tb
            if g >= 2:
                P.add("dve", lambda e: e.tensor_tensor(A[:, :, 7:528], B[:, :, 7:528], B[:, :, 3:524], ALU.add), reads=[tb], writes=[ta])
                cur, curt = A, ta
            if g >= 3:
                P.add("dve", lambda e: e.tensor_tensor(B[:, :, 15:528], A[:, :, 15:528], A[:, :, 7:520], ALU.add), reads=[ta], writes=[tb])
                cur, curt = B, tb
            w = 2 ** (g + 1)
            ptl = pt["p"][4 * g:4 * g + 4]
            P.add("dve", lambda e, cur=cur, z=z, g=g, w=w: e.scalar_tensor_tensor(p_buf[:, 4 * g:4 * g + 4, :], cur[:, :, 16:528], 1.0 / w, z[:, :, 16:528], ALU.mult, ALU.subtract),
                  reads=[curt, zt], writes=ptl)
            if ti in self.seq_start_tiles:
                si = self.seq_start_tiles.index(ti)
                oth, otht = (B, tb) if cur is A else (A, ta)

                def fix1(e, cur=cur, oth=oth, g=g, si=si):
                    iv = self.invc[:, si, g, :].unsqueeze(1).to_broadcast([128, 4, 16])
                    return e.tensor_tensor(oth[:, :, 0:16], cur[:, :, 16:32], iv, ALU.mult)
                P.add("dve", fix1, reads=[curt, self.invc_t], writes=[otht])
                P.add("dve", lambda e, oth=oth, z=z, g=g: e.tensor_tensor(p_buf[:, 4 * g:4 * g + 4, 0:16], oth[:, :, 0:16], z[:, :, 16:32], ALU.subtract),
                      reads=[otht, zt], writes=ptl)
        if mode != "full":
            return
        if self.pair:
            self.pool_exchange_patch(l, ti, p_buf)
            self.pe_warm(110)
        Wgrp = self.dram[pre + "pool_w_group"]
        scale = self.vec[pre + "pool_scale"]
        for g in range(4):
            wv, wt = self.wslot(Wgrp[g].rearrange("(c p) d -> p c d", p=128), [128, 4, 512])
            for dch in range(4):
                bk, bt = P.bank()

                def mm(e, wv=wv, dch=dch, bk=bk, g=g):
                    ins = None
                    for c in range(4):
                        ins = e.matmul(bk[:], wv[:, c, dch * 128:(dch + 1) * 128], p_buf[:, 4 * g + c, :], start=(c == 0), stop=(c == 3))
                    return ins
                P.add("pe", mm, reads=[wt] + pt["p"][4 * g:4 * g + 4], writes=[bt])
                m = 4 * g + dch
                P.add("act", lambda e, m=m, bk=bk: e.activation(ys[:, m, :], bk[:], AF.Copy, scale=scale[:, m:m + 1]),
                      reads=[bt, self.vec_t], writes=[pt["ys"][m]])
        self.out_proj_residual(self.dram[pre + "pool_w_out"], ys, pt["ys"])

    def pool_exchange_patch(self, l, ti, p_buf):
        P = self.P
        nc = self.nc
        pt = self.pool_t
        halo, halo_t = self.halo[l], self.halo_t[l]
        if l not in self.xbuf:
            self.xbuf[l] = (nc.dram_tensor("ph_src_%d" % l, [128, 256], F32).ap(), nc.dram_tensor("ph_dst_%d" % l, [256, 256], F32).ap(),
                            P.tile("phs%d" % l), P.tile("phd%d" % l))
        src, dst, t_src, t_dst = self.xbuf[l]
        hs, hc = self.xs_src[self.xs_n % 8], self.xs_cc[self.xs_n % 8]
        self.xs_n += 1
        P.add("sp", lambda e: e.dma_start(out=src.rearrange("p (c t) -> p c t", c=NC16), in_=self.tail[:]), reads=self.tail_t, writes=[t_src], dma=True, sem_tile=hs)
        P.add("pool", lambda e: e.collective_compute("AllGather", ALU.bypass, replica_groups=PAIR_GROUPS, ins=[src], outs=[dst]),
              reads=[t_src], writes=[t_dst], dma=True, sem_tile=hc, inc=1)
        P.add("sp", lambda e: e.dma_start(out=self.Hy[:], in_=dst[0:128, :].rearrange("p (c t) -> p c t", c=NC16)), reads=[t_dst], writes=[self.Hy_t], dma=True, sem_tile=self.Hy_t)
        P.add("sp", lambda e: e.dma_start(out=halo[:], in_=dst[128:256, :].rearrange("p (c t) -> p c t", c=NC16)), reads=[t_dst], writes=halo_t, dma=True, sem_tile=self.halo_dma_t[l])
        Eg, Ag, Bg = self.pw[:, 0], self.pw[:, 1], self.pw[:, 2]
        pyg = self.pyg
        ptc = self.patch_t
        fX, fY = self.fl[:, 0:1], self.fl[:, 1:2]
        for g in range(4):
            cs = slice(4 * g, 4 * g + 4)
            P.add("dve", lambda e, cs=cs: e.tensor_copy(Eg[:, :, 0:16], self.Hy[:, cs, :]), reads=[self.Hy_t, ptc], writes=[ptc])
            P.add("dve", lambda e, cs=cs: e.tensor_copy(Eg[:, :, 16:32], self.zf[:, cs, :]), reads=[self.zf_t[g], ptc], writes=[ptc])
            P.add("dve", lambda e: e.tensor_tensor(Ag[:, :, 1:32], Eg[:, :, 1:32], Eg[:, :, 0:31], ALU.add), reads=[ptc], writes=[ptc])
            cur = Ag
            if g >= 1:
                P.add("dve", lambda e: e.tensor_tensor(Bg[:, :, 3:32], Ag[:, :, 3:32], Ag[:, :, 1:30], ALU.add), reads=[ptc], writes=[ptc])
                cur = Bg
            if g >= 2:
                P.add("dve", lambda e: e.tensor_tensor(Ag[:, :, 7:32], Bg[:, :, 7:32], Bg[:, :, 3:28], ALU.add), reads=[ptc], writes=[ptc])
                cur = Ag
            if g >= 3:
                P.add("dve", lambda e: e.tensor_tensor(Bg[:, :, 15:32], Ag[:, :, 15:32], Ag[:, :, 7:24], ALU.add), reads=[ptc], writes=[ptc])
                cur = Bg
            w = 2 ** (g + 1)
            P.add("dve", lambda e, cur=cur, cs=cs, w=w: e.scalar_tensor_tensor(pyg[:], cur[:, :, 16:32], 1.0 / w, self.zf[:, cs, :], ALU.mult, ALU.subtract),
                  reads=[ptc, self.zf_t[g]], writes=[ptc])
            P.add("dve", lambda e: e.tensor_scalar(pyg[:], pyg[:], fY, None, ALU.mult), reads=[ptc, self.fl_t], writes=[ptc])
            P.add("dve", lambda e, cs=cs: e.scalar_tensor_tensor(p_buf[:, cs, 0:16], p_buf[:, cs, 0:16], fX, pyg[:], ALU.mult, ALU.add),
                  reads=[ptc, self.fl_t] + pt["p"][4 * g:4 * g + 4], writes=pt["p"][4 * g:4 * g + 4])

    def setup_sgu(self):
        P = self.P
        ls = [l for l in self.layers if LAYER_KIND[l] == "sgu"]
        if not ls:
            return
        l = ls[0]
        pre = "l%d_" % l
        a2 = self.arena2
        wst = a2[:, 0:1024].rearrange("p (h t) -> p h t", h=8)
        bsp = a2[:, 1024:2048].rearrange("p (h t) -> p h t", h=8)
        Rsb = a2[:, 2048:3072].rearrange("p (h t) -> p h t", h=8)
        wst_t, bsp_t, R_t = P.tile("wst"), P.tile("bsp"), P.tile("Rsb")
        wst_d = self.din(pre + "ws_t", [128, 8, 128])
        bsp_d = self.din(pre + "bsp_b", [128, 8, 128])
        P.add("sp", lambda e: e.dma_start(out=wst, in_=wst_d), writes=[wst_t], dma=True, sem_tile=wst_t)
        P.add("sp", lambda e: e.dma_start(out=bsp, in_=bsp_d), writes=[bsp_t], dma=True, sem_tile=bsp_t)
        msk, msk_t = self.const_load("mask_tri", [128, 128])
        self.wsb = P.sbuf("wsb", [128, 8, 128], BF16)
        self.wsb_t = P.tile("wsb")
        P.add("dve", lambda e: e.tensor_tensor(self.wsb[:], wst, msk[:].unsqueeze(1).to_broadcast([128, 8, 128]), ALU.mult),
              reads=[wst_t, msk_t], writes=[self.wsb_t])
        for half in range(2):
            bk, bt = P.bank()

            def mm(e, bk=bk, half=half):
                ins = None
                for hh in range(4):
                    ins = e.matmul(bk[:, hh * 128:(hh + 1) * 128], self.ones[:], self.wsb[:, half * 4 + hh, :], start=True, stop=True)
                return ins
            P.add("pe", mm, reads=[self.ones_t, self.wsb_t], writes=[bt])
            P.add("act", lambda e, bk=bk, half=half: e.activation(Rsb[:, half * 4:half * 4 + 4, :], bk[:].rearrange("p (h t) -> p h t", h=4), AF.Copy),
                  reads=[bt], partial=[R_t])
        self.Bt = P.sbuf("Bt", [128, NC16, 128], F32)
        self.Bt_t = P.tile("Bt")
        vnb = self.vec[pre + "sgu_v_norm_b"]
        for fc in range(NC16):
            hd = fc // 2
            P.add("dve", lambda e, fc=fc, hd=hd: e.scalar_tensor_tensor(self.Bt[:, fc, :], Rsb[:, hd, :], vnb[:, fc:fc + 1], bsp[:, hd, :], ALU.mult, ALU.add),
                  reads=[R_t, bsp_t, self.vec_t], partial=[self.Bt_t])
        self.sgu_t = dict(u=P.tiles("su", NC16), vraw=[P.tiles("svr%d_" % b, 4) for b in range(4)], vn=P.tiles("svn", 4),
                          st=P.tiles("sst", 4))
        self.bnst = P.sbuf("bnst", [128, 4, 4 * 6], F32)
        self.mv = P.sbuf("mv", [128, 4, 4], F32)

    def sgu_mixer(self, l):
        P = self.P
        pre = "l%d_" % l
        st_ = self.sgu_t
        P.barrier()
        self.rmsnorm(self.vec[pre + "norm1"], self.h, self.ht)
        W_in = self.dram[pre + "sgu_w_in"]
        u = self.arena[:, 0:NC16 * T].rearrange("p (c t) -> p c t", c=NC16)
        vn = self.arena[:, NC16 * T:NC16 * T + 4 * D].rearrange("p (b f) -> p b f", b=4)
        vraw = self.arena2[:, 0:4 * D].rearrange("p (b f) -> p b f", b=4)
        for m, bk, bt in self.proj_fm(W_in, 0, D, self.h, self.ht):
            P.add("act", lambda e, m=m, bk=bk: e.activation(u[:, m, :], bk[:], AF.Gelu), reads=[bt], writes=[st_["u"][m]])
        for n in range(4):
            for b, bk, bt in self.proj_tm(W_in, D + n * 512, 512, self.h, self.ht):
                P.add("act", lambda e, b=b, n=n, bk=bk: e.activation(vraw[:, b, n * 512:(n + 1) * 512], bk[:], AF.Gelu),
                      reads=[bt], writes=[st_["vraw"][b][n]])
        for b in range(4):
            vt = st_["vraw"][b]
            stt = st_["st"][b]
            for n in range(4):
                P.add("dve", lambda e, b=b, n=n: e.bn_stats(self.bnst[:, b, n * 6:(n + 1) * 6], vraw[:, b, n * 512:(n + 1) * 512]),
                      reads=[vt[n]], partial=[stt] if n else (), writes=() if n else [stt])
            P.add("dve", lambda e, b=b: e.bn_aggr(self.mv[:, b, 0:2], self.bnst[:, b, :]), reads=[stt], writes=[stt])
            P.add("act", lambda e, b=b: e.activation(self.mv[:, b, 2:3], self.mv[:, b, 1:2], AF.Sqrt, bias=EPS), reads=[stt], writes=[stt])
            P.add("dve", lambda e, b=b: e.reciprocal(self.mv[:, b, 3:4], self.mv[:, b, 2:3]), reads=[stt], writes=[stt])
            P.add("dve", lambda e, b=b: e.tensor_scalar(vn[:, b, :], vraw[:, b, :], self.mv[:, b, 0:1], self.mv[:, b, 3:4], ALU.subtract, ALU.mult),
                  reads=[stt] + vt, writes=[st_["vn"][b]])
        self.pe_warm(90)
        ug = self.h
        gt_ = self.sg
        vnw = self.vec[pre + "sgu_v_norm_w"]
        for fc in range(NC16):
            hd = fc // 2
            bk, bt = P.bank()

            def mm(e, fc=fc, hd=hd, bk=bk):
                ins = None
                for b in range(4):
                    ins = e.matmul(bk[:, b * 128:(b + 1) * 128], vn[:, b, fc * 128:(fc + 1) * 128], self.wsb[:, hd, :], start=True, stop=True)
                return ins
            P.add("pe", mm, reads=st_["vn"] + [self.wsb_t], writes=[bt])
            i = fc % 2

            def gate(e, i=i, fc=fc, bk=bk):
                return e.scalar_tensor_tensor(gt_[i][:].rearrange("p (b t) -> p b t", b=4), bk[:].rearrange("p (b t) -> p b t", b=4),
                                              vnw[:, fc:fc + 1], self.Bt[:, fc, :].unsqueeze(1).to_broadcast([128, 4, 128]), ALU.mult, ALU.add)
            P.add("dve", gate, reads=[bt, self.Bt_t, self.vec_t], writes=[self.sg_t[i]])
            P.add("dve", lambda e, i=i, fc=fc: e.tensor_tensor(ug[:, fc, :], gt_[i][:], u[:, fc, :], ALU.mult),
                  reads=[self.sg_t[i], st_["u"][fc]], writes=[self.ht[fc]])
        self.out_proj_residual(self.dram[pre + "sgu_w_out"], ug, self.ht)

    def setup_gla(self):
        P = self.P
        ls = [l for l in self.layers if LAYER_KIND[l] == "gla"]
        if not ls:
            return
        l = ls[0]
        pre = "l%d_" % l
        self.wgk = P.sbuf("wgk", [128, NC16, 16], BF16)
        self.wgk_t = P.tile("wgk")
        W_in = self.dram[pre + "gla_w_in"]
        P.add("pool", lambda e: e.dma_start(out=self.wgk[:], in_=W_in.rearrange("(k p) n -> p k n", p=128)[:, :, 6144:6160]),
              writes=[self.wgk_t], dma=True, sem_tile=self.wgk_t)
        self.wup = P.sbuf("wup", [16, 1024], BF16)
        self.wup_t = P.tile("wup")
        wup_d = self.din(pre + "gla_w_gk_up", [16, 1024])
        P.add("pool", lambda e: e.dma_start(out=self.wup[:], in_=wup_d), writes=[self.wup_t], dma=True, sem_tile=self.wup_t)
        self.bgk, self.bgk_t = self.const_load(pre + "bgk_b", [128, 1024])
        ml, ml_t = self.const_load("mask_L", [128, 128])
        mu, mu_t = self.const_load("mask_U", [128, 128])
        self.m01, self.m01_t = ml, ml_t
        self.Lm = P.sbuf("Lm", [128, 128], BF16)
        self.Um = P.sbuf("Um", [128, 128], BF16)
        self.Lm_t, self.Um_t = P.tile("Lm"), P.tile("Um")
        P.add("dve", lambda e: e.tensor_scalar(self.Lm[:], ml[:], -1.0 / 16.0, None, ALU.mult), reads=[ml_t], writes=[self.Lm_t])
        P.add("dve", lambda e: e.tensor_scalar(self.Um[:], mu[:], -1.0 / 16.0, None, ALU.mult), reads=[mu_t], writes=[self.Um_t])
        if not self.pair:
            self.S = P.sbuf("S", [128, 8, 512], F32)
            self.S_t = P.tiles("S", 8)
            for i in range(8):
                P.add("dve", lambda e, i=i: e.memset(self.S[:, i, :], 0.0), writes=[self.S_t[i]])
        self.G1, self.G2 = {}, {}
        self.sw_dma_t = P.tile("swdma")
        self.gl = P.sbuf("gl", [16, T], BF16)
        self.gl_t = P.tile("gl")
        self.gla_t = dict(G=P.tiles("gG", 8), eb=P.tiles("geb", 2), em=P.tiles("gem", 2), edl=P.tiles("gedl", 4), Q=P.tiles("gQ", 2),
                          K=P.tiles("gK", 2), kd=P.tiles("gkd", 4), V=P.tiles("gV", 4), SR=P.tiles("gSR", 4), sc=P.tiles("gsc", 2),
                          oraw=P.tiles("goraw", 4), OG=P.tiles("gOG", 4), xg=P.tiles("gxg", 2), ot=P.tiles("got", 2),
                          QA=P.tile("gQA"), QB=P.tile("gQB"), Sb=P.tiles("gSb", 2), Sw=P.tiles("gSw", 2))

    def gla_mixer(self, l, full, ti=0):
        P = self.P
        nc = self.nc
        pre = "l%d_" % l
        gt = self.gla_t
        P.barrier()
        self.rmsnorm(self.vec[pre + "norm1"], self.h, self.ht)
        W_in = self.dram[pre + "gla_w_in"]
        ar = self.arena
        o = 0

        def carve(n):
            nonlocal o
            v = ar[:, o:o + n]
            o += n
            return v
        Ghi = carve(4 * 1024).rearrange("p (b d) -> p b d", b=4)
        Glo = carve(4 * 1024).rearrange("p (b d) -> p b d", b=4)
        QK = carve(4 * T)
        Q = QK[:, 0:2 * T].rearrange("p (c t) -> p c t", c=2)
        Kin = QK[:, 2 * T:4 * T].rearrange("p (c t) -> p c t", c=2)
        osq = QK.rearrange("p (c t) -> p c t", c=4)
        kdec = carve(4 * 256).rearrange("p (b d) -> p b d", b=4)
        V = carve(4 * 512).rearrange("p (b v) -> p b v", b=4)
        SR = carve(4 * T).rearrange("p (c t) -> p c t", c=4)
        scT = [carve(128) for _ in range(2)]
        OG = carve(4 * T).rearrange("p (c t) -> p c t", c=4)
        QA = carve(2 * T).rearrange("p (c t) -> p c t", c=2)
        QB = carve(2 * T).rearrange("p (c t) -> p c t", c=2)
        Sb = carve(2 * 512).rearrange("p (c v) -> p c v", c=2)
        assert o <= 44 * T, o
        a2 = self.arena2
        eb = a2[:, 0:2 * T].rearrange("p (c t) -> p c t", c=2)
        em = a2[:, 2 * T:4 * T].rearrange("p (c t) -> p c t", c=2)
        edl = a2[:, 4 * T:4 * T + 1024].rearrange("p (b d) -> p b d", b=4)
        oraw = a2[:, 3072:3072 + 4 * T].rearrange("p (c t) -> p c t", c=4)
        xg = [a2[:, 5120 + i * 512:5120 + (i + 1) * 512] for i in range(2)]
        otmp = [a2[:, 6144 + i * 512:6144 + (i + 1) * 512] for i in range(2)]
        Sw = a2[:, 7168:8192].rearrange("p (c v) -> p c v", c=2)
        bk, bt = P.bank()

        def mmgl(e, bk=bk):
            ins = None
            for k in range(NC16):
                ins = e.matmul(bk[0:16, :], self.wgk[:, k, :], self.h[:, k, :], start=(k == 0), stop=(k == NC16 - 1))
            return ins
        P.add("pe", mmgl, reads=[self.wgk_t] + self.ht, writes=[bt])
        P.add("act", lambda e, bk=bk: e.activation(self.gl[:], bk[0:16, :], AF.Copy), reads=[bt], writes=[self.gl_t])
        for b in range(4):
            for n in range(2):
                bk, bt = P.bank()
                P.add("pe", lambda e, b=b, n=n, bk=bk: e.matmul(bk[:], self.gl[:, b * 128:(b + 1) * 128], self.wup[:, n * 512:(n + 1) * 512], start=True, stop=True),
                      reads=[self.gl_t, self.wup_t], writes=[bt])
                i = n
                P.add("dve", lambda e, i=i, n=n, bk=bk: e.tensor_tensor(xg[i], bk[:], self.bgk[:, n * 512:(n + 1) * 512], ALU.add),
                      reads=[bt, self.bgk_t], writes=[gt["xg"][i]])
                P.add("act", lambda e, i=i: e.activation(xg[i], xg[i], AF.Exp, scale=-1.0), reads=[gt["xg"][i]], writes=[gt["xg"][i]])
                P.add("act", lambda e, i=i: e.activation(xg[i], xg[i], AF.Ln, bias=1.0), reads=[gt["xg"][i]], writes=[gt["xg"][i]])
                Gt = gt["G"][b * 2 + n]
                P.add("dve", lambda e, i=i, b=b, n=n: e.tensor_copy(Ghi[:, b, n * 512:(n + 1) * 512], xg[i]), reads=[gt["xg"][i]], writes=[Gt])
                P.add("dve", lambda e, i=i, b=b, n=n: e.tensor_tensor(Glo[:, b, n * 512:(n + 1) * 512], xg[i], Ghi[:, b, n * 512:(n + 1) * 512], ALU.subtract),
                      reads=[gt["xg"][i], Gt], writes=[Gt])
        Wout_v = self.dram[pre + "gla_w_out"].rearrange("(k p) n -> p k n", p=128)
        self.pe_warm(50)
        for hd in range(4):
            Gtl = [gt["G"][b * 2 + hd // 2] for b in range(4)]
            for dc in range(2):
                bk, bt = P.bank()
                c0 = hd * 256 + dc * 128

                def mmb(e, bk=bk, c0=c0):
                    ins = None
                    for b in range(4):
                        e.matmul(bk[:, b * 128:(b + 1) * 128], Ghi[:, b, c0:c0 + 128], self.Lm[:], start=True, stop=False)
                        ins = e.matmul(bk[:, b * 128:(b + 1) * 128], Glo[:, b, c0:c0 + 128], self.Lm[:], start=False, stop=True)
                    return ins
                P.add("pe", mmb, reads=[self.Lm_t] + Gtl, writes=[bt])
                P.add("act", lambda e, dc=dc, bk=bk: e.activation(eb[:, dc, :], bk[:], AF.Exp), reads=[bt], writes=[gt["eb"][dc]])
                if full:
                    P.add("act", lambda e, dc=dc, bk=bk: e.activation(em[:, dc, :], bk[:], AF.Exp, scale=-1.0), reads=[bt], writes=[gt["em"][dc]])
            for half in range(2):
                bk, bt = P.bank()

                def mmu(e, bk=bk, half=half, hd=hd):
                    ins = None
                    for bb in range(2):
                        b = half * 2 + bb
                        e.matmul(bk[:, bb * 256:(bb + 1) * 256], self.Um[:], Ghi[:, b, hd * 256:(hd + 1) * 256], start=True, stop=False)
                        ins = e.matmul(bk[:, bb * 256:(bb + 1) * 256], self.Um[:], Glo[:, b, hd * 256:(hd + 1) * 256], start=False, stop=True)
                    return ins
                P.add("pe", mmu, reads=[self.Um_t] + Gtl[half * 2:half * 2 + 2], writes=[bt])
                P.add("act", lambda e, bk=bk, half=half: e.activation(edl[:, half * 2:half * 2 + 2, :], bk[:].rearrange("p (b d) -> p b d", b=2), AF.Exp),
                      reads=[bt], writes=gt["edl"][half * 2:half * 2 + 2])
            for dc in range(2 if (full and not self.pair) else 0):
                P.add("act", lambda e, dc=dc, hd=hd: e.activation(Sb[:, dc, :], self.S[:, hd * 2 + dc, :], AF.Copy),
                      reads=[self.S_t[hd * 2 + dc]], writes=[gt["Sb"][dc]])
            if full:
                for m, bk, bt in self.proj_fm(W_in, hd * 256, 256, self.h, self.ht):
                    P.add("dve", lambda e, m=m, bk=bk: e.scalar_tensor_tensor(Q[:, m, :], bk[:], 0.0625, eb[:, m, :], ALU.mult, ALU.mult),
                          reads=[bt, gt["eb"][m]], writes=[gt["Q"][m]])
                Qv = lambda buf, par: buf.rearrange("p c (b two t) -> p c b two t", two=2, t=64)[:, :, :, par, :]
                P.add("dve", lambda e: e.memset(QA, 0.0), writes=[gt["QA"]])
                P.add("dve", lambda e: e.memset(QB, 0.0), writes=[gt["QB"]])
                P.add("act", lambda e: e.activation(Qv(QA, 0), Qv(Q, 0), AF.Copy), reads=gt["Q"] + [gt["QA"]], writes=[gt["QA"]])
                P.add("act", lambda e: e.activation(Qv(QB, 1), Qv(Q, 1), AF.Copy), reads=gt["Q"] + [gt["QB"]], writes=[gt["QB"]])
                for m, bk, bt in self.proj_fm(W_in, 1024 + hd * 256, 256, self.h, self.ht):
                    P.add("dve", lambda e, m=m, bk=bk: e.tensor_tensor(Kin[:, m, :], bk[:], em[:, m, :], ALU.mult),
                          reads=[bt, gt["em"][m]], writes=[gt["K"][m]])
            for b, bk, bt in self.proj_tm(W_in, 1024 + hd * 256, 256, self.h, self.ht):
                P.add("dve", lambda e, b=b, bk=bk: e.tensor_tensor(kdec[:, b, :], bk[:, 0:256], edl[:, b, :], ALU.mult),
                      reads=[bt, gt["edl"][b]], writes=[gt["kd"][b]])
            for b, bk, bt in self.proj_tm(W_in, 2048 + hd * 512, 512, self.h, self.ht):
                P.add("act", lambda e, b=b, bk=bk: e.activation(V[:, b, :], bk[:], AF.Copy), reads=[bt], writes=[gt["V"][b]])
            if full:
                for m, bk, bt in self.proj_fm(W_in, 4096 + hd * 512, 512, self.h, self.ht):
                    P.add("act", lambda e, m=m, bk=bk: e.activation(SR[:, m, :], bk[:], AF.Silu), reads=[bt], writes=[gt["SR"][m]])
            if self.pair:
                self.gla_pair_head(l, ti, hd, dict(Q=Q, Kin=Kin, kdec=kdec, V=V, SR=SR, scT=scT, OG=OG, QA=QA, QB=QB, Sb=Sb, Sw=Sw,
                                                   eb=eb, oraw=oraw, otmp=otmp, osq=osq, Wout_v=Wout_v))
                continue
            obanks = P.reserve(4) if full else None
            for b in range(4):
                blk = slice(b * 128, (b + 1) * 128)
                if full:
                    bks, bts = P.bank()

                    def mms(e, bks=bks, blk=blk):
                        e.matmul(bks[:, 0:128], Kin[:, 0, blk], Q[:, 0, blk], start=True, stop=False)
                        return e.matmul(bks[:, 0:128], Kin[:, 1, blk], Q[:, 1, blk], start=False, stop=True)
                    P.add("pe", mms, reads=gt["K"] + gt["Q"], writes=[bts])
                    si = b % 2
                    P.add("dve", lambda e, si=si, bks=bks: e.tensor_tensor(scT[si], bks[:, 0:128], self.m01[:], ALU.mult),
                          reads=[bts, self.m01_t], writes=[gt["sc"][si]])
                for cc in range(2):
                    c = 2 * b + cc
                    prt = slice(cc * 64, cc * 64 + 64)
                    if full:
                        Qx, Qxt = (QA, gt["QA"]) if cc == 0 else (QB, gt["QB"])

                        def mmo(e, b=b, cc=cc, blk=blk, Qx=Qx, si=b % 2, obanks=obanks):
                            ins = None
                            for vc in range(4):
                                ob = P.banks[obanks[vc]]
                                if cc == 0:
                                    e.matmul(ob[:, blk], V[:, b, vc * 128:(vc + 1) * 128], scT[si], start=True, stop=False)
                                for dc in range(2):
                                    ins = e.matmul(ob[:, blk], Sb[:, dc, vc * 128:(vc + 1) * 128], Qx[:, dc, blk],
                                                   start=False, stop=(cc == 1 and dc == 1))
                            return ins
                        obt = [P.bank_tiles[i] for i in obanks]
                        rds = [Qxt, gt["V"][b], gt["sc"][b % 2]] + gt["Sb"]
                        first = (b == 0 and cc == 0)
                        P.add("pe", mmo, reads=rds, writes=obt if first else (), partial=() if first else obt)
                    for dc in range(2):
                        si_ = hd * 2 + dc
                        bku, btu = P.bank()
                        P.add("pe", lambda e, bku=bku, prt=prt, b=b, dc=dc: e.matmul(bku[:], kdec[prt, b, dc * 128:(dc + 1) * 128], V[prt, b, :], start=True, stop=True),
                              reads=[gt["kd"][b], gt["V"][b]], writes=[btu])
                        col = 64 * c + 63
                        P.add("dve", lambda e, si_=si_, dc=dc, col=col, bku=bku: e.scalar_tensor_tensor(self.S[:, si_, :], self.S[:, si_, :], eb[:, dc, col:col + 1], bku[:], ALU.mult, ALU.add),
                              reads=[btu, gt["eb"][dc], self.S_t[si_]], writes=[self.S_t[si_]])
                        if full and not (b == 3 and cc == 1):
                            P.add("act", lambda e, si_=si_, dc=dc: e.activation(Sb[:, dc, :], self.S[:, si_, :], AF.Copy),
                                  reads=[self.S_t[si_]], writes=[gt["Sb"][dc]])
            if not full:
                continue
            qk_t = gt["Q"] + gt["K"]
            for vc in range(4):
                ob, obt = P.banks[obanks[vc]], P.bank_tiles[obanks[vc]]
                P.add("act", lambda e, vc=vc, ob=ob: e.activation(oraw[:, vc, :], ob[:], AF.Copy), reads=[obt], writes=[gt["oraw"][vc]])
                P.add("act", lambda e, vc=vc, ob=ob: e.activation(osq[:, vc, :], ob[:], AF.Square), reads=[obt],
                      partial=qk_t if vc else (), writes=() if vc else qk_t)
            P.release(obanks)
            bkn, btn = P.bank()

            def mmn(e, bkn=bkn):
                ins = None
                for vc in range(4):
                    ins = e.matmul(bkn[:], self.ones[:], osq[:, vc, :], start=(vc == 0), stop=(vc == 3))
                return ins
            P.add("pe", mmn, reads=qk_t + [self.ones_t], writes=[btn])
            P.add("act", lambda e, bkn=bkn: e.activation(self.rs_a[:], bkn[:], AF.Sqrt, scale=1.0 / 512.0, bias=EPS), reads=[btn], writes=[self.rs_at])
            P.add("dve", lambda e: e.reciprocal(self.rstd[:], self.rs_a[:]), reads=[self.rs_at], writes=[self.rstd_t])
            onw = self.vec[pre + "gla_out_norm_w"]
            for vc in range(4):
                m = hd * 4 + vc
                i = vc % 2
                P.add("dve", lambda e, vc=vc, m=m, i=i: e.scalar_tensor_tensor(otmp[i], oraw[:, vc, :], onw[:, m:m + 1], self.rstd[:], ALU.mult, ALU.mult),
                      reads=[gt["oraw"][vc], self.rstd_t, self.vec_t], writes=[gt["ot"][i]])
                P.add("dve", lambda e, vc=vc, i=i: e.tensor_tensor(OG[:, vc, :], otmp[i], SR[:, vc, :], ALU.mult),
                      reads=[gt["ot"][i], gt["SR"][vc]], writes=[gt["OG"][vc]])
            for s0 in (0, 1024):
                wv, wt = self.wslot(Wout_v[:, hd * 4:hd * 4 + 4, s0:s0 + 1024], [128, 4, 1024])
                for mi in range(8):
                    m = s0 // 128 + mi
                    bk, bt = P.bank()

                    def mmw(e, wv=wv, mi=mi, bk=bk):
                        ins = None
                        for k in range(4):
                            ins = e.matmul(bk[:], wv[:, k, mi * 128:(mi + 1) * 128], OG[:, k, :], start=(k == 0), stop=(k == 3))
                        return ins
                    P.add("pe", mmw, reads=[wt] + gt["OG"], writes=[bt])
                    P.add("dve", lambda e, m=m, bk=bk: e.tensor_tensor(self.x[:, m, :], self.x[:, m, :], bk[:], ALU.add),
                          reads=[bt, self.xt[m]], writes=[self.xt[m]])

    def gla_pair_head(self, l, ti, hd, B_):
        P = self.P
        nc = self.nc
        gt = self.gla_t
        pre = "l%d_" % l
        Q, Kin, kdec, V, SR, scT, OG, QA, QB, Sb, Sw = (B_[k] for k in ("Q", "Kin", "kdec", "V", "SR", "scT", "OG", "QA", "QB", "Sb", "Sw"))
        eb, oraw, otmp, osq, Wout_v = (B_[k] for k in ("eb", "oraw", "otmp", "osq", "Wout_v"))
        fX, fY = self.fl[:, 0:1], self.fl[:, 1:2]
        wouts = [self.wslot(Wout_v[:, hd * 4:hd * 4 + 4, s0:s0 + 1024], [128, 4, 1024]) for s0 in (0, 1024)]
        for slot in range(2):
            if slot == 0 and ti == 0:
                for dc in range(2):
                    P.add("dve", lambda e, dc=dc: e.memset(Sw[:, dc, :], 0.0), writes=[gt["Sw"][dc]])
            else:
                if slot == 0:
                    gap, gtile = self.G2[(ti - 1, hd)]
                    gsrc = gap[128:256, :]
                else:
                    gap, gtile = self.G1[(ti, hd)]
                    gsrc = gap[0:128, :]
                P.add("sp", lambda e, gsrc=gsrc: e.dma_start(out=Sw, in_=gsrc.rearrange("p (c v) -> p c v", c=2)),
                      reads=[gtile, self.gl_t], writes=gt["Sw"], dma=True, sem_tile=self.sw_dma_t)
            for dc in range(2):
                P.add("act", lambda e, dc=dc: e.activation(Sb[:, dc, :], Sw[:, dc, :], AF.Copy), reads=[gt["Sw"][dc]], writes=[gt["Sb"][dc]])
            obanks = P.reserve(4)
            obt = [P.bank_tiles[i] for i in obanks]
            for b in range(4):
                blk = slice(b * 128, (b + 1) * 128)
                bks, bts = P.bank()

                def mms(e, bks=bks, blk=blk):
                    e.matmul(bks[:, 0:128], Kin[:, 0, blk], Q[:, 0, blk], start=True, stop=False)
                    return e.matmul(bks[:, 0:128], Kin[:, 1, blk], Q[:, 1, blk], start=False, stop=True)
                P.add("pe", mms, reads=gt["K"] + gt["Q"], writes=[bts])
                si = b % 2
                P.add("dve", lambda e, si=si, bks=bks: e.tensor_tensor(scT[si], bks[:, 0:128], self.m01[:], ALU.mult),
                      reads=[bts, self.m01_t], writes=[gt["sc"][si]])
                for cc in range(2):
                    c = 2 * b + cc
                    prt = slice(cc * 64, cc * 64 + 64)
                    Qx, Qxt = (QA, gt["QA"]) if cc == 0 else (QB, gt["QB"])

                    def mmo(e, b=b, cc=cc, blk=blk, Qx=Qx, si=si, obanks=obanks):
                        ins = None
                        for vc in range(4):
                            ob = P.banks[obanks[vc]]
                            if cc == 0:
                                e.matmul(ob[:, blk], V[:, b, vc * 128:(vc + 1) * 128], scT[si], start=True, stop=False)
                            for dc in range(2):
                                ins = e.matmul(ob[:, blk], Sb[:, dc, vc * 128:(vc + 1) * 128], Qx[:, dc, blk],
                                               start=False, stop=(cc == 1 and dc == 1))
                        return ins
                    rds = [Qxt, gt["V"][b], gt["sc"][si]] + gt["Sb"]
                    first = (b == 0 and cc == 0)
                    P.add("pe", mmo, reads=rds, writes=obt if first else (), partial=() if first else obt)
                    for dc in range(2):
                        bku, btu = P.bank()
                        P.add("pe", lambda e, bku=bku, prt=prt, b=b, dc=dc: e.matmul(bku[:], kdec[prt, b, dc * 128:(dc + 1) * 128], V[prt, b, :], start=True, stop=True),
                              reads=[gt["kd"][b], gt["V"][b]], writes=[btu])
                        col = 64 * c + 63
                        P.add("dve", lambda e, dc=dc, col=col, bku=bku: e.scalar_tensor_tensor(Sw[:, dc, :], Sw[:, dc, :], eb[:, dc, col:col + 1], bku[:], ALU.mult, ALU.add),
                              reads=[btu, gt["eb"][dc], gt["Sw"][dc]], writes=[gt["Sw"][dc]])
                        if not (b == 3 and cc == 1):
                            P.add("act", lambda e, dc=dc: e.activation(Sb[:, dc, :], Sw[:, dc, :], AF.Copy), reads=[gt["Sw"][dc]], writes=[gt["Sb"][dc]])
                    self.pe_warm(4)
            self.pe_warm(100 if slot == 0 else 50)
            key = ("g", hd, slot)
            if key not in self.xbuf:
                nm = "%d_%d" % (hd, slot)
                self.xbuf[key] = (nc.dram_tensor("gs_src_" + nm, [128, 1024], F32).ap(), nc.dram_tensor("gs_dst_" + nm, [256, 1024], F32).ap(),
                                  P.tile("gss" + nm), P.tile("gsd" + nm))
            src_d, dst_d, t_src, t_dst = self.xbuf[key]
            hs, hc = self.xs_src[self.xs_n % 8], self.xs_cc[self.xs_n % 8]
            self.xs_n += 1
            P.add("sp", lambda e, src_d=src_d: e.dma_start(out=src_d.rearrange("p (c v) -> p c v", c=2), in_=Sw), reads=gt["Sw"], writes=[t_src], dma=True, sem_tile=hs)
            P.add("pool", lambda e, src_d=src_d, dst_d=dst_d: e.collective_compute("AllGather", ALU.bypass, replica_groups=PAIR_GROUPS, ins=[src_d], outs=[dst_d]),
                  reads=[t_src], writes=[t_dst], dma=True, sem_tile=hc, inc=1)
            (self.G1 if slot == 0 else self.G2)[(ti, hd)] = (dst_d, t_dst)
            for vc in range(4):
                ob = P.banks[obanks[vc]]
                if slot == 0:
                    P.add("act", lambda e, vc=vc, ob=ob: e.activation(oraw[:, vc, :], ob[:], AF.Copy, scale=fX),
                          reads=[obt[vc], self.fl_t], writes=[gt["oraw"][vc]])
                else:
                    P.add("dve", lambda e, vc=vc, ob=ob: e.scalar_tensor_tensor(oraw[:, vc, :], ob[:], fY, oraw[:, vc, :], ALU.mult, ALU.add),
                          reads=[obt[vc], self.fl_t, gt["oraw"][vc]], writes=[gt["oraw"][vc]])
            P.release(obanks)
        qk_t = gt["Q"] + gt["K"]
        for vc in range(4):
            P.add("act", lambda e, vc=vc: e.activation(osq[:, vc, :], oraw[:, vc, :], AF.Square), reads=[gt["oraw"][vc]],
                  partial=qk_t if vc else (), writes=() if vc else qk_t)
        bkn, btn = P.bank()

        def mmn(e, bkn=bkn):
            ins = None
            for vc in range(4):
                ins = e.matmul(bkn[:], self.ones[:], osq[:, vc, :], start=(vc == 0), stop=(vc == 3))
            return ins
        P.add("pe", mmn, reads=qk_t + [self.ones_t], writes=[btn])
        self.pe_warm(28)
        P.add("act", lambda e, bkn=bkn: e.activation(self.rs_a[:], bkn[:], AF.Sqrt, scale=1.0 / 512.0, bias=EPS), reads=[btn], writes=[self.rs_at])
        P.add("dve", lambda e: e.reciprocal(self.rstd[:], self.rs_a[:]), reads=[self.rs_at], writes=[self.rstd_t])
        onw = self.vec[pre + "gla_out_norm_w"]
        for vc in range(4):
            m = hd * 4 + vc
            i = vc % 2
            P.add("dve", lambda e, vc=vc, m=m, i=i: e.scalar_tensor_tensor(otmp[i], oraw[:, vc, :], onw[:, m:m + 1], self.rstd[:], ALU.mult, ALU.mult),
                  reads=[gt["oraw"][vc], self.rstd_t, self.vec_t], writes=[gt["ot"][i]])
            P.add("dve", lambda e, vc=vc, i=i: e.tensor_tensor(OG[:, vc, :], otmp[i], SR[:, vc, :], ALU.mult),
                  reads=[gt["ot"][i], gt["SR"][vc]], writes=[gt["OG"][vc]])
        for (wv, wt), s0 in zip(wouts, (0, 1024)):
            for mi in range(8):
                m = s0 // 128 + mi
                bk, bt = P.bank()

                def mmw(e, wv=wv, mi=mi, bk=bk):
                    ins = None
                    for k in range(4):
                        ins = e.matmul(bk[:], wv[:, k, mi * 128:(mi + 1) * 128], OG[:, k, :], start=(k == 0), stop=(k == 3))
                    return ins
                P.add("pe", mmw, reads=[wt] + gt["OG"], writes=[bt])
                P.add("dve", lambda e, m=m, bk=bk: e.tensor_tensor(self.x[:, m, :], self.x[:, m, :], bk[:], ALU.add),
                      reads=[bt, self.xt[m]], writes=[self.xt[m]])

    def build(self):
        P = self.P
        self.setup()
        n_out = 0
        for ti in range(self.n_tiles):
            modes = self.plan[ti]
            if not modes:
                continue
            self.fill_id = 0
            P.add("sp", lambda e, ti=ti: e.dma_start(out=self.x[:], in_=self.x_in[ti]), writes=self.xt, dma=True, sem_tile=self.t_x_dma)
            for l in self.layers:
                mode = modes.get(l)
                if mode is None:
                    continue
                kind = LAYER_KIND[l]
                if kind == "pool":
                    self.pool_mixer(l, ti, mode)
                elif kind == "sgu":
                    self.sgu_mixer(l)
                else:
                    self.gla_mixer(l, full=(mode == "full"), ti=ti)
                if mode == "full":
                    self.ffn(l)
            if ti in self.store_tiles:
                if self.final_norm:
                    self.rmsnorm(self.vec["final_norm_w"], self.x, self.xt)
                oi = n_out
                n_out += 1
                P.add("sp", lambda e, oi=oi: e.dma_start(out=self.y_out[oi], in_=self.x[:]), reads=self.xt, writes=[P.tile("yo%d" % oi)],
                      dma=True, sem_tile=self.t_x_dma)
        P.emit(final_wait_tiles=[self.t_x_dma])
        return self.nc


def fm_vec(v):
    return np.ascontiguousarray(np.asarray(v, np.float32).reshape(NC16, 128).T)


def tiles_fm(xs):
    n = xs.shape[0] // T
    return np.ascontiguousarray(xs.reshape(n, T, NC16, 128).transpose(0, 3, 2, 1))


def tiles_fm_inv(y):
    n = y.shape[0]
    return np.ascontiguousarray(y.transpose(0, 3, 2, 1).reshape(n * T, D))


def chunk_masks():
    s = np.arange(128)[:, None]
    t = np.arange(128)[None, :]
    same = (s // 64) == (t // 64)
    return ((same & (s <= t)).astype(np.float32), (same & (s > t)).astype(np.float32), (s <= t).astype(np.float32))


def invc_table(is_start):
    out = np.zeros((128, 4, 16), np.float32)
    for g in range(4):
        w = 2 ** (g + 1)
        tt = np.arange(16)
        cnt = np.minimum(tt + 1, w) if is_start else np.full(16, w)
        out[:, g, :] = (1.0 / cnt.astype(np.float32))[None, :]
    return out


def common_inputs(inputs, layers, final_norm=True):
    m = {}
    mL, mU, mT = chunk_masks()
    for l in layers:
        pre = "l%d_" % l
        k = LAYER_KIND[l]
        m["v_" + pre + "norm1"] = fm_vec(inputs[pre + "norm1"])
        m["v_" + pre + "norm2"] = fm_vec(inputs[pre + "norm2"])
        for nm in FFN_PARAMS[1:]:
            m[pre + nm] = np.ascontiguousarray(inputs[pre + nm], dtype=np.float32)
        if k == "pool":
            m["v_" + pre + "pool_scale"] = fm_vec(inputs[pre + "pool_scale"])
            for nm in ("pool_w_in", "pool_w_group", "pool_w_out"):
                m[pre + nm] = np.ascontiguousarray(inputs[pre + nm], dtype=np.float32)
        elif k == "sgu":
            for nm in ("sgu_w_in", "sgu_w_out"):
                m[pre + nm] = np.ascontiguousarray(inputs[pre + nm], dtype=np.float32)
            m["v_" + pre + "sgu_v_norm_w"] = fm_vec(inputs[pre + "sgu_v_norm_w"])
            m["v_" + pre + "sgu_v_norm_b"] = fm_vec(inputs[pre + "sgu_v_norm_b"])
            m[pre + "bsp_b"] = np.ascontiguousarray(np.broadcast_to(np.asarray(inputs[pre + "sgu_b_spatial"], np.float32)[None, :, :], (128, 8, 128)))
            m[pre + "ws_t"] = np.ascontiguousarray(np.asarray(inputs[pre + "sgu_w_spatial"], np.float32).transpose(2, 0, 1))
            m["mask_tri"] = mT
        elif k == "gla":
            m["v_" + pre + "gla_out_norm_w"] = fm_vec(inputs[pre + "gla_out_norm_w"])
            for nm in ("gla_w_in", "gla_w_out", "gla_w_gk_up"):
                m[pre + nm] = np.ascontiguousarray(inputs[pre + nm], dtype=np.float32)
            m[pre + "bgk_b"] = np.ascontiguousarray(np.broadcast_to(np.asarray(inputs[pre + "gla_b_gk"], np.float32)[None, :], (128, 1024)))
            m["mask_L"] = mL
            m["mask_U"] = mU
    if final_norm:
        m["v_final_norm_w"] = fm_vec(inputs["final_norm_w"])
    return m


N_PRE = 8
N_MAIN = 8
PAIR_GROUPS = [[0, 1], [2, 3], [4, 5], [6, 7]]


def full_plan():
    plan = []
    for ti in range(N_PRE + N_MAIN):
        if ti < N_PRE - 1:
            plan.append({0: "full", 1: "full", 2: "state"})
        elif ti == N_PRE - 1:
            plan.append({0: "full", 1: "full", 2: "full", 3: "z"})
        else:
            plan.append({0: "full", 1: "full", 2: "full", 3: "full"})
    return plan


ALL_INPUTS = (
    "x",
    "l0_norm1", "l0_pool_w_in", "l0_pool_w_group", "l0_pool_scale", "l0_pool_w_out",
    "l0_norm2", "l0_ffn_w_gate", "l0_ffn_w_up", "l0_ffn_w_down",
    "l1_norm1", "l1_sgu_w_in", "l1_sgu_v_norm_w", "l1_sgu_v_norm_b", "l1_sgu_w_spatial", "l1_sgu_b_spatial", "l1_sgu_w_out",
    "l1_norm2", "l1_ffn_w_gate", "l1_ffn_w_up", "l1_ffn_w_down",
    "l2_norm1", "l2_gla_w_in", "l2_gla_w_gk_up", "l2_gla_b_gk", "l2_gla_out_norm_w", "l2_gla_w_out",
    "l2_norm2", "l2_ffn_w_gate", "l2_ffn_w_up", "l2_ffn_w_down",
    "l3_norm1", "l3_pool_w_in", "l3_pool_w_group", "l3_pool_scale", "l3_pool_w_out",
    "l3_norm2", "l3_ffn_w_gate", "l3_ffn_w_up", "l3_ffn_w_down",
    "final_norm_w",
)


def kernel(**inputs):
    missing = [n for n in ALL_INPUTS if n not in inputs]
    assert not missing, missing
    x = np.asarray(inputs["x"], np.float32)
    B, S, _ = x.shape
    layers = [0, 1, 2, 3]
    NT = 8
    plan = [{l: "full" for l in layers} for _ in range(NT)]
    b = Builder(layers, NT, plan, list(range(NT)), [0], pair=True)
    nc = b.build()
    common = common_inputs(inputs, layers)
    in_maps = []
    for core in range(8):
        bi, r = core // 2, core % 2
        m = dict(common)
        xt_all = tiles_fm(x[bi])
        m["x_in"] = np.ascontiguousarray(xt_all[r::2])
        m["invc"] = np.ascontiguousarray(np.stack([invc_table(r == 0)], axis=1))
        fl = np.zeros((128, 2), np.float32)
        fl[:, r] = 1.0
        m["flags"] = fl
        in_maps.append(m)
    res = run_bass_kernel_spmd(nc, in_maps, core_ids=list(range(8)))
    out = np.zeros((B, S, D), np.float32)
    for core in range(8):
        bi, r = core // 2, core % 2
        y = res.results[core]["y_out"]
        o4 = out[bi].reshape(S // T, T, D)
        for k in range(NT):
            o4[2 * k + r] = tiles_fm_inv(y[k:k + 1])
    return out
```

```python
import contextlib
import numpy as np
import concourse.bass as bass
import concourse.mybir as mybir
from concourse.bass_utils import run_bass_kernel_spmd

F32 = mybir.dt.float32
BF16 = mybir.dt.bfloat16
AF = mybir.ActivationFunctionType
ALU = mybir.AluOpType

D = 2048
DFF = 5632
NJ = DFF // 128
T = 512
NC16 = 16
EPS = 1e-6
ENGS = ("pe", "act", "dve", "pool", "sp")


class Tile:
    __slots__ = ("name", "writers", "readers", "sem", "dma_count")

    def __init__(self, name):
        self.name = name
        self.writers = []
        self.readers = []
        self.sem = None
        self.dma_count = 0


class Op:
    __slots__ = ("eng", "fn", "deps", "is_dma", "sem_tile", "sig_val", "needs_signal", "inc")

    def __init__(self, eng, fn, is_dma, sem_tile, inc=16):
        self.inc = inc
        self.eng = eng
        self.fn = fn
        self.deps = []
        self.is_dma = is_dma
        self.sem_tile = sem_tile
        self.sig_val = None
        self.needs_signal = False


class Prog:
    def __init__(self, nc):
        self.nc = nc
        self.ops = {e: [] for e in ENGS}
        self.stack = contextlib.ExitStack()
        self.all_tiles = []
        self.banks = []
        self.bank_tiles = []
        self.bank_reserved = [False] * 8
        self.bank_rr = 0
        self.last_op = {}
        self.bar_prev = []
        self.bar_done = {"pe": True, "act": True, "dve": True}

    def barrier(self):
        self.bar_prev = [self.last_op[e] for e in ("pe", "act", "dve") if e in self.last_op]
        self.bar_done = {"pe": False, "act": False, "dve": False}

    def sbuf(self, name, shape, dtype):
        return self.stack.enter_context(self.nc.sbuf_tensor(name, list(shape), dtype))

    def tile(self, name):
        t = Tile(name)
        self.all_tiles.append(t)
        return t

    def tiles(self, name, n):
        return [self.tile("%s%d" % (name, i)) for i in range(n)]

    def init_banks(self):
        for i in range(8):
            self.banks.append(self.stack.enter_context(self.nc.psum_tensor("bank%d" % i, [128, 512], F32)))
            self.bank_tiles.append(self.tile("bank%d" % i))

    def bank(self):
        for _ in range(16):
            i = self.bank_rr
            self.bank_rr = (self.bank_rr + 1) % 8
            if not self.bank_reserved[i]:
                return self.banks[i], self.bank_tiles[i]
        raise RuntimeError("no free psum bank")

    def reserve(self, n):
        out = []
        for _ in range(n):
            for _ in range(16):
                i = self.bank_rr
                self.bank_rr = (self.bank_rr + 1) % 8
                if not self.bank_reserved[i]:
                    break
            else:
                raise RuntimeError("no free psum bank")
            self.bank_reserved[i] = True
            out.append(i)
        return out

    def release(self, idxs):
        for i in idxs:
            self.bank_reserved[i] = False

    def add(self, eng, fn, reads=(), writes=(), partial=(), dma=False, sem_tile=None, inc=16):
        op = Op(eng, fn, dma, sem_tile, inc)
        deps = []
        for t in reads:
            deps.extend(t.writers)
        for t in writes:
            deps.extend(t.writers)
            deps.extend(t.readers)
        for t in partial:
            deps.extend(t.readers)
            if t.readers:
                deps.extend(t.writers)
            elif t.writers:
                deps.append(t.writers[0])
        if eng in self.bar_done and not self.bar_done[eng]:
            deps.extend(self.bar_prev)
            self.bar_done[eng] = True
        if eng in self.bar_done:
            self.last_op[eng] = op
        for t in reads:
            t.readers.append(op)
        for t in writes:
            t.writers = [op]
            t.readers = []
        for t in partial:
            if t.readers:
                t.writers = [op]
                t.readers = []
            else:
                t.writers.append(op)
        seen = set()
        for d in deps:
            if d is op or id(d) in seen:
                continue
            seen.add(id(d))
            if d.eng == "pe" and eng == "pe" and not d.is_dma:
                continue
            op.deps.append(d)
            d.needs_signal = True
        if dma:
            op.needs_signal = True
        self.ops[eng].append(op)
        return op

    def emit(self, final_wait_tiles=()):
        nc = self.nc
        stack = self.stack
        eng_sem = {e: stack.enter_context(nc.semaphore("s_" + e)) for e in ("pe", "act", "dve", "pool")}
        for e in ENGS:
            for op in self.ops[e]:
                if op.is_dma and op.sem_tile.sem is None:
                    op.sem_tile.sem = stack.enter_context(nc.semaphore("d_" + op.sem_tile.name))
        cnt = {e: 0 for e in ENGS}
        for e in ENGS:
            for op in self.ops[e]:
                if op.is_dma:
                    op.sem_tile.dma_count += op.inc
                    op.sig_val = (op.sem_tile.sem, op.sem_tile.dma_count)
                elif op.needs_signal:
                    cnt[e] += 1
                    op.sig_val = (eng_sem[e], cnt[e])
        final_waits = [(t.sem, t.dma_count) for t in final_wait_tiles if t.sem is not None]
        ops = self.ops

        def run_engine(e, engine):
            waited = {}
            for op in ops[e]:
                need = {}
                for d in op.deps:
                    s, v = d.sig_val
                    k = id(s)
                    if k not in need or need[k][1] < v:
                        need[k] = (s, v)
                for k, (s, v) in need.items():
                    if waited.get(k, 0) >= v:
                        continue
                    engine.wait_ge(s, v)
                    waited[k] = v
                ins = op.fn(engine)
                if op.sig_val is not None:
                    s, v = op.sig_val
                    ins.then_inc(s, op.inc if op.is_dma else 1)
            if e == "sp":
                for s, v in final_waits:
                    engine.wait_ge(s, v)

        with nc.Block() as block:
            @block.tensor
            def _(eng):
                run_engine("pe", eng)

            @block.scalar
            def _(eng):
                run_engine("act", eng)

            @block.vector
            def _(eng):
                run_engine("dve", eng)

            @block.gpsimd
            def _(eng):
                run_engine("pool", eng)

            @block.sync
            def _(eng):
                run_engine("sp", eng)
        stack.close()


POOL_PARAMS = ("norm1", "pool_w_in", "pool_w_group", "pool_scale", "pool_w_out")
SGU_PARAMS = ("norm1", "sgu_w_in", "sgu_v_norm_w", "sgu_v_norm_b", "sgu_w_spatial", "sgu_b_spatial", "sgu_w_out")
GLA_PARAMS = ("norm1", "gla_w_in", "gla_w_gk_up", "gla_b_gk", "gla_out_norm_w", "gla_w_out")
FFN_PARAMS = ("norm2", "ffn_w_gate", "ffn_w_up", "ffn_w_down")
LAYER_KIND = ("pool", "sgu", "gla", "pool")
NSLOT = 4


class Builder:
    def __init__(self, layers, n_tiles, plan, store_tiles, seq_start_tiles, final_norm=True, pair=False):
        self.pair = pair
        self.layers = layers
        self.n_tiles = n_tiles
        self.plan = plan
        self.store_tiles = store_tiles
        self.seq_start_tiles = seq_start_tiles
        self.final_norm = final_norm
        nc = self.nc = bass.Bass("TRN2", target_bir_lowering=False)
        self.P = Prog(nc)
        self.dram = {}

    def din(self, name, shape, dtype=F32):
        ap = self.nc.dram_tensor(name, list(shape), dtype, kind="ExternalInput").ap()
        self.dram[name] = ap
        return ap

    def wslot(self, src_ap, shape):
        P = self.P
        i = self.slot_rr
        self.slot_rr = (self.slot_rr + 1) % NSLOT
        sl, tl = self.slots[i], self.slot_tiles[i]
        n = 1
        for s in shape[1:]:
            n *= s
        assert shape[0] == 128
        if len(shape) == 3:
            view = sl[0:shape[0], 0:n].rearrange("p (a b) -> p a b", a=shape[1])
        else:
            view = sl[0:shape[0], 0:n]
        f = self.fill_id
        self.fill_id += 1
        if not self.use_scratch:
            P.add("pool", lambda e, view=view, src=src_ap: e.dma_start(out=view, in_=src), writes=[tl], dma=True, sem_tile=tl)
        elif f not in self.scr:
            P.add("pool", lambda e, view=view, src=src_ap: e.dma_start(out=view, in_=src), writes=[tl], dma=True, sem_tile=tl)
            scr = self.nc.dram_tensor("wscr%d" % f, [128, n], BF16).ap()
            st = P.tile("wscr%d" % f)
            self.scr[f] = (scr, st, n)
            P.add("sp", lambda e, scr=scr, sl=sl, n=n: e.dma_start(out=scr, in_=sl[:, 0:n]), reads=[tl], writes=[st], dma=True, sem_tile=self.scr_sem[i])
        else:
            scr, st, n2 = self.scr[f]
            assert n2 == n, (f, n, n2)
            P.add("pool", lambda e, scr=scr, sl=sl, n=n: e.dma_start(out=sl[:, 0:n], in_=scr), reads=[st], writes=[tl], dma=True, sem_tile=tl)
        return view, tl

    def setup(self):
        P = self.P
        nc = self.nc
        L = self.layers
        P.init_banks()
        self.warm_bank = P.reserve(1)[0]
        self.x_in = self.din("x_in", [self.n_tiles, 128, NC16, T])
        n_store = len(self.store_tiles)
        self.y_out = nc.dram_tensor("y_out", [max(n_store, 1), 128, NC16, T], F32, kind="ExternalOutput").ap()
        self.x = P.sbuf("x", [128, NC16, T], F32)
        self.xt = P.tiles("x", NC16)
        self.h = P.sbuf("h", [128, NC16, T], BF16)
        self.ht = P.tiles("h", NC16)
        self.slots = [P.sbuf("slot%d" % i, [128, 4096], BF16) for i in range(NSLOT)]
        self.slot_tiles = P.tiles("slot", NSLOT)
        self.slot_rr = 0
        self.fill_id = 0
        self.scr = {}
        self.scr_sem = P.tiles("scrsem", NSLOT)
        self.use_scratch = self.pair
        self.t_x_dma = P.tile("xdma")
        self.arena = P.sbuf("arena", [128, 44 * T], BF16)
        self.arena2 = P.sbuf("arena2", [128, 8448], F32)
        self.sq = [P.sbuf("sq%d" % i, [128, 2, T], BF16) for i in range(2)]
        self.sq_t = P.tiles("sq", 2)
        self.sq_rr = 0
        self.rs_a = P.sbuf("rs_a", [128, T], F32)
        self.rs_at = P.tile("rs_a")
        self.rstd = P.sbuf("rstd", [128, T], F32)
        self.rstd_t = P.tile("rstd")
        self.ones = P.sbuf("ones", [128, 128], BF16)
        self.ones_t = P.tile("ones")
        self.warm = P.sbuf("warm", [128, T], BF16)
        P.add("dve", lambda e: e.memset(self.warm[:], 1.0), writes=[self.ones_t])
        P.add("dve", lambda e: e.memset(self.ones[:], 1.0), reads=[self.ones_t], writes=[self.ones_t])
        self.vec = {}
        self.vec_t = P.tile("vecs")
        names = []
        for l in L:
            names += ["l%d_norm1" % l, "l%d_norm2" % l]
            k = LAYER_KIND[l]
            if k == "pool":
                names.append("l%d_pool_scale" % l)
            if k == "gla":
                names.append("l%d_gla_out_norm_w" % l)
            if k == "sgu":
                names += ["l%d_sgu_v_norm_w" % l, "l%d_sgu_v_norm_b" % l]
        if self.final_norm:
            names.append("final_norm_w")
        vb = P.sbuf("vecbuf", [128, len(names), NC16], F32)
        for i, nm in enumerate(names):
            src = self.din("v_" + nm, [128, NC16])
            self.vec[nm] = vb[:, i, :]
            P.add("sp", lambda e, i=i, src=src: e.dma_start(out=vb[:, i, :], in_=src), partial=[self.vec_t], dma=True, sem_tile=self.vec_t)
        for l in L:
            k = LAYER_KIND[l]
            pre = "l%d_" % l
            for nm in FFN_PARAMS[1:]:
                shp = [D, DFF] if nm != "ffn_w_down" else [DFF, D]
                self.din(pre + nm, shp)
            if k == "pool":
                self.din(pre + "pool_w_in", [D, D])
                self.din(pre + "pool_w_group", [4, 512, 512])
                self.din(pre + "pool_w_out", [D, D])
            elif k == "sgu":
                self.din(pre + "sgu_w_in", [D, 2 * D])
                self.din(pre + "sgu_w_out", [D, D])
            elif k == "gla":
                self.din(pre + "gla_w_in", [D, 6160])
                self.din(pre + "gla_w_out", [D, D])
        if self.pair:
            self.fl, self.fl_t = self.const_load("flags", [128, 2])
            self.xs_src = P.tiles("xsrc", 8)
            self.xs_cc = P.tiles("xcc", 8)
            self.xs_n = 0
            self.xbuf = {}
        self.setup_pool()
        self.setup_sgu()
        self.setup_gla()
        P.barrier()

    def const_load(self, name, shape, dtype_sb=F32, eng="sp"):
        P = self.P
        src = self.din(name, shape)
        buf = P.sbuf("c_" + name, shape, dtype_sb)
        t = P.tile("c_" + name)
        P.add(eng, lambda e: e.dma_start(out=buf[:], in_=src), writes=[t], dma=True, sem_tile=t)
        return buf, t

    def rmsnorm(self, wvec, dst, dst_tiles, nfeat=D):
        P = self.P
        x, xt = self.x, self.xt
        self.pe_warm(24)
        bk, bt = P.bank()
        for g4 in range(8):
            i = self.sq_rr
            self.sq_rr ^= 1
            sq, sqt = self.sq[i], self.sq_t[i]
            P.add("act", lambda e, g4=g4, sq=sq: e.activation(sq[:], x[:, 2 * g4:2 * g4 + 2, :], AF.Square),
                  reads=xt[2 * g4:2 * g4 + 2], writes=[sqt])

            def mm(e, g4=g4, sq=sq):
                ins = None
                for c in range(2):
                    ins = e.matmul(bk[:], self.ones[:], sq[:, c, :], start=(g4 == 0 and c == 0), stop=(g4 == 7 and c == 1))
                return ins
            P.add("pe", mm, reads=[sqt, self.ones_t], partial=[bt] if g4 else (), writes=() if g4 else [bt])
        P.add("act", lambda e: e.activation(self.rs_a[:], bk[:], AF.Sqrt, scale=1.0 / nfeat, bias=EPS),
              reads=[bt], writes=[self.rs_at])
        if dst is not self.x:
            self.pe_warm(48)
        P.add("dve", lambda e: e.reciprocal(self.rstd[:], self.rs_a[:]), reads=[self.rs_at], writes=[self.rstd_t])
        for c in range(NC16):
            P.add("dve", lambda e, c=c: e.scalar_tensor_tensor(dst[:, c, :], x[:, c, :], wvec[:, c:c + 1], self.rstd[:], ALU.mult, ALU.mult),
                  reads=[xt[c], self.rstd_t, self.vec_t], writes=[dst_tiles[c]])

    def pe_warm(self, n):
        P = self.P
        bk = P.banks[self.warm_bank]

        def mm(e, bk=bk):
            ins = None
            for _ in range(n):
                ins = e.matmul(bk[:], self.ones[:], self.warm[:], start=True, stop=True)
            return ins
        P.add("pe", mm, reads=[self.ones_t])

    def proj_fm(self, W, c0, ncols, src, src_tiles, nk=NC16):
        P = self.P
        Wv = W.rearrange("(k p) n -> p k n", p=128)
        for s0 in range(0, ncols, 256):
            w = min(256, ncols - s0)
            view, tl = self.wslot(Wv[:, :, c0 + s0:c0 + s0 + w], [128, nk, w])
            for mi in range(w // 128):
                bk, bt = P.bank()

                def mm(e, view=view, mi=mi, bk=bk):
                    ins = None
                    for k in range(nk):
                        ins = e.matmul(bk[:], view[:, k, mi * 128:(mi + 1) * 128], src[:, k, :], start=(k == 0), stop=(k == nk - 1))
                    return ins
                P.add("pe", mm, reads=[tl] + list(src_tiles), writes=[bt])
                yield (s0 // 128 + mi, bk, bt)

    def proj_tm(self, W, n0, w, src, src_tiles):
        P = self.P
        Wv = W.rearrange("(k p) n -> p k n", p=128)
        if w == 512:
            v0, t0 = self.wslot(Wv[:, 0:8, n0:n0 + w], [128, 8, w])
            v1, t1 = self.wslot(Wv[:, 8:16, n0:n0 + w], [128, 8, w])
            views = [(v0, t0), (v1, t1)]
            per = 8
        else:
            v0, t0 = self.wslot(Wv[:, :, n0:n0 + w], [128, 16, w])
            views = [(v0, t0)]
            per = 16
        for b in range(4):
            bk, bt = P.bank()

            def mm(e, b=b, bk=bk):
                ins = None
                for k in range(NC16):
                    vw = views[k // per][0]
                    ins = e.matmul(bk[:, 0:w], src[:, k, b * 128:(b + 1) * 128], vw[:, k % per, :], start=(k == 0), stop=(k == NC16 - 1))
                return ins
            P.add("pe", mm, reads=[v[1] for v in views] + list(src_tiles), writes=[bt])
            yield (b, bk, bt)

    def out_proj_residual(self, W, src, src_tiles):
        P = self.P
        for m, bk, bt in self.proj_fm(W, 0, D, src, src_tiles):
            P.add("dve", lambda e, m=m, bk=bk: e.tensor_tensor(self.x[:, m, :], self.x[:, m, :], bk[:], ALU.add),
                  reads=[bt, self.xt[m]], writes=[self.xt[m]])

    def ffn(self, l):
        P = self.P
        pre = "l%d_" % l
        P.barrier()
        self.rmsnorm(self.vec[pre + "norm2"], self.h, self.ht)
        Wg = self.dram[pre + "ffn_w_gate"].rearrange("(k p) n -> p k n", p=128)
        Wu = self.dram[pre + "ffn_w_up"].rearrange("(k p) n -> p k n", p=128)
        Wd = self.dram[pre + "ffn_w_down"].rearrange("(j p) n -> p j n", p=128)
        act = self.arena[:, :].rearrange("p (j t) -> p j t", j=NJ)
        act_t = self.act_t
        sg, sg_t = self.sg, self.sg_t
        for jp in range(NJ // 2):
            gv, gt = self.wslot(Wg[:, :, jp * 256:(jp + 1) * 256], [128, 16, 256])
            uv, ut = self.wslot(Wu[:, :, jp * 256:(jp + 1) * 256], [128, 16, 256])
            for ji in range(2):
                j = jp * 2 + ji
                bg, bgt = P.bank()
                bu, but = P.bank()

                def mmg(e, v=gv, ji=ji, bk=bg):
                    ins = None
                    for k in range(NC16):
                        ins = e.matmul(bk[:], v[:, k, ji * 128:(ji + 1) * 128], self.h[:, k, :], start=(k == 0), stop=(k == NC16 - 1))
                    return ins
                P.add("pe", mmg, reads=[gt] + self.ht, writes=[bgt])
                P.add("pe", lambda e, v=uv, ji=ji, bk=bu: mmg(e, v, ji, bk), reads=[ut] + self.ht, writes=[but])
                i = j % 2
                P.add("act", lambda e, i=i, bk=bg: e.activation(sg[i][:], bk[:], AF.Silu), reads=[bgt], writes=[sg_t[i]])
                P.add("dve", lambda e, i=i, j=j, bk=bu: e.tensor_tensor(act[:, j, :], sg[i][:], bk[:], ALU.mult),
                      reads=[sg_t[i], but], writes=[act_t[j]])
        groups = [(0, 8), (8, 8), (16, 8), (24, 8), (32, 8), (40, 4)]
        for q in range(4):
            bidx = P.reserve(4)
            for (j0, nj) in groups:
                dv, dt = self.wslot(Wd[:, j0:j0 + nj, q * 512:(q + 1) * 512], [128, nj, 512])

                def mmd(e, dv=dv, j0=j0, nj=nj, bidx=bidx):
                    ins = None
                    for jj in range(nj):
                        j = j0 + jj
                        for m in range(4):
                            ins = e.matmul(P.banks[bidx[m]][:], dv[:, jj, m * 128:(m + 1) * 128], act[:, j, :],
                                           start=(j == 0), stop=(j == NJ - 1))
                    return ins
                bts = [P.bank_tiles[i] for i in bidx]
                P.add("pe", mmd, reads=[dt] + act_t[j0:j0 + nj], writes=bts if j0 == 0 else (), partial=() if j0 == 0 else bts)
            for m in range(4):
                c = 4 * q + m
                P.add("dve", lambda e, c=c, bk=P.banks[bidx[m]]: e.tensor_tensor(self.x[:, c, :], self.x[:, c, :], bk[:], ALU.add),
                      reads=[P.bank_tiles[bidx[m]], self.xt[c]], writes=[self.xt[c]])
            P.release(bidx)

    def setup_pool(self):
        P = self.P
        self.act_t = P.tiles("act", NJ)
        self.sg = [P.sbuf("sg%d" % i, [128, T], F32) for i in range(2)]
        self.sg_t = P.tiles("sg", 2)
        if not any(LAYER_KIND[l] == "pool" for l in self.layers):
            return
        self.halo = {}
        self.halo_t = {}
        for l in self.layers:
            if LAYER_KIND[l] == "pool":
                self.halo[l] = P.sbuf("halo%d" % l, [128, NC16, 16], F32)
                self.halo_t[l] = P.tiles("halo%d_" % l, 4)
                for g in range(4):
                    P.add("dve", lambda e, l=l, g=g: e.memset(self.halo[l][:, 4 * g:4 * g + 4, :], 0.0), writes=[self.halo_t[l][g]])
        self.invc, self.invc_t = self.const_load("invc", [128, len(self.seq_start_tiles), 4, 16])
        self.pool_t = dict(z=P.tiles("pz", 2), a=P.tile("pa"), b=P.tile("pb"), p=P.tiles("pp", NC16), ys=P.tiles("pys", NC16))
        if self.pair:
            self.tail = P.sbuf("ptail", [128, NC16, 16], F32)
            self.tail_t = P.tiles("ptail", 4)
            self.zf = P.sbuf("pzf", [128, NC16, 16], F32)
            self.zf_t = P.tiles("pzf", 4)
            self.Hy = P.sbuf("pHy", [128, NC16, 16], F32)
            self.Hy_t = P.tile("pHy")
            self.pw = P.sbuf("ppatch", [128, 3, 4, 32], F32)
            self.pyg = P.sbuf("ppyg", [128, 4, 16], F32)
            self.patch_t = P.tile("ppatch")
            self.halo_dma_t = {l: P.tile("halodma%d" % l) for l in self.halo}

    def pool_mixer(self, l, ti, mode):
        P = self.P
        pre = "l%d_" % l
        P.barrier()
        self.rmsnorm(self.vec[pre + "norm1"], self.h, self.ht)
        W_in = self.dram[pre + "pool_w_in"]
        zb = [self.arena2[:, i * 2112:(i + 1) * 2112].rearrange("p (c t) -> p c t", c=4) for i in range(2)]
        A = self.arena2[:, 2 * 2112:3 * 2112].rearrange("p (c t) -> p c t", c=4)
        B = self.arena2[:, 3 * 2112:4 * 2112].rearrange("p (c t) -> p c t", c=4)
        pt = self.pool_t
        p_buf = self.arena[:, 0:NC16 * T].rearrange("p (c t) -> p c t", c=NC16)
        ys = self.arena[:, NC16 * T:2 * NC16 * T].rearrange("p (c t) -> p c t", c=NC16)
        halo, halo_t = self.halo[l], self.halo_t[l]
        gen = self.proj_fm(W_in, 0, D, self.h, self.ht)
        for g in range(4):
            z, zt = zb[g % 2], pt["z"][g % 2]
            first = True
            for _ in range(4):
                m, bk, bt = next(gen)
                mi = m % 4
                if first:
                    P.add("dve", lambda e, z=z, g=g: e.tensor_copy(z[:, :, 0:16], halo[:, 4 * g:4 * g + 4, :]),
                          reads=[halo_t[g]], writes=[zt])
                    first = False
                P.add("act", lambda e, z=z, mi=mi, bk=bk: e.activation(z[:, mi, 16:528], bk[:], AF.Copy), reads=[bt], partial=[zt])
            if self.pair:
                P.add("dve", lambda e, z=z, g=g: e.tensor_copy(self.tail[:, 4 * g:4 * g + 4, :], z[:, :, 512:528]), reads=[zt], writes=[self.tail_t[g]])
                P.add("dve", lambda e, z=z, g=g: e.tensor_copy(self.zf[:, 4 * g:4 * g + 4, :], z[:, :, 16:32]), reads=[zt], writes=[self.zf_t[g]])
            else:
                P.add("dve", lambda e, z=z, g=g: e.tensor_copy(halo[:, 4 * g:4 * g + 4, :], z[:, :, 512:528]), reads=[zt], writes=[halo_t[g]])
            if mode != "full":
                continue
            ta, tb = pt["a"], pt["b"]
            P.add("dve", lambda e, z=z: e.tensor_tensor(A[:, :, 1:528], z[:, :, 1:528], z[:, :, 0:527], ALU.add), reads=[zt], writes=[ta])
            cur, curt = A, ta
            if g >= 1:
                P.add("dve", lambda e: e.tensor_tensor(B[:, :, 3:528], A[:, :, 3:528], A[:, :, 1:526], ALU.add), reads=[ta], writes=[tb])
                cur, curt = B, tb
            if g >= 2:
                P.add("dve", lambda e: e.tensor_tensor(A[:, :, 7:528], B[:, :, 7:528], B[:, :, 3:524], ALU.add), reads=[tb], writes=[ta])
                cur, curt = A, ta
            if g >= 3:
                P.add("dve", lambda e: e.tensor_tensor(B[:, :, 15:528], A[:, :, 15:528], A[:, :, 7:520], ALU.add), reads=[ta], writes=[tb])
                cur, curt = B, tb
            w = 2 ** (g + 1)
            ptl = pt["p"][4 * g:4 * g + 4]
            P.add("dve", lambda e, cur=cur, z=z, g=g, w=w: e.scalar_tensor_tensor(p_buf[:, 4 * g:4 * g + 4, :], cur[:, :, 16:528], 1.0 / w, z[:, :, 16:528], ALU.mult, ALU.subtract),
                  reads=[curt, zt], writes=ptl)
            if ti in self.seq_start_tiles:
                si = self.seq_start_tiles.index(ti)
                oth, otht = (B, tb) if cur is A else (A, ta)

                def fix1(e, cur=cur, oth=oth, g=g, si=si):
                    iv = self.invc[:, si, g, :].unsqueeze(1).to_broadcast([128, 4, 16])
                    return e.tensor_tensor(oth[:, :, 0:16], cur[:, :, 16:32], iv, ALU.mult)
                P.add("dve", fix1, reads=[curt, self.invc_t], writes=[otht])
                P.add("dve", lambda e, oth=oth, z=z, g=g: e.tensor_tensor(p_buf[:, 4 * g:4 * g + 4, 0:16], oth[:, :, 0:16], z[:, :, 16:32], ALU.subtract),
                      reads=[otht, zt], writes=ptl)
        if mode != "full":
            return
        if self.pair:
            self.pool_exchange_patch(l, ti, p_buf)
            self.pe_warm(110)
        Wgrp = self.dram[pre + "pool_w_group"]
        scale = self.vec[pre + "pool_scale"]
        for g in range(4):
            wv, wt = self.wslot(Wgrp[g].rearrange("(c p) d -> p c d", p=128), [128, 4, 512])
            for dch in range(4):
                bk, bt = P.bank()

                def mm(e, wv=wv, dch=dch, bk=bk, g=g):
                    ins = None
                    for c in range(4):
                        ins = e.matmul(bk[:], wv[:, c, dch * 128:(dch + 1) * 128], p_buf[:, 4 * g + c, :], start=(c == 0), stop=(c == 3))
                    return ins
                P.add("pe", mm, reads=[wt] + pt["p"][4 * g:4 * g + 4], writes=[bt])
                m = 4 * g + dch
                P.add("act", lambda e, m=m, bk=bk: e.activation(ys[:, m, :], bk[:], AF.Copy, scale=scale[:, m:m + 1]),
                      reads=[bt, self.vec_t], writes=[pt["ys"][m]])
        self.out_proj_residual(self.dram[pre + "pool_w_out"], ys, pt["ys"])

    def pool_exchange_patch(self, l, ti, p_buf):
        P = self.P
        nc = self.nc
        pt = self.pool_t
        halo, halo_t = self.halo[l], self.halo_t[l]
        if l not in self.xbuf:
            self.xbuf[l] = (nc.dram_tensor("ph_src_%d" % l, [128, 256], F32).ap(), nc.dram_tensor("ph_dst_%d" % l, [256, 256], F32).ap(),
                            P.tile("phs%d" % l), P.tile("phd%d" % l))
        src, dst, t_src, t_dst = self.xbuf[l]
        hs, hc = self.xs_src[self.xs_n % 8], self.xs_cc[self.xs_n % 8]
        self.xs_n += 1
        P.add("sp", lambda e: e.dma_start(out=src.rearrange("p (c t) -> p c t", c=NC16), in_=self.tail[:]), reads=self.tail_t, writes=[t_src], dma=True, sem_tile=hs)
        P.add("pool", lambda e: e.collective_compute("AllGather", ALU.bypass, replica_groups=PAIR_GROUPS, ins=[src], outs=[dst]),
              reads=[t_src], writes=[t_dst], dma=True, sem_tile=hc, inc=1)
        P.add("sp", lambda e: e.dma_start(out=self.Hy[:], in_=dst[0:128, :].rearrange("p (c t) -> p c t", c=NC16)), reads=[t_dst], writes=[self.Hy_t], dma=True, sem_tile=self.Hy_t)
        P.add("sp", lambda e: e.dma_start(out=halo[:], in_=dst[128:256, :].rearrange("p (c t) -> p c t", c=NC16)), reads=[t_dst], writes=halo_t, dma=True, sem_tile=self.halo_dma_t[l])
        Eg, Ag, Bg = self.pw[:, 0], self.pw[:, 1], self.pw[:, 2]
        pyg = self.pyg
        ptc = self.patch_t
        fX, fY = self.fl[:, 0:1], self.fl[:, 1:2]
        for g in range(4):
            cs = slice(4 * g, 4 * g + 4)
            P.add("dve", lambda e, cs=cs: e.tensor_copy(Eg[:, :, 0:16], self.Hy[:, cs, :]), reads=[self.Hy_t, ptc], writes=[ptc])
            P.add("dve", lambda e, cs=cs: e.tensor_copy(Eg[:, :, 16:32], self.zf[:, cs, :]), reads=[self.zf_t[g], ptc], writes=[ptc])
            P.add("dve", lambda e: e.tensor_tensor(Ag[:, :, 1:32], Eg[:, :, 1:32], Eg[:, :, 0:31], ALU.add), reads=[ptc], writes=[ptc])
            cur = Ag
            if g >= 1:
                P.add("dve", lambda e: e.tensor_tensor(Bg[:, :, 3:32], Ag[:, :, 3:32], Ag[:, :, 1:30], ALU.add), reads=[ptc], writes=[ptc])
                cur = Bg
            if g >= 2:
                P.add("dve", lambda e: e.tensor_tensor(Ag[:, :, 7:32], Bg[:, :, 7:32], Bg[:, :, 3:28], ALU.add), reads=[ptc], writes=[ptc])
                cur = Ag
            if g >= 3:
                P.add("dve", lambda e: e.tensor_tensor(Bg[:, :, 15:32], Ag[:, :, 15:32], Ag[:, :, 7:24], ALU.add), reads=[ptc], writes=[ptc])
                cur = Bg
            w = 2 ** (g + 1)
            P.add("dve", lambda e, cur=cur, cs=cs, w=w: e.scalar_tensor_tensor(pyg[:], cur[:, :, 16:32], 1.0 / w, self.zf[:, cs, :], ALU.mult, ALU.subtract),
                  reads=[ptc, self.zf_t[g]], writes=[ptc])
            P.add("dve", lambda e: e.tensor_scalar(pyg[:], pyg[:], fY, None, ALU.mult), reads=[ptc, self.fl_t], writes=[ptc])
            P.add("dve", lambda e, cs=cs: e.scalar_tensor_tensor(p_buf[:, cs, 0:16], p_buf[:, cs, 0:16], fX, pyg[:], ALU.mult, ALU.add),
                  reads=[ptc, self.fl_t] + pt["p"][4 * g:4 * g + 4], writes=pt["p"][4 * g:4 * g + 4])

    def setup_sgu(self):
        P = self.P
        ls = [l for l in self.layers if LAYER_KIND[l] == "sgu"]
        if not ls:
            return
        l = ls[0]
        pre = "l%d_" % l
        a2 = self.arena2
        wst = a2[:, 0:1024].rearrange("p (h t) -> p h t", h=8)
        bsp = a2[:, 1024:2048].rearrange("p (h t) -> p h t", h=8)
        Rsb = a2[:, 2048:3072].rearrange("p (h t) -> p h t", h=8)
        wst_t, bsp_t, R_t = P.tile("wst"), P.tile("bsp"), P.tile("Rsb")
        wst_d = self.din(pre + "ws_t", [128, 8, 128])
        bsp_d = self.din(pre + "bsp_b", [128, 8, 128])
        P.add("sp", lambda e: e.dma_start(out=wst, in_=wst_d), writes=[wst_t], dma=True, sem_tile=wst_t)
        P.add("sp", lambda e: e.dma_start(out=bsp, in_=bsp_d), writes=[bsp_t], dma=True, sem_tile=bsp_t)
        msk, msk_t = self.const_load("mask_tri", [128, 128])
        self.wsb = P.sbuf("wsb", [128, 8, 128], BF16)
        self.wsb_t = P.tile("wsb")
        P.add("dve", lambda e: e.tensor_tensor(self.wsb[:], wst, msk[:].unsqueeze(1).to_broadcast([128, 8, 128]), ALU.mult),
              reads=[wst_t, msk_t], writes=[self.wsb_t])
        for half in range(2):
            bk, bt = P.bank()

            def mm(e, bk=bk, half=half):
                ins = None
                for hh in range(4):
                    ins = e.matmul(bk[:, hh * 128:(hh + 1) * 128], self.ones[:], self.wsb[:, half * 4 + hh, :], start=True, stop=True)
                return ins
            P.add("pe", mm, reads=[self.ones_t, self.wsb_t], writes=[bt])
            P.add("act", lambda e, bk=bk, half=half: e.activation(Rsb[:, half * 4:half * 4 + 4, :], bk[:].rearrange("p (h t) -> p h t", h=4), AF.Copy),
                  reads=[bt], partial=[R_t])
        self.Bt = P.sbuf("Bt", [128, NC16, 128], F32)
        self.Bt_t = P.tile("Bt")
        vnb = self.vec[pre + "sgu_v_norm_b"]
        for fc in range(NC16):
            hd = fc // 2
            P.add("dve", lambda e, fc=fc, hd=hd: e.scalar_tensor_tensor(self.Bt[:, fc, :], Rsb[:, hd, :], vnb[:, fc:fc + 1], bsp[:, hd, :], ALU.mult, ALU.add),
                  reads=[R_t, bsp_t, self.vec_t], partial=[self.Bt_t])
        self.sgu_t = dict(u=P.tiles("su", NC16), vraw=[P.tiles("svr%d_" % b, 4) for b in range(4)], vn=P.tiles("svn", 4),
                          st=P.tiles("sst", 4))
        self.bnst = P.sbuf("bnst", [128, 4, 4 * 6], F32)
        self.mv = P.sbuf("mv", [128, 4, 4], F32)

    def sgu_mixer(self, l):
        P = self.P
        pre = "l%d_" % l
        st_ = self.sgu_t
        P.barrier()
        self.rmsnorm(self.vec[pre + "norm1"], self.h, self.ht)
        W_in = self.dram[pre + "sgu_w_in"]
        u = self.arena[:, 0:NC16 * T].rearrange("p (c t) -> p c t", c=NC16)
        vn = self.arena[:, NC16 * T:NC16 * T + 4 * D].rearrange("p (b f) -> p b f", b=4)
        vraw = self.arena2[:, 0:4 * D].rearrange("p (b f) -> p b f", b=4)
        for m, bk, bt in self.proj_fm(W_in, 0, D, self.h, self.ht):
            P.add("act", lambda e, m=m, bk=bk: e.activation(u[:, m, :], bk[:], AF.Gelu), reads=[bt], writes=[st_["u"][m]])
        for n in range(4):
            for b, bk, bt in self.proj_tm(W_in, D + n * 512, 512, self.h, self.ht):
                P.add("act", lambda e, b=b, n=n, bk=bk: e.activation(vraw[:, b, n * 512:(n + 1) * 512], bk[:], AF.Gelu),
                      reads=[bt], writes=[st_["vraw"][b][n]])
        for b in range(4):
            vt = st_["vraw"][b]
            stt = st_["st"][b]
            for n in range(4):
                P.add("dve", lambda e, b=b, n=n: e.bn_stats(self.bnst[:, b, n * 6:(n + 1) * 6], vraw[:, b, n * 512:(n + 1) * 512]),
                      reads=[vt[n]], partial=[stt] if n else (), writes=() if n else [stt])
            P.add("dve", lambda e, b=b: e.bn_aggr(self.mv[:, b, 0:2], self.bnst[:, b, :]), reads=[stt], writes=[stt])
            P.add("act", lambda e, b=b: e.activation(self.mv[:, b, 2:3], self.mv[:, b, 1:2], AF.Sqrt, bias=EPS), reads=[stt], writes=[stt])
            P.add("dve", lambda e, b=b: e.reciprocal(self.mv[:, b, 3:4], self.mv[:, b, 2:3]), reads=[stt], writes=[stt])
            P.add("dve", lambda e, b=b: e.tensor_scalar(vn[:, b, :], vraw[:, b, :], self.mv[:, b, 0:1], self.mv[:, b, 3:4], ALU.subtract, ALU.mult),
                  reads=[stt] + vt, writes=[st_["vn"][b]])
        self.pe_warm(90)
        ug = self.h
        gt_ = self.sg
        vnw = self.vec[pre + "sgu_v_norm_w"]
        for fc in range(NC16):
            hd = fc // 2
            bk, bt = P.bank()

            def mm(e, fc=fc, hd=hd, bk=bk):
                ins = None
                for b in range(4):
                    ins = e.matmul(bk[:, b * 128:(b + 1) * 128], vn[:, b, fc * 128:(fc + 1) * 128], self.wsb[:, hd, :], start=True, stop=True)
                return ins
            P.add("pe", mm, reads=st_["vn"] + [self.wsb_t], writes=[bt])
            i = fc % 2

            def gate(e, i=i, fc=fc, bk=bk):
                return e.scalar_tensor_tensor(gt_[i][:].rearrange("p (b t) -> p b t", b=4), bk[:].rearrange("p (b t) -> p b t", b=4),
                                              vnw[:, fc:fc + 1], self.Bt[:, fc, :].unsqueeze(1).to_broadcast([128, 4, 128]), ALU.mult, ALU.add)
            P.add("dve", gate, reads=[bt, self.Bt_t, self.vec_t], writes=[self.sg_t[i]])
            P.add("dve", lambda e, i=i, fc=fc: e.tensor_tensor(ug[:, fc, :], gt_[i][:], u[:, fc, :], ALU.mult),
                  reads=[self.sg_t[i], st_["u"][fc]], writes=[self.ht[fc]])
        self.out_proj_residual(self.dram[pre + "sgu_w_out"], ug, self.ht)

    def setup_gla(self):
        P = self.P
        ls = [l for l in self.layers if LAYER_KIND[l] == "gla"]
        if not ls:
            return
        l = ls[0]
        pre = "l%d_" % l
        self.wgk = P.sbuf("wgk", [128, NC16, 16], BF16)
        self.wgk_t = P.tile("wgk")
        W_in = self.dram[pre + "gla_w_in"]
        P.add("pool", lambda e: e.dma_start(out=self.wgk[:], in_=W_in.rearrange("(k p) n -> p k n", p=128)[:, :, 6144:6160]),
              writes=[self.wgk_t], dma=True, sem_tile=self.wgk_t)
        self.wup = P.sbuf("wup", [16, 1024], BF16)
        self.wup_t = P.tile("wup")
        wup_d = self.din(pre + "gla_w_gk_up", [16, 1024])
        P.add("pool", lambda e: e.dma_start(out=self.wup[:], in_=wup_d), writes=[self.wup_t], dma=True, sem_tile=self.wup_t)
        self.bgk, self.bgk_t = self.const_load(pre + "bgk_b", [128, 1024])
        ml, ml_t = self.const_load("mask_L", [128, 128])
        mu, mu_t = self.const_load("mask_U", [128, 128])
        self.m01, self.m01_t = ml, ml_t
        self.Lm = P.sbuf("Lm", [128, 128], BF16)
        self.Um = P.sbuf("Um", [128, 128], BF16)
        self.Lm_t, self.Um_t = P.tile("Lm"), P.tile("Um")
        P.add("dve", lambda e: e.tensor_scalar(self.Lm[:], ml[:], -1.0 / 16.0, None, ALU.mult), reads=[ml_t], writes=[self.Lm_t])
        P.add("dve", lambda e: e.tensor_scalar(self.Um[:], mu[:], -1.0 / 16.0, None, ALU.mult), reads=[mu_t], writes=[self.Um_t])
        if not self.pair:
            self.S = P.sbuf("S", [128, 8, 512], F32)
            self.S_t = P.tiles("S", 8)
            for i in range(8):
                P.add("dve", lambda e, i=i: e.memset(self.S[:, i, :], 0.0), writes=[self.S_t[i]])
        self.G1, self.G2 = {}, {}
        self.sw_dma_t = P.tile("swdma")
        self.gl = P.sbuf("gl", [16, T], BF16)
        self.gl_t = P.tile("gl")
        self.gla_t = dict(G=P.tiles("gG", 8), eb=P.tiles("geb", 2), em=P.tiles("gem", 2), edl=P.tiles("gedl", 4), Q=P.tiles("gQ", 2),
                          K=P.tiles("gK", 2), kd=P.tiles("gkd", 4), V=P.tiles("gV", 4), SR=P.tiles("gSR", 4), sc=P.tiles("gsc", 2),
                          oraw=P.tiles("goraw", 4), OG=P.tiles("gOG", 4), xg=P.tiles("gxg", 2), ot=P.tiles("got", 2),
                          QA=P.tile("gQA"), QB=P.tile("gQB"), Sb=P.tiles("gSb", 2), Sw=P.tiles("gSw", 2))

    def gla_mixer(self, l, full, ti=0):
        P = self.P
        nc = self.nc
        pre = "l%d_" % l
        gt = self.gla_t
        P.barrier()
        self.rmsnorm(self.vec[pre + "norm1"], self.h, self.ht)
        W_in = self.dram[pre + "gla_w_in"]
        ar = self.arena
        o = 0

        def carve(n):
            nonlocal o
            v = ar[:, o:o + n]
            o += n
            return v
        Ghi = carve(4 * 1024).rearrange("p (b d) -> p b d", b=4)
        Glo = carve(4 * 1024).rearrange("p (b d) -> p b d", b=4)
        QK = carve(4 * T)
        Q = QK[:, 0:2 * T].rearrange("p (c t) -> p c t", c=2)
        Kin = QK[:, 2 * T:4 * T].rearrange("p (c t) -> p c t", c=2)
        osq = QK.rearrange("p (c t) -> p c t", c=4)
        kdec = carve(4 * 256).rearrange("p (b d) -> p b d", b=4)
        V = carve(4 * 512).rearrange("p (b v) -> p b v", b=4)
        SR = carve(4 * T).rearrange("p (c t) -> p c t", c=4)
        scT = [carve(128) for _ in range(2)]
        OG = carve(4 * T).rearrange("p (c t) -> p c t", c=4)
        QA = carve(2 * T).rearrange("p (c t) -> p c t", c=2)
        QB = carve(2 * T).rearrange("p (c t) -> p c t", c=2)
        Sb = carve(2 * 512).rearrange("p (c v) -> p c v", c=2)
        assert o <= 44 * T, o
        a2 = self.arena2
        eb = a2[:, 0:2 * T].rearrange("p (c t) -> p c t", c=2)
        em = a2[:, 2 * T:4 * T].rearrange("p (c t) -> p c t", c=2)
        edl = a2[:, 4 * T:4 * T + 1024].rearrange("p (b d) -> p b d", b=4)
        oraw = a2[:, 3072:3072 + 4 * T].rearrange("p (c t) -> p c t", c=4)
        xg = [a2[:, 5120 + i * 512:5120 + (i + 1) * 512] for i in range(2)]
        otmp = [a2[:, 6144 + i * 512:6144 + (i + 1) * 512] for i in range(2)]
        Sw = a2[:, 7168:8192].rearrange("p (c v) -> p c v", c=2)
        bk, bt = P.bank()

        def mmgl(e, bk=bk):
            ins = None
            for k in range(NC16):
                ins = e.matmul(bk[0:16, :], self.wgk[:, k, :], self.h[:, k, :], start=(k == 0), stop=(k == NC16 - 1))
            return ins
        P.add("pe", mmgl, reads=[self.wgk_t] + self.ht, writes=[bt])
        P.add("act", lambda e, bk=bk: e.activation(self.gl[:], bk[0:16, :], AF.Copy), reads=[bt], writes=[self.gl_t])
        for b in range(4):
            for n in range(2):
                bk, bt = P.bank()
                P.add("pe", lambda e, b=b, n=n, bk=bk: e.matmul(bk[:], self.gl[:, b * 128:(b + 1) * 128], self.wup[:, n * 512:(n + 1) * 512], start=True, stop=True),
                      reads=[self.gl_t, self.wup_t], writes=[bt])
                i = n
                P.add("dve", lambda e, i=i, n=n, bk=bk: e.tensor_tensor(xg[i], bk[:], self.bgk[:, n * 512:(n + 1) * 512], ALU.add),
                      reads=[bt, self.bgk_t], writes=[gt["xg"][i]])
                P.add("act", lambda e, i=i: e.activation(xg[i], xg[i], AF.Exp, scale=-1.0), reads=[gt["xg"][i]], writes=[gt["xg"][i]])
                P.add("act", lambda e, i=i: e.activation(xg[i], xg[i], AF.Ln, bias=1.0), reads=[gt["xg"][i]], writes=[gt["xg"][i]])
                Gt = gt["G"][b * 2 + n]
                P.add("dve", lambda e, i=i, b=b, n=n: e.tensor_copy(Ghi[:, b, n * 512:(n + 1) * 512], xg[i]), reads=[gt["xg"][i]], writes=[Gt])
                P.add("dve", lambda e, i=i, b=b, n=n: e.tensor_tensor(Glo[:, b, n * 512:(n + 1) * 512], xg[i], Ghi[:, b, n * 512:(n + 1) * 512], ALU.subtract),
                      reads=[gt["xg"][i], Gt], writes=[Gt])
        Wout_v = self.dram[pre + "gla_w_out"].rearrange("(k p) n -> p k n", p=128)
        self.pe_warm(50)
        for hd in range(4):
            Gtl = [gt["G"][b * 2 + hd // 2] for b in range(4)]
            for dc in range(2):
                bk, bt = P.bank()
                c0 = hd * 256 + dc * 128

                def mmb(e, bk=bk, c0=c0):
                    ins = None
                    for b in range(4):
                        e.matmul(bk[:, b * 128:(b + 1) * 128], Ghi[:, b, c0:c0 + 128], self.Lm[:], start=True, stop=False)
                        ins = e.matmul(bk[:, b * 128:(b + 1) * 128], Glo[:, b, c0:c0 + 128], self.Lm[:], start=False, stop=True)
                    return ins
                P.add("pe", mmb, reads=[self.Lm_t] + Gtl, writes=[bt])
                P.add("act", lambda e, dc=dc, bk=bk: e.activation(eb[:, dc, :], bk[:], AF.Exp), reads=[bt], writes=[gt["eb"][dc]])
                if full:
                    P.add("act", lambda e, dc=dc, bk=bk: e.activation(em[:, dc, :], bk[:], AF.Exp, scale=-1.0), reads=[bt], writes=[gt["em"][dc]])
            for half in range(2):
                bk, bt = P.bank()

                def mmu(e, bk=bk, half=half, hd=hd):
                    ins = None
                    for bb in range(2):
                        b = half * 2 + bb
                        e.matmul(bk[:, bb * 256:(bb + 1) * 256], self.Um[:], Ghi[:, b, hd * 256:(hd + 1) * 256], start=True, stop=False)
                        ins = e.matmul(bk[:, bb * 256:(bb + 1) * 256], self.Um[:], Glo[:, b, hd * 256:(hd + 1) * 256], start=False, stop=True)
                    return ins
                P.add("pe", mmu, reads=[self.Um_t] + Gtl[half * 2:half * 2 + 2], writes=[bt])
                P.add("act", lambda e, bk=bk, half=half: e.activation(edl[:, half * 2:half * 2 + 2, :], bk[:].rearrange("p (b d) -> p b d", b=2), AF.Exp),
                      reads=[bt], writes=gt["edl"][half * 2:half * 2 + 2])
            for dc in range(2 if (full and not self.pair) else 0):
                P.add("act", lambda e, dc=dc, hd=hd: e.activation(Sb[:, dc, :], self.S[:, hd * 2 + dc, :], AF.Copy),
                      reads=[self.S_t[hd * 2 + dc]], writes=[gt["Sb"][dc]])
            if full:
                for m, bk, bt in self.proj_fm(W_in, hd * 256, 256, self.h, self.ht):
                    P.add("dve", lambda e, m=m, bk=bk: e.scalar_tensor_tensor(Q[:, m, :], bk[:], 0.0625, eb[:, m, :], ALU.mult, ALU.mult),
                          reads=[bt, gt["eb"][m]], writes=[gt["Q"][m]])
                Qv = lambda buf, par: buf.rearrange("p c (b two t) -> p c b two t", two=2, t=64)[:, :, :, par, :]
                P.add("dve", lambda e: e.memset(QA, 0.0), writes=[gt["QA"]])
                P.add("dve", lambda e: e.memset(QB, 0.0), writes=[gt["QB"]])
                P.add("act", lambda e: e.activation(Qv(QA, 0), Qv(Q, 0), AF.Copy), reads=gt["Q"] + [gt["QA"]], writes=[gt["QA"]])
                P.add("act", lambda e: e.activation(Qv(QB, 1), Qv(Q, 1), AF.Copy), reads=gt["Q"] + [gt["QB"]], writes=[gt["QB"]])
                for m, bk, bt in self.proj_fm(W_in, 1024 + hd * 256, 256, self.h, self.ht):
                    P.add("dve", lambda e, m=m, bk=bk: e.tensor_tensor(Kin[:, m, :], bk[:], em[:, m, :], ALU.mult),
                          reads=[bt, gt["em"][m]], writes=[gt["K"][m]])
            for b, bk, bt in self.proj_tm(W_in, 1024 + hd * 256, 256, self.h, self.ht):
                P.add("dve", lambda e, b=b, bk=bk: e.tensor_tensor(kdec[:, b, :], bk[:, 0:256], edl[:, b, :], ALU.mult),
                      reads=[bt, gt["edl"][b]], writes=[gt["kd"][b]])
            for b, bk, bt in self.proj_tm(W_in, 2048 + hd * 512, 512, self.h, self.ht):
                P.add("act", lambda e, b=b, bk=bk: e.activation(V[:, b, :], bk[:], AF.Copy), reads=[bt], writes=[gt["V"][b]])
            if full:
                for m, bk, bt in self.proj_fm(W_in, 4096 + hd * 512, 512, self.h, self.ht):
                    P.add("act", lambda e, m=m, bk=bk: e.activation(SR[:, m, :], bk[:], AF.Silu), reads=[bt], writes=[gt["SR"][m]])
            if self.pair:
                self.gla_pair_head(l, ti, hd, dict(Q=Q, Kin=Kin, kdec=kdec, V=V, SR=SR, scT=scT, OG=OG, QA=QA, QB=QB, Sb=Sb, Sw=Sw,
                                                   eb=eb, oraw=oraw, otmp=otmp, osq=osq, Wout_v=Wout_v))
                continue
            obanks = P.reserve(4) if full else None
            for b in range(4):
                blk = slice(b * 128, (b + 1) * 128)
                if full:
                    bks, bts = P.bank()

                    def mms(e, bks=bks, blk=blk):
                        e.matmul(bks[:, 0:128], Kin[:, 0, blk], Q[:, 0, blk], start=True, stop=False)
                        return e.matmul(bks[:, 0:128], Kin[:, 1, blk], Q[:, 1, blk], start=False, stop=True)
                    P.add("pe", mms, reads=gt["K"] + gt["Q"], writes=[bts])
                    si = b % 2
                    P.add("dve", lambda e, si=si, bks=bks: e.tensor_tensor(scT[si], bks[:, 0:128], self.m01[:], ALU.mult),
                          reads=[bts, self.m01_t], writes=[gt["sc"][si]])
                for cc in range(2):
                    c = 2 * b + cc
                    prt = slice(cc * 64, cc * 64 + 64)
                    if full:
                        Qx, Qxt = (QA, gt["QA"]) if cc == 0 else (QB, gt["QB"])

                        def mmo(e, b=b, cc=cc, blk=blk, Qx=Qx, si=b % 2, obanks=obanks):
                            ins = None
                            for vc in range(4):
                                ob = P.banks[obanks[vc]]
                                if cc == 0:
                                    e.matmul(ob[:, blk], V[:, b, vc * 128:(vc + 1) * 128], scT[si], start=True, stop=False)
                                for dc in range(2):
                                    ins = e.matmul(ob[:, blk], Sb[:, dc, vc * 128:(vc + 1) * 128], Qx[:, dc, blk],
                                                   start=False, stop=(cc == 1 and dc == 1))
                            return ins
                        obt = [P.bank_tiles[i] for i in obanks]
                        rds = [Qxt, gt["V"][b], gt["sc"][b % 2]] + gt["Sb"]
                        first = (b == 0 and cc == 0)
                        P.add("pe", mmo, reads=rds, writes=obt if first else (), partial=() if first else obt)
                    for dc in range(2):
                        si_ = hd * 2 + dc
                        bku, btu = P.bank()
                        P.add("pe", lambda e, bku=bku, prt=prt, b=b, dc=dc: e.matmul(bku[:], kdec[prt, b, dc * 128:(dc + 1) * 128], V[prt, b, :], start=True, stop=True),
                              reads=[gt["kd"][b], gt["V"][b]], writes=[btu])
                        col = 64 * c + 63
                        P.add("dve", lambda e, si_=si_, dc=dc, col=col, bku=bku: e.scalar_tensor_tensor(self.S[:, si_, :], self.S[:, si_, :], eb[:, dc, col:col + 1], bku[:], ALU.mult, ALU.add),
                              reads=[btu, gt["eb"][dc], self.S_t[si_]], writes=[self.S_t[si_]])
                        if full and not (b == 3 and cc == 1):
                            P.add("act", lambda e, si_=si_, dc=dc: e.activation(Sb[:, dc, :], self.S[:, si_, :], AF.Copy),
                                  reads=[self.S_t[si_]], writes=[gt["Sb"][dc]])
            if not full:
                continue
            qk_t = gt["Q"] + gt["K"]
            for vc in range(4):
                ob, obt = P.banks[obanks[vc]], P.bank_tiles[obanks[vc]]
                P.add("act", lambda e, vc=vc, ob=ob: e.activation(oraw[:, vc, :], ob[:], AF.Copy), reads=[obt], writes=[gt["oraw"][vc]])
                P.add("act", lambda e, vc=vc, ob=ob: e.activation(osq[:, vc, :], ob[:], AF.Square), reads=[obt],
                      partial=qk_t if vc else (), writes=() if vc else qk_t)
            P.release(obanks)
            bkn, btn = P.bank()

            def mmn(e, bkn=bkn):
                ins = None
                for vc in range(4):
                    ins = e.matmul(bkn[:], self.ones[:], osq[:, vc, :], start=(vc == 0), stop=(vc == 3))
                return ins
            P.add("pe", mmn, reads=qk_t + [self.ones_t], writes=[btn])
            P.add("act", lambda e, bkn=bkn: e.activation(self.rs_a[:], bkn[:], AF.Sqrt, scale=1.0 / 512.0, bias=EPS), reads=[btn], writes=[self.rs_at])
            P.add("dve", lambda e: e.reciprocal(self.rstd[:], self.rs_a[:]), reads=[self.rs_at], writes=[self.rstd_t])
            onw = self.vec[pre + "gla_out_norm_w"]
            for vc in range(4):
                m = hd * 4 + vc
                i = vc % 2
                P.add("dve", lambda e, vc=vc, m=m, i=i: e.scalar_tensor_tensor(otmp[i], oraw[:, vc, :], onw[:, m:m + 1], self.rstd[:], ALU.mult, ALU.mult),
                      reads=[gt["oraw"][vc], self.rstd_t, self.vec_t], writes=[gt["ot"][i]])
                P.add("dve", lambda e, vc=vc, i=i: e.tensor_tensor(OG[:, vc, :], otmp[i], SR[:, vc, :], ALU.mult),
                      reads=[gt["ot"][i], gt["SR"][vc]], writes=[gt["OG"][vc]])
            for s0 in (0, 1024):
                wv, wt = self.wslot(Wout_v[:, hd * 4:hd * 4 + 4, s0:s0 + 1024], [128, 4, 1024])
                for mi in range(8):
                    m = s0 // 128 + mi
                    bk, bt = P.bank()

                    def mmw(e, wv=wv, mi=mi, bk=bk):
                        ins = None
                        for k in range(4):
                            ins = e.matmul(bk[:], wv[:, k, mi * 128:(mi + 1) * 128], OG[:, k, :], start=(k == 0), stop=(k == 3))
                        return ins
                    P.add("pe", mmw, reads=[wt] + gt["OG"], writes=[bt])
                    P.add("dve", lambda e, m=m, bk=bk: e.tensor_tensor(self.x[:, m, :], self.x[:, m, :], bk[:], ALU.add),
                          reads=[bt, self.xt[m]], writes=[self.xt[m]])

    def gla_pair_head(self, l, ti, hd, B_):
        P = self.P
        nc = self.nc
        gt = self.gla_t
        pre = "l%d_" % l
        Q, Kin, kdec, V, SR, scT, OG, QA, QB, Sb, Sw = (B_[k] for k in ("Q", "Kin", "kdec", "V", "SR", "scT", "OG", "QA", "QB", "Sb", "Sw"))
        eb, oraw, otmp, osq, Wout_v = (B_[k] for k in ("eb", "oraw", "otmp", "osq", "Wout_v"))
        fX, fY = self.fl[:, 0:1], self.fl[:, 1:2]
        wouts = [self.wslot(Wout_v[:, hd * 4:hd * 4 + 4, s0:s0 + 1024], [128, 4, 1024]) for s0 in (0, 1024)]
        for slot in range(2):
            if slot == 0 and ti == 0:
                for dc in range(2):
                    P.add("dve", lambda e, dc=dc: e.memset(Sw[:, dc, :], 0.0), writes=[gt["Sw"][dc]])
            else:
                if slot == 0:
                    gap, gtile = self.G2[(ti - 1, hd)]
                    gsrc = gap[128:256, :]
                else:
                    gap, gtile = self.G1[(ti, hd)]
                    gsrc = gap[0:128, :]
                P.add("sp", lambda e, gsrc=gsrc: e.dma_start(out=Sw, in_=gsrc.rearrange("p (c v) -> p c v", c=2)),
                      reads=[gtile, self.gl_t], writes=gt["Sw"], dma=True, sem_tile=self.sw_dma_t)
            for dc in range(2):
                P.add("act", lambda e, dc=dc: e.activation(Sb[:, dc, :], Sw[:, dc, :], AF.Copy), reads=[gt["Sw"][dc]], writes=[gt["Sb"][dc]])
            obanks = P.reserve(4)
            obt = [P.bank_tiles[i] for i in obanks]
            for b in range(4):
                blk = slice(b * 128, (b + 1) * 128)
                bks, bts = P.bank()

                def mms(e, bks=bks, blk=blk):
                    e.matmul(bks[:, 0:128], Kin[:, 0, blk], Q[:, 0, blk], start=True, stop=False)
                    return e.matmul(bks[:, 0:128], Kin[:, 1, blk], Q[:, 1, blk], start=False, stop=True)
                P.add("pe", mms, reads=gt["K"] + gt["Q"], writes=[bts])
                si = b % 2
                P.add("dve", lambda e, si=si, bks=bks: e.tensor_tensor(scT[si], bks[:, 0:128], self.m01[:], ALU.mult),
                      reads=[bts, self.m01_t], writes=[gt["sc"][si]])
                for cc in range(2):
                    c = 2 * b + cc
                    prt = slice(cc * 64, cc * 64 + 64)
                    Qx, Qxt = (QA, gt["QA"]) if cc == 0 else (QB, gt["QB"])

                    def mmo(e, b=b, cc=cc, blk=blk, Qx=Qx, si=si, obanks=obanks):
                        ins = None
                        for vc in range(4):
                            ob = P.banks[obanks[vc]]
                            if cc == 0:
                                e.matmul(ob[:, blk], V[:, b, vc * 128:(vc + 1) * 128], scT[si], start=True, stop=False)
                            for dc in range(2):
                                ins = e.matmul(ob[:, blk], Sb[:, dc, vc * 128:(vc + 1) * 128], Qx[:, dc, blk],
                                               start=False, stop=(cc == 1 and dc == 1))
                        return ins
                    rds = [Qxt, gt["V"][b], gt["sc"][si]] + gt["Sb"]
                    first = (b == 0 and cc == 0)
                    P.add("pe", mmo, reads=rds, writes=obt if first else (), partial=() if first else obt)
                    for dc in range(2):
                        bku, btu = P.bank()
                        P.add("pe", lambda e, bku=bku, prt=prt, b=b, dc=dc: e.matmul(bku[:], kdec[prt, b, dc * 128:(dc + 1) * 128], V[prt, b, :], start=True, stop=True),
                              reads=[gt["kd"][b], gt["V"][b]], writes=[btu])
                        col = 64 * c + 63
                        P.add("dve", lambda e, dc=dc, col=col, bku=bku: e.scalar_tensor_tensor(Sw[:, dc, :], Sw[:, dc, :], eb[:, dc, col:col + 1], bku[:], ALU.mult, ALU.add),
                              reads=[btu, gt["eb"][dc], gt["Sw"][dc]], writes=[gt["Sw"][dc]])
                        if not (b == 3 and cc == 1):
                            P.add("act", lambda e, dc=dc: e.activation(Sb[:, dc, :], Sw[:, dc, :], AF.Copy), reads=[gt["Sw"][dc]], writes=[gt["Sb"][dc]])
                    self.pe_warm(4)
            self.pe_warm(36 if slot == 0 else 28)
            key = ("g", hd, slot)
            if key not in self.xbuf:
                nm = "%d_%d" % (hd, slot)
                self.xbuf[key] = (nc.dram_tensor("gs_src_" + nm, [128, 1024], F32).ap(), nc.dram_tensor("gs_dst_" + nm, [256, 1024], F32).ap(),
                                  P.tile("gss" + nm), P.tile("gsd" + nm))
            src_d, dst_d, t_src, t_dst = self.xbuf[key]
            hs, hc = self.xs_src[self.xs_n % 8], self.xs_cc[self.xs_n % 8]
            self.xs_n += 1
            P.add("sp", lambda e, src_d=src_d: e.dma_start(out=src_d.rearrange("p (c v) -> p c v", c=2), in_=Sw), reads=gt["Sw"], writes=[t_src], dma=True, sem_tile=hs)
            P.add("pool", lambda e, src_d=src_d, dst_d=dst_d: e.collective_compute("AllGather", ALU.bypass, replica_groups=PAIR_GROUPS, ins=[src_d], outs=[dst_d]),
                  reads=[t_src], writes=[t_dst], dma=True, sem_tile=hc, inc=1)
            (self.G1 if slot == 0 else self.G2)[(ti, hd)] = (dst_d, t_dst)
            for vc in range(4):
                ob = P.banks[obanks[vc]]
                if slot == 0:
                    P.add("act", lambda e, vc=vc, ob=ob: e.activation(oraw[:, vc, :], ob[:], AF.Copy, scale=fX),
                          reads=[obt[vc], self.fl_t], writes=[gt["oraw"][vc]])
                else:
                    P.add("dve", lambda e, vc=vc, ob=ob: e.scalar_tensor_tensor(oraw[:, vc, :], ob[:], fY, oraw[:, vc, :], ALU.mult, ALU.add),
                          reads=[obt[vc], self.fl_t, gt["oraw"][vc]], writes=[gt["oraw"][vc]])
            P.release(obanks)
        qk_t = gt["Q"] + gt["K"]
        for vc in range(4):
            P.add("act", lambda e, vc=vc: e.activation(osq[:, vc, :], oraw[:, vc, :], AF.Square), reads=[gt["oraw"][vc]],
                  partial=qk_t if vc else (), writes=() if vc else qk_t)
        bkn, btn = P.bank()

        def mmn(e, bkn=bkn):
            ins = None
            for vc in range(4):
                ins = e.matmul(bkn[:], self.ones[:], osq[:, vc, :], start=(vc == 0), stop=(vc == 3))
            return ins
        P.add("pe", mmn, reads=qk_t + [self.ones_t], writes=[btn])
        self.pe_warm(28)
        P.add("act", lambda e, bkn=bkn: e.activation(self.rs_a[:], bkn[:], AF.Sqrt, scale=1.0 / 512.0, bias=EPS), reads=[btn], writes=[self.rs_at])
        P.add("dve", lambda e: e.reciprocal(self.rstd[:], self.rs_a[:]), reads=[self.rs_at], writes=[self.rstd_t])
        onw = self.vec[pre + "gla_out_norm_w"]
        for vc in range(4):
            m = hd * 4 + vc
            i = vc % 2
            P.add("dve", lambda e, vc=vc, m=m, i=i: e.scalar_tensor_tensor(otmp[i], oraw[:, vc, :], onw[:, m:m + 1], self.rstd[:], ALU.mult, ALU.mult),
                  reads=[gt["oraw"][vc], self.rstd_t, self.vec_t], writes=[gt["ot"][i]])
            P.add("dve", lambda e, vc=vc, i=i: e.tensor_tensor(OG[:, vc, :], otmp[i], SR[:, vc, :], ALU.mult),
                  reads=[gt["ot"][i], gt["SR"][vc]], writes=[gt["OG"][vc]])
        for (wv, wt), s0 in zip(wouts, (0, 1024)):
            for mi in range(8):
                m = s0 // 128 + mi
                bk, bt = P.bank()

                def mmw(e, wv=wv, mi=mi, bk=bk):
                    ins = None
                    for k in range(4):
                        ins = e.matmul(bk[:], wv[:, k, mi * 128:(mi + 1) * 128], OG[:, k, :], start=(k == 0), stop=(k == 3))
                    return ins
                P.add("pe", mmw, reads=[wt] + gt["OG"], writes=[bt])
                P.add("dve", lambda e, m=m, bk=bk: e.tensor_tensor(self.x[:, m, :], self.x[:, m, :], bk[:], ALU.add),
                      reads=[bt, self.xt[m]], writes=[self.xt[m]])

    def build(self):
        P = self.P
        self.setup()
        n_out = 0
        for ti in range(self.n_tiles):
            modes = self.plan[ti]
            if not modes:
                continue
            self.fill_id = 0
            P.add("sp", lambda e, ti=ti: e.dma_start(out=self.x[:], in_=self.x_in[ti]), writes=self.xt, dma=True, sem_tile=self.t_x_dma)
            for l in self.layers:
                mode = modes.get(l)
                if mode is None:
                    continue
                kind = LAYER_KIND[l]
                if kind == "pool":
                    self.pool_mixer(l, ti, mode)
                elif kind == "sgu":
                    self.sgu_mixer(l)
                else:
                    self.gla_mixer(l, full=(mode == "full"), ti=ti)
                if mode == "full":
                    self.ffn(l)
            if ti in self.store_tiles:
                if self.final_norm:
                    self.rmsnorm(self.vec["final_norm_w"], self.x, self.xt)
                oi = n_out
                n_out += 1
                P.add("sp", lambda e, oi=oi: e.dma_start(out=self.y_out[oi], in_=self.x[:]), reads=self.xt, writes=[P.tile("yo%d" % oi)],
                      dma=True, sem_tile=self.t_x_dma)
        P.emit(final_wait_tiles=[self.t_x_dma])
        return self.nc


def fm_vec(v):
    return np.ascontiguousarray(np.asarray(v, np.float32).reshape(NC16, 128).T)


def tiles_fm(xs):
    n = xs.shape[0] // T
    return np.ascontiguousarray(xs.reshape(n, T, NC16, 128).transpose(0, 3, 2, 1))


def tiles_fm_inv(y):
    n = y.shape[0]
    return np.ascontiguousarray(y.transpose(0, 3, 2, 1).reshape(n * T, D))


def chunk_masks():
    s = np.arange(128)[:, None]
    t = np.arange(128)[None, :]
    same = (s // 64) == (t // 64)
    return ((same & (s <= t)).astype(np.float32), (same & (s > t)).astype(np.float32), (s <= t).astype(np.float32))


def invc_table(is_start):
    out = np.zeros((128, 4, 16), np.float32)
    for g in range(4):
        w = 2 ** (g + 1)
        tt = np.arange(16)
        cnt = np.minimum(tt + 1, w) if is_start else np.full(16, w)
        out[:, g, :] = (1.0 / cnt.astype(np.float32))[None, :]
    return out


def common_inputs(inputs, layers, final_norm=True):
    m = {}
    mL, mU, mT = chunk_masks()
    for l in layers:
        pre = "l%d_" % l
        k = LAYER_KIND[l]
        m["v_" + pre + "norm1"] = fm_vec(inputs[pre + "norm1"])
        m["v_" + pre + "norm2"] = fm_vec(inputs[pre + "norm2"])
        for nm in FFN_PARAMS[1:]:
            m[pre + nm] = np.ascontiguousarray(inputs[pre + nm], dtype=np.float32)
        if k == "pool":
            m["v_" + pre + "pool_scale"] = fm_vec(inputs[pre + "pool_scale"])
            for nm in ("pool_w_in", "pool_w_group", "pool_w_out"):
                m[pre + nm] = np.ascontiguousarray(inputs[pre + nm], dtype=np.float32)
        elif k == "sgu":
            for nm in ("sgu_w_in", "sgu_w_out"):
                m[pre + nm] = np.ascontiguousarray(inputs[pre + nm], dtype=np.float32)
            m["v_" + pre + "sgu_v_norm_w"] = fm_vec(inputs[pre + "sgu_v_norm_w"])
            m["v_" + pre + "sgu_v_norm_b"] = fm_vec(inputs[pre + "sgu_v_norm_b"])
            m[pre + "bsp_b"] = np.ascontiguousarray(np.broadcast_to(np.asarray(inputs[pre + "sgu_b_spatial"], np.float32)[None, :, :], (128, 8, 128)))
            m[pre + "ws_t"] = np.ascontiguousarray(np.asarray(inputs[pre + "sgu_w_spatial"], np.float32).transpose(2, 0, 1))
            m["mask_tri"] = mT
        elif k == "gla":
            m["v_" + pre + "gla_out_norm_w"] = fm_vec(inputs[pre + "gla_out_norm_w"])
            for nm in ("gla_w_in", "gla_w_out", "gla_w_gk_up"):
                m[pre + nm] = np.ascontiguousarray(inputs[pre + nm], dtype=np.float32)
            m[pre + "bgk_b"] = np.ascontiguousarray(np.broadcast_to(np.asarray(inputs[pre + "gla_b_gk"], np.float32)[None, :], (128, 1024)))
            m["mask_L"] = mL
            m["mask_U"] = mU
    if final_norm:
        m["v_final_norm_w"] = fm_vec(inputs["final_norm_w"])
    return m


N_PRE = 8
N_MAIN = 8
PAIR_GROUPS = [[0, 1], [2, 3], [4, 5], [6, 7]]


def full_plan():
    plan = []
    for ti in range(N_PRE + N_MAIN):
        if ti < N_PRE - 1:
            plan.append({0: "full", 1: "full", 2: "state"})
        elif ti == N_PRE - 1:
            plan.append({0: "full", 1: "full", 2: "full", 3: "z"})
        else:
            plan.append({0: "full", 1: "full", 2: "full", 3: "full"})
    return plan


ALL_INPUTS = (
    "x",
    "l0_norm1", "l0_pool_w_in", "l0_pool_w_group", "l0_pool_scale", "l0_pool_w_out",
    "l0_norm2", "l0_ffn_w_gate", "l0_ffn_w_up", "l0_ffn_w_down",
    "l1_norm1", "l1_sgu_w_in", "l1_sgu_v_norm_w", "l1_sgu_v_norm_b", "l1_sgu_w_spatial", "l1_sgu_b_spatial", "l1_sgu_w_out",
    "l1_norm2", "l1_ffn_w_gate", "l1_ffn_w_up", "l1_ffn_w_down",
    "l2_norm1", "l2_gla_w_in", "l2_gla_w_gk_up", "l2_gla_b_gk", "l2_gla_out_norm_w", "l2_gla_w_out",
    "l2_norm2", "l2_ffn_w_gate", "l2_ffn_w_up", "l2_ffn_w_down",
    "l3_norm1", "l3_pool_w_in", "l3_pool_w_group", "l3_pool_scale", "l3_pool_w_out",
    "l3_norm2", "l3_ffn_w_gate", "l3_ffn_w_up", "l3_ffn_w_down",
    "final_norm_w",
)


def kernel(**inputs):
    missing = [n for n in ALL_INPUTS if n not in inputs]
    assert not missing, missing
    x = np.asarray(inputs["x"], np.float32)
    B, S, _ = x.shape
    layers = [0, 1, 2, 3]
    NT = 8
    plan = [{l: "full" for l in layers} for _ in range(NT)]
    b = Builder(layers, NT, plan, list(range(NT)), [0], pair=True)
    nc = b.build()
    common = common_inputs(inputs, layers)
    in_maps = []
    for core in range(8):
        bi, r = core // 2, core % 2
        m = dict(common)
        xt_all = tiles_fm(x[bi])
        m["x_in"] = np.ascontiguousarray(xt_all[r::2])
        m["invc"] = np.ascontiguousarray(np.stack([invc_table(r == 0)], axis=1))
        fl = np.zeros((128, 2), np.float32)
        fl[:, r] = 1.0
        m["flags"] = fl
        in_maps.append(m)
    res = run_bass_kernel_spmd(nc, in_maps, core_ids=list(range(8)))
    out = np.zeros((B, S, D), np.float32)
    for core in range(8):
        bi, r = core // 2, core % 2
        y = res.results[core]["y_out"]
        o4 = out[bi].reshape(S // T, T, D)
        for k in range(NT):
            o4[2 * k + r] = tiles_fm_inv(y[k:k + 1])
    return out
```

```python
import contextlib
import numpy as np
import concourse.bass as bass
import concourse.mybir as mybir
from concourse.bass_utils import run_bass_kernel_spmd

F32 = mybir.dt.float32
BF16 = mybir.dt.bfloat16
AF = mybir.ActivationFunctionType
ALU = mybir.AluOpType

D = 2048
DFF = 5632
NJ = DFF // 128
T = 512
NC16 = 16
EPS = 1e-6
ENGS = ("pe", "act", "dve", "pool", "sp")


class Tile:
    __slots__ = ("name", "writers", "readers", "sem", "dma_count")

    def __init__(self, name):
        self.name = name
        self.writers = []
        self.readers = []
        self.sem = None
        self.dma_count = 0


class Op:
    __slots__ = ("eng", "fn", "deps", "is_dma", "sem_tile", "sig_val", "needs_signal", "inc")

    def __init__(self, eng, fn, is_dma, sem_tile, inc=16):
        self.inc = inc
        self.eng = eng
        self.fn = fn
        self.deps = []
        self.is_dma = is_dma
        self.sem_tile = sem_tile
        self.sig_val = None
        self.needs_signal = False


class Prog:
    def __init__(self, nc):
        self.nc = nc
        self.ops = {e: [] for e in ENGS}
        self.stack = contextlib.ExitStack()
        self.all_tiles = []
        self.banks = []
        self.bank_tiles = []
        self.bank_reserved = [False] * 8
        self.bank_rr = 0
        self.last_op = {}
        self.bar_prev = []
        self.bar_done = {"pe": True, "act": True, "dve": True}

    def barrier(self):
        self.bar_prev = [self.last_op[e] for e in ("pe", "act", "dve") if e in self.last_op]
        self.bar_done = {"pe": False, "act": False, "dve": False}

    def sbuf(self, name, shape, dtype):
        return self.stack.enter_context(self.nc.sbuf_tensor(name, list(shape), dtype))

    def tile(self, name):
        t = Tile(name)
        self.all_tiles.append(t)
        return t

    def tiles(self, name, n):
        return [self.tile("%s%d" % (name, i)) for i in range(n)]

    def init_banks(self):
        for i in range(8):
            self.banks.append(self.stack.enter_context(self.nc.psum_tensor("bank%d" % i, [128, 512], F32)))
            self.bank_tiles.append(self.tile("bank%d" % i))

    def bank(self):
        for _ in range(16):
            i = self.bank_rr
            self.bank_rr = (self.bank_rr + 1) % 8
            if not self.bank_reserved[i]:
                return self.banks[i], self.bank_tiles[i]
        raise RuntimeError("no free psum bank")

    def reserve(self, n):
        out = []
        for _ in range(n):
            for _ in range(16):
                i = self.bank_rr
                self.bank_rr = (self.bank_rr + 1) % 8
                if not self.bank_reserved[i]:
                    break
            else:
                raise RuntimeError("no free psum bank")
            self.bank_reserved[i] = True
            out.append(i)
        return out

    def release(self, idxs):
        for i in idxs:
            self.bank_reserved[i] = False

    def add(self, eng, fn, reads=(), writes=(), partial=(), dma=False, sem_tile=None, inc=16):
        op = Op(eng, fn, dma, sem_tile, inc)
        deps = []
        for t in reads:
            deps.extend(t.writers)
        for t in writes:
            deps.extend(t.writers)
            deps.extend(t.readers)
        for t in partial:
            deps.extend(t.readers)
            if t.readers:
                deps.extend(t.writers)
            elif t.writers:
                deps.append(t.writers[0])
        if eng in self.bar_done and not self.bar_done[eng]:
            deps.extend(self.bar_prev)
            self.bar_done[eng] = True
        if eng in self.bar_done:
            self.last_op[eng] = op
        for t in reads:
            t.readers.append(op)
        for t in writes:
            t.writers = [op]
            t.readers = []
        for t in partial:
            if t.readers:
                t.writers = [op]
                t.readers = []
            else:
                t.writers.append(op)
        seen = set()
        for d in deps:
            if d is op or id(d) in seen:
                continue
            seen.add(id(d))
            if d.eng == "pe" and eng == "pe" and not d.is_dma:
                continue
            op.deps.append(d)
            d.needs_signal = True
        if dma:
            op.needs_signal = True
        self.ops[eng].append(op)
        return op

    def emit(self, final_wait_tiles=()):
        nc = self.nc
        stack = self.stack
        eng_sem = {e: stack.enter_context(nc.semaphore("s_" + e)) for e in ("pe", "act", "dve", "pool")}
        for e in ENGS:
            for op in self.ops[e]:
                if op.is_dma and op.sem_tile.sem is None:
                    op.sem_tile.sem = stack.enter_context(nc.semaphore("d_" + op.sem_tile.name))
        cnt = {e: 0 for e in ENGS}
        for e in ENGS:
            for op in self.ops[e]:
                if op.is_dma:
                    op.sem_tile.dma_count += op.inc
                    op.sig_val = (op.sem_tile.sem, op.sem_tile.dma_count)
                elif op.needs_signal:
                    cnt[e] += 1
                    op.sig_val = (eng_sem[e], cnt[e])
        final_waits = [(t.sem, t.dma_count) for t in final_wait_tiles if t.sem is not None]
        ops = self.ops

        def run_engine(e, engine):
            waited = {}
            for op in ops[e]:
                need = {}
                for d in op.deps:
                    s, v = d.sig_val
                    k = id(s)
                    if k not in need or need[k][1] < v:
                        need[k] = (s, v)
                for k, (s, v) in need.items():
                    if waited.get(k, 0) >= v:
                        continue
                    engine.wait_ge(s, v)
                    waited[k] = v
                ins = op.fn(engine)
                if op.sig_val is not None:
                    s, v = op.sig_val
                    ins.then_inc(s, op.inc if op.is_dma else 1)
            if e == "sp":
                for s, v in final_waits:
                    engine.wait_ge(s, v)

        with nc.Block() as block:
            @block.tensor
            def _(eng):
                run_engine("pe", eng)

            @block.scalar
            def _(eng):
                run_engine("act", eng)

            @block.vector
            def _(eng):
                run_engine("dve", eng)

            @block.gpsimd
            def _(eng):
                run_engine("pool", eng)

            @block.sync
            def _(eng):
                run_engine("sp", eng)
        stack.close()


POOL_PARAMS = ("norm1", "pool_w_in", "pool_w_group", "pool_scale", "pool_w_out")
SGU_PARAMS = ("norm1", "sgu_w_in", "sgu_v_norm_w", "sgu_v_norm_b", "sgu_w_spatial", "sgu_b_spatial", "sgu_w_out")
GLA_PARAMS = ("norm1", "gla_w_in", "gla_w_gk_up", "gla_b_gk", "gla_out_norm_w", "gla_w_out")
FFN_PARAMS = ("norm2", "ffn_w_gate", "ffn_w_up", "ffn_w_down")
LAYER_KIND = ("pool", "sgu", "gla", "pool")
NSLOT = 4


class Builder:
    def __init__(self, layers, n_tiles, plan, store_tiles, seq_start_tiles, final_norm=True, pair=False):
        self.pair = pair
        self.layers = layers
        self.n_tiles = n_tiles
        self.plan = plan
        self.store_tiles = store_tiles
        self.seq_start_tiles = seq_start_tiles
        self.final_norm = final_norm
        nc = self.nc = bass.Bass("TRN2", target_bir_lowering=False)
        self.P = Prog(nc)
        self.dram = {}

    def din(self, name, shape, dtype=F32):
        ap = self.nc.dram_tensor(name, list(shape), dtype, kind="ExternalInput").ap()
        self.dram[name] = ap
        return ap

    def wslot(self, src_ap, shape):
        P = self.P
        i = self.slot_rr
        self.slot_rr = (self.slot_rr + 1) % NSLOT
        sl, tl = self.slots[i], self.slot_tiles[i]
        n = 1
        for s in shape[1:]:
            n *= s
        assert shape[0] == 128
        if len(shape) == 3:
            view = sl[0:shape[0], 0:n].rearrange("p (a b) -> p a b", a=shape[1])
        else:
            view = sl[0:shape[0], 0:n]
        f = self.fill_id
        self.fill_id += 1
        if not self.use_scratch:
            P.add("pool", lambda e, view=view, src=src_ap: e.dma_start(out=view, in_=src), writes=[tl], dma=True, sem_tile=tl)
        elif f not in self.scr:
            P.add("pool", lambda e, view=view, src=src_ap: e.dma_start(out=view, in_=src), writes=[tl], dma=True, sem_tile=tl)
            scr = self.nc.dram_tensor("wscr%d" % f, [128, n], BF16).ap()
            st = P.tile("wscr%d" % f)
            self.scr[f] = (scr, st, n)
            P.add("sp", lambda e, scr=scr, sl=sl, n=n: e.dma_start(out=scr, in_=sl[:, 0:n]), reads=[tl], writes=[st], dma=True, sem_tile=self.scr_sem[i])
        else:
            scr, st, n2 = self.scr[f]
            assert n2 == n, (f, n, n2)
            P.add("pool", lambda e, scr=scr, sl=sl, n=n: e.dma_start(out=sl[:, 0:n], in_=scr), reads=[st], writes=[tl], dma=True, sem_tile=tl)
        return view, tl

    def setup(self):
        P = self.P
        nc = self.nc
        L = self.layers
        P.init_banks()
        self.x_in = self.din("x_in", [self.n_tiles, 128, NC16, T])
        n_store = len(self.store_tiles)
        self.y_out = nc.dram_tensor("y_out", [max(n_store, 1), 128, NC16, T], F32, kind="ExternalOutput").ap()
        self.x = P.sbuf("x", [128, NC16, T], F32)
        self.xt = P.tiles("x", NC16)
        self.h = P.sbuf("h", [128, NC16, T], BF16)
        self.ht = P.tiles("h", NC16)
        self.slots = [P.sbuf("slot%d" % i, [128, 4096], BF16) for i in range(NSLOT)]
        self.slot_tiles = P.tiles("slot", NSLOT)
        self.slot_rr = 0
        self.fill_id = 0
        self.scr = {}
        self.scr_sem = P.tiles("scrsem", NSLOT)
        self.use_scratch = self.pair
        self.t_x_dma = P.tile("xdma")
        self.arena = P.sbuf("arena", [128, 44 * T], BF16)
        self.arena2 = P.sbuf("arena2", [128, 8448], F32)
        self.sq = [P.sbuf("sq%d" % i, [128, 2, T], BF16) for i in range(2)]
        self.sq_t = P.tiles("sq", 2)
        self.sq_rr = 0
        self.rs_a = P.sbuf("rs_a", [128, T], F32)
        self.rs_at = P.tile("rs_a")
        self.rstd = P.sbuf("rstd", [128, T], F32)
        self.rstd_t = P.tile("rstd")
        self.ones = P.sbuf("ones", [128, 128], BF16)
        self.ones_t = P.tile("ones")
        self.warm = P.sbuf("warm", [128, T], BF16)
        P.add("dve", lambda e: e.memset(self.warm[:], 1.0), writes=[self.ones_t])
        P.add("dve", lambda e: e.memset(self.ones[:], 1.0), reads=[self.ones_t], writes=[self.ones_t])
        self.vec = {}
        self.vec_t = P.tile("vecs")
        names = []
        for l in L:
            names += ["l%d_norm1" % l, "l%d_norm2" % l]
            k = LAYER_KIND[l]
            if k == "pool":
                names.append("l%d_pool_scale" % l)
            if k == "gla":
                names.append("l%d_gla_out_norm_w" % l)
            if k == "sgu":
                names += ["l%d_sgu_v_norm_w" % l, "l%d_sgu_v_norm_b" % l]
        if self.final_norm:
            names.append("final_norm_w")
        vb = P.sbuf("vecbuf", [128, len(names), NC16], F32)
        for i, nm in enumerate(names):
            src = self.din("v_" + nm, [128, NC16])
            self.vec[nm] = vb[:, i, :]
            P.add("sp", lambda e, i=i, src=src: e.dma_start(out=vb[:, i, :], in_=src), partial=[self.vec_t], dma=True, sem_tile=self.vec_t)
        for l in L:
            k = LAYER_KIND[l]
            pre = "l%d_" % l
            for nm in FFN_PARAMS[1:]:
                shp = [D, DFF] if nm != "ffn_w_down" else [DFF, D]
                self.din(pre + nm, shp)
            if k == "pool":
                self.din(pre + "pool_w_in", [D, D])
                self.din(pre + "pool_w_group", [4, 512, 512])
                self.din(pre + "pool_w_out", [D, D])
            elif k == "sgu":
                self.din(pre + "sgu_w_in", [D, 2 * D])
                self.din(pre + "sgu_w_out", [D, D])
            elif k == "gla":
                self.din(pre + "gla_w_in", [D, 6160])
                self.din(pre + "gla_w_out", [D, D])
        if self.pair:
            self.fl, self.fl_t = self.const_load("flags", [128, 2])
            self.xs_src = P.tiles("xsrc", 8)
            self.xs_cc = P.tiles("xcc", 8)
            self.xs_n = 0
            self.xbuf = {}
        self.setup_pool()
        self.setup_sgu()
        self.setup_gla()
        P.barrier()

    def const_load(self, name, shape, dtype_sb=F32, eng="sp"):
        P = self.P
        src = self.din(name, shape)
        buf = P.sbuf("c_" + name, shape, dtype_sb)
        t = P.tile("c_" + name)
        P.add(eng, lambda e: e.dma_start(out=buf[:], in_=src), writes=[t], dma=True, sem_tile=t)
        return buf, t

    def rmsnorm(self, wvec, dst, dst_tiles, nfeat=D):
        P = self.P
        x, xt = self.x, self.xt
        self.pe_warm(24)
        bk, bt = P.bank()
        for g4 in range(8):
            i = self.sq_rr
            self.sq_rr ^= 1
            sq, sqt = self.sq[i], self.sq_t[i]
            P.add("act", lambda e, g4=g4, sq=sq: e.activation(sq[:], x[:, 2 * g4:2 * g4 + 2, :], AF.Square),
                  reads=xt[2 * g4:2 * g4 + 2], writes=[sqt])

            def mm(e, g4=g4, sq=sq):
                ins = None
                for c in range(2):
                    ins = e.matmul(bk[:], self.ones[:], sq[:, c, :], start=(g4 == 0 and c == 0), stop=(g4 == 7 and c == 1))
                return ins
            P.add("pe", mm, reads=[sqt, self.ones_t], partial=[bt] if g4 else (), writes=() if g4 else [bt])
        P.add("act", lambda e: e.activation(self.rs_a[:], bk[:], AF.Sqrt, scale=1.0 / nfeat, bias=EPS),
              reads=[bt], writes=[self.rs_at])
        if dst is not self.x:
            self.pe_warm(48)
        P.add("dve", lambda e: e.reciprocal(self.rstd[:], self.rs_a[:]), reads=[self.rs_at], writes=[self.rstd_t])
        for c in range(NC16):
            P.add("dve", lambda e, c=c: e.scalar_tensor_tensor(dst[:, c, :], x[:, c, :], wvec[:, c:c + 1], self.rstd[:], ALU.mult, ALU.mult),
                  reads=[xt[c], self.rstd_t, self.vec_t], writes=[dst_tiles[c]])

    def pe_warm(self, n):
        P = self.P
        bk, bt = P.bank()

        def mm(e, bk=bk):
            ins = None
            for _ in range(n):
                ins = e.matmul(bk[:], self.ones[:], self.warm[:], start=True, stop=True)
            return ins
        P.add("pe", mm, reads=[self.ones_t], writes=[bt])

    def proj_fm(self, W, c0, ncols, src, src_tiles, nk=NC16):
        P = self.P
        Wv = W.rearrange("(k p) n -> p k n", p=128)
        for s0 in range(0, ncols, 256):
            w = min(256, ncols - s0)
            view, tl = self.wslot(Wv[:, :, c0 + s0:c0 + s0 + w], [128, nk, w])
            for mi in range(w // 128):
                bk, bt = P.bank()

                def mm(e, view=view, mi=mi, bk=bk):
                    ins = None
                    for k in range(nk):
                        ins = e.matmul(bk[:], view[:, k, mi * 128:(mi + 1) * 128], src[:, k, :], start=(k == 0), stop=(k == nk - 1))
                    return ins
                P.add("pe", mm, reads=[tl] + list(src_tiles), writes=[bt])
                yield (s0 // 128 + mi, bk, bt)

    def proj_tm(self, W, n0, w, src, src_tiles):
        P = self.P
        Wv = W.rearrange("(k p) n -> p k n", p=128)
        if w == 512:
            v0, t0 = self.wslot(Wv[:, 0:8, n0:n0 + w], [128, 8, w])
            v1, t1 = self.wslot(Wv[:, 8:16, n0:n0 + w], [128, 8, w])
            views = [(v0, t0), (v1, t1)]
            per = 8
        else:
            v0, t0 = self.wslot(Wv[:, :, n0:n0 + w], [128, 16, w])
            views = [(v0, t0)]
            per = 16
        for b in range(4):
            bk, bt = P.bank()

            def mm(e, b=b, bk=bk):
                ins = None
                for k in range(NC16):
                    vw = views[k // per][0]
                    ins = e.matmul(bk[:, 0:w], src[:, k, b * 128:(b + 1) * 128], vw[:, k % per, :], start=(k == 0), stop=(k == NC16 - 1))
                return ins
            P.add("pe", mm, reads=[v[1] for v in views] + list(src_tiles), writes=[bt])
            yield (b, bk, bt)

    def out_proj_residual(self, W, src, src_tiles):
        P = self.P
        for m, bk, bt in self.proj_fm(W, 0, D, src, src_tiles):
            P.add("dve", lambda e, m=m, bk=bk: e.tensor_tensor(self.x[:, m, :], self.x[:, m, :], bk[:], ALU.add),
                  reads=[bt, self.xt[m]], writes=[self.xt[m]])

    def ffn(self, l):
        P = self.P
        pre = "l%d_" % l
        P.barrier()
        self.rmsnorm(self.vec[pre + "norm2"], self.h, self.ht)
        Wg = self.dram[pre + "ffn_w_gate"].rearrange("(k p) n -> p k n", p=128)
        Wu = self.dram[pre + "ffn_w_up"].rearrange("(k p) n -> p k n", p=128)
        Wd = self.dram[pre + "ffn_w_down"].rearrange("(j p) n -> p j n", p=128)
        act = self.arena[:, :].rearrange("p (j t) -> p j t", j=NJ)
        act_t = self.act_t
        sg, sg_t = self.sg, self.sg_t
        for jp in range(NJ // 2):
            gv, gt = self.wslot(Wg[:, :, jp * 256:(jp + 1) * 256], [128, 16, 256])
            uv, ut = self.wslot(Wu[:, :, jp * 256:(jp + 1) * 256], [128, 16, 256])
            for ji in range(2):
                j = jp * 2 + ji
                bg, bgt = P.bank()
                bu, but = P.bank()

                def mmg(e, v=gv, ji=ji, bk=bg):
                    ins = None
                    for k in range(NC16):
                        ins = e.matmul(bk[:], v[:, k, ji * 128:(ji + 1) * 128], self.h[:, k, :], start=(k == 0), stop=(k == NC16 - 1))
                    return ins
                P.add("pe", mmg, reads=[gt] + self.ht, writes=[bgt])
                P.add("pe", lambda e, v=uv, ji=ji, bk=bu: mmg(e, v, ji, bk), reads=[ut] + self.ht, writes=[but])
                i = j % 2
                P.add("act", lambda e, i=i, bk=bg: e.activation(sg[i][:], bk[:], AF.Silu), reads=[bgt], writes=[sg_t[i]])
                P.add("dve", lambda e, i=i, j=j, bk=bu: e.tensor_tensor(act[:, j, :], sg[i][:], bk[:], ALU.mult),
                      reads=[sg_t[i], but], writes=[act_t[j]])
        groups = [(0, 8), (8, 8), (16, 8), (24, 8), (32, 8), (40, 4)]
        for q in range(4):
            bidx = P.reserve(4)
            for (j0, nj) in groups:
                dv, dt = self.wslot(Wd[:, j0:j0 + nj, q * 512:(q + 1) * 512], [128, nj, 512])

                def mmd(e, dv=dv, j0=j0, nj=nj, bidx=bidx):
                    ins = None
                    for jj in range(nj):
                        j = j0 + jj
                        for m in range(4):
                            ins = e.matmul(P.banks[bidx[m]][:], dv[:, jj, m * 128:(m + 1) * 128], act[:, j, :],
                                           start=(j == 0), stop=(j == NJ - 1))
                    return ins
                bts = [P.bank_tiles[i] for i in bidx]
                P.add("pe", mmd, reads=[dt] + act_t[j0:j0 + nj], writes=bts if j0 == 0 else (), partial=() if j0 == 0 else bts)
            for m in range(4):
                c = 4 * q + m
                P.add("dve", lambda e, c=c, bk=P.banks[bidx[m]]: e.tensor_tensor(self.x[:, c, :], self.x[:, c, :], bk[:], ALU.add),
                      reads=[P.bank_tiles[bidx[m]], self.xt[c]], writes=[self.xt[c]])
            P.release(bidx)

    def setup_pool(self):
        P = self.P
        self.act_t = P.tiles("act", NJ)
        self.sg = [P.sbuf("sg%d" % i, [128, T], F32) for i in range(2)]
        self.sg_t = P.tiles("sg", 2)
        if not any(LAYER_KIND[l] == "pool" for l in self.layers):
            return
        self.halo = {}
        self.halo_t = {}
        for l in self.layers:
            if LAYER_KIND[l] == "pool":
                self.halo[l] = P.sbuf("halo%d" % l, [128, NC16, 16], F32)
                self.halo_t[l] = P.tiles("halo%d_" % l, 4)
                for g in range(4):
                    P.add("dve", lambda e, l=l, g=g: e.memset(self.halo[l][:, 4 * g:4 * g + 4, :], 0.0), writes=[self.halo_t[l][g]])
        self.invc, self.invc_t = self.const_load("invc", [128, len(self.seq_start_tiles), 4, 16])
        self.pool_t = dict(z=P.tiles("pz", 2), a=P.tile("pa"), b=P.tile("pb"), p=P.tiles("pp", NC16), ys=P.tiles("pys", NC16))
        if self.pair:
            self.tail = P.sbuf("ptail", [128, NC16, 16], F32)
            self.tail_t = P.tiles("ptail", 4)
            self.zf = P.sbuf("pzf", [128, NC16, 16], F32)
            self.zf_t = P.tiles("pzf", 4)
            self.Hy = P.sbuf("pHy", [128, NC16, 16], F32)
            self.Hy_t = P.tile("pHy")
            self.pw = P.sbuf("ppatch", [128, 3, 4, 32], F32)
            self.pyg = P.sbuf("ppyg", [128, 4, 16], F32)
            self.patch_t = P.tile("ppatch")
            self.halo_dma_t = {l: P.tile("halodma%d" % l) for l in self.halo}

    def pool_mixer(self, l, ti, mode):
        P = self.P
        pre = "l%d_" % l
        P.barrier()
        self.rmsnorm(self.vec[pre + "norm1"], self.h, self.ht)
        W_in = self.dram[pre + "pool_w_in"]
        zb = [self.arena2[:, i * 2112:(i + 1) * 2112].rearrange("p (c t) -> p c t", c=4) for i in range(2)]
        A = self.arena2[:, 2 * 2112:3 * 2112].rearrange("p (c t) -> p c t", c=4)
        B = self.arena2[:, 3 * 2112:4 * 2112].rearrange("p (c t) -> p c t", c=4)
        pt = self.pool_t
        p_buf = self.arena[:, 0:NC16 * T].rearrange("p (c t) -> p c t", c=NC16)
        ys = self.arena[:, NC16 * T:2 * NC16 * T].rearrange("p (c t) -> p c t", c=NC16)
        halo, halo_t = self.halo[l], self.halo_t[l]
        gen = self.proj_fm(W_in, 0, D, self.h, self.ht)
        for g in range(4):
            z, zt = zb[g % 2], pt["z"][g % 2]
            first = True
            for _ in range(4):
                m, bk, bt = next(gen)
                mi = m % 4
                if first:
                    P.add("dve", lambda e, z=z, g=g: e.tensor_copy(z[:, :, 0:16], halo[:, 4 * g:4 * g + 4, :]),
                          reads=[halo_t[g]], writes=[zt])
                    first = False
                P.add("act", lambda e, z=z, mi=mi, bk=bk: e.activation(z[:, mi, 16:528], bk[:], AF.Copy), reads=[bt], partial=[zt])
            if self.pair:
                P.add("dve", lambda e, z=z, g=g: e.tensor_copy(self.tail[:, 4 * g:4 * g + 4, :], z[:, :, 512:528]), reads=[zt], writes=[self.tail_t[g]])
                P.add("dve", lambda e, z=z, g=g: e.tensor_copy(self.zf[:, 4 * g:4 * g + 4, :], z[:, :, 16:32]), reads=[zt], writes=[self.zf_t[g]])
            else:
                P.add("dve", lambda e, z=z, g=g: e.tensor_copy(halo[:, 4 * g:4 * g + 4, :], z[:, :, 512:528]), reads=[zt], writes=[halo_t[g]])
            if mode != "full":
                continue
            ta, tb = pt["a"], pt["b"]
            P.add("dve", lambda e, z=z: e.tensor_tensor(A[:, :, 1:528], z[:, :, 1:528], z[:, :, 0:527], ALU.add), reads=[zt], writes=[ta])
            cur, curt = A, ta
            if g >= 1:
                P.add("dve", lambda e: e.tensor_tensor(B[:, :, 3:528], A[:, :, 3:528], A[:, :, 1:526], ALU.add), reads=[ta], writes=[tb])
                cur, curt = B, tb
            if g >= 2:
                P.add("dve", lambda e: e.tensor_tensor(A[:, :, 7:528], B[:, :, 7:528], B[:, :, 3:524], ALU.add), reads=[tb], writes=[ta])
                cur, curt = A, ta
            if g >= 3:
                P.add("dve", lambda e: e.tensor_tensor(B[:, :, 15:528], A[:, :, 15:528], A[:, :, 7:520], ALU.add), reads=[ta], writes=[tb])
                cur, curt = B, tb
            w = 2 ** (g + 1)
            ptl = pt["p"][4 * g:4 * g + 4]
            P.add("dve", lambda e, cur=cur, z=z, g=g, w=w: e.scalar_tensor_tensor(p_buf[:, 4 * g:4 * g + 4, :], cur[:, :, 16:528], 1.0 / w, z[:, :, 16:528], ALU.mult, ALU.subtract),
                  reads=[curt, zt], writes=ptl)
            if ti in self.seq_start_tiles:
                si = self.seq_start_tiles.index(ti)
                oth, otht = (B, tb) if cur is A else (A, ta)

                def fix1(e, cur=cur, oth=oth, g=g, si=si):
                    iv = self.invc[:, si, g, :].unsqueeze(1).to_broadcast([128, 4, 16])
                    return e.tensor_tensor(oth[:, :, 0:16], cur[:, :, 16:32], iv, ALU.mult)
                P.add("dve", fix1, reads=[curt, self.invc_t], writes=[otht])
                P.add("dve", lambda e, oth=oth, z=z, g=g: e.tensor_tensor(p_buf[:, 4 * g:4 * g + 4, 0:16], oth[:, :, 0:16], z[:, :, 16:32], ALU.subtract),
                      reads=[otht, zt], writes=ptl)
        if mode != "full":
            return
        if self.pair:
            self.pool_exchange_patch(l, ti, p_buf)
        Wgrp = self.dram[pre + "pool_w_group"]
        scale = self.vec[pre + "pool_scale"]
        for g in range(4):
            wv, wt = self.wslot(Wgrp[g].rearrange("(c p) d -> p c d", p=128), [128, 4, 512])
            for dch in range(4):
                bk, bt = P.bank()

                def mm(e, wv=wv, dch=dch, bk=bk, g=g):
                    ins = None
                    for c in range(4):
                        ins = e.matmul(bk[:], wv[:, c, dch * 128:(dch + 1) * 128], p_buf[:, 4 * g + c, :], start=(c == 0), stop=(c == 3))
                    return ins
                P.add("pe", mm, reads=[wt] + pt["p"][4 * g:4 * g + 4], writes=[bt])
                m = 4 * g + dch
                P.add("act", lambda e, m=m, bk=bk: e.activation(ys[:, m, :], bk[:], AF.Copy, scale=scale[:, m:m + 1]),
                      reads=[bt, self.vec_t], writes=[pt["ys"][m]])
        self.out_proj_residual(self.dram[pre + "pool_w_out"], ys, pt["ys"])

    def pool_exchange_patch(self, l, ti, p_buf):
        P = self.P
        nc = self.nc
        pt = self.pool_t
        halo, halo_t = self.halo[l], self.halo_t[l]
        if l not in self.xbuf:
            self.xbuf[l] = (nc.dram_tensor("ph_src_%d" % l, [128, 256], F32).ap(), nc.dram_tensor("ph_dst_%d" % l, [256, 256], F32).ap(),
                            P.tile("phs%d" % l), P.tile("phd%d" % l))
        src, dst, t_src, t_dst = self.xbuf[l]
        hs, hc = self.xs_src[self.xs_n % 8], self.xs_cc[self.xs_n % 8]
        self.xs_n += 1
        P.add("sp", lambda e: e.dma_start(out=src.rearrange("p (c t) -> p c t", c=NC16), in_=self.tail[:]), reads=self.tail_t, writes=[t_src], dma=True, sem_tile=hs)
        P.add("pool", lambda e: e.collective_compute("AllGather", ALU.bypass, replica_groups=PAIR_GROUPS, ins=[src], outs=[dst]),
              reads=[t_src], writes=[t_dst], dma=True, sem_tile=hc, inc=1)
        P.add("sp", lambda e: e.dma_start(out=self.Hy[:], in_=dst[0:128, :].rearrange("p (c t) -> p c t", c=NC16)), reads=[t_dst], writes=[self.Hy_t], dma=True, sem_tile=self.Hy_t)
        P.add("sp", lambda e: e.dma_start(out=halo[:], in_=dst[128:256, :].rearrange("p (c t) -> p c t", c=NC16)), reads=[t_dst], writes=halo_t, dma=True, sem_tile=self.halo_dma_t[l])
        Eg, Ag, Bg = self.pw[:, 0], self.pw[:, 1], self.pw[:, 2]
        pyg = self.pyg
        ptc = self.patch_t
        fX, fY = self.fl[:, 0:1], self.fl[:, 1:2]
        for g in range(4):
            cs = slice(4 * g, 4 * g + 4)
            P.add("dve", lambda e, cs=cs: e.tensor_copy(Eg[:, :, 0:16], self.Hy[:, cs, :]), reads=[self.Hy_t, ptc], writes=[ptc])
            P.add("dve", lambda e, cs=cs: e.tensor_copy(Eg[:, :, 16:32], self.zf[:, cs, :]), reads=[self.zf_t[g], ptc], writes=[ptc])
            P.add("dve", lambda e: e.tensor_tensor(Ag[:, :, 1:32], Eg[:, :, 1:32], Eg[:, :, 0:31], ALU.add), reads=[ptc], writes=[ptc])
            cur = Ag
            if g >= 1:
                P.add("dve", lambda e: e.tensor_tensor(Bg[:, :, 3:32], Ag[:, :, 3:32], Ag[:, :, 1:30], ALU.add), reads=[ptc], writes=[ptc])
                cur = Bg
            if g >= 2:
                P.add("dve", lambda e: e.tensor_tensor(Ag[:, :, 7:32], Bg[:, :, 7:32], Bg[:, :, 3:28], ALU.add), reads=[ptc], writes=[ptc])
                cur = Ag
            if g >= 3:
                P.add("dve", lambda e: e.tensor_tensor(Bg[:, :, 15:32], Ag[:, :, 15:32], Ag[:, :, 7:24], ALU.add), reads=[ptc], writes=[ptc])
                cur = Bg
            w = 2 ** (g + 1)
            P.add("dve", lambda e, cur=cur, cs=cs, w=w: e.scalar_tensor_tensor(pyg[:], cur[:, :, 16:32], 1.0 / w, self.zf[:, cs, :], ALU.mult, ALU.subtract),
                  reads=[ptc, self.zf_t[g]], writes=[ptc])
            P.add("dve", lambda e: e.tensor_scalar(pyg[:], pyg[:], fY, None, ALU.mult), reads=[ptc, self.fl_t], writes=[ptc])
            P.add("dve", lambda e, cs=cs: e.scalar_tensor_tensor(p_buf[:, cs, 0:16], p_buf[:, cs, 0:16], fX, pyg[:], ALU.mult, ALU.add),
                  reads=[ptc, self.fl_t] + pt["p"][4 * g:4 * g + 4], writes=pt["p"][4 * g:4 * g + 4])

    def setup_sgu(self):
        P = self.P
        ls = [l for l in self.layers if LAYER_KIND[l] == "sgu"]
        if not ls:
            return
        l = ls[0]
        pre = "l%d_" % l
        a2 = self.arena2
        wst = a2[:, 0:1024].rearrange("p (h t) -> p h t", h=8)
        bsp = a2[:, 1024:2048].rearrange("p (h t) -> p h t", h=8)
        Rsb = a2[:, 2048:3072].rearrange("p (h t) -> p h t", h=8)
        wst_t, bsp_t, R_t = P.tile("wst"), P.tile("bsp"), P.tile("Rsb")
        wst_d = self.din(pre + "ws_t", [128, 8, 128])
        bsp_d = self.din(pre + "bsp_b", [128, 8, 128])
        P.add("sp", lambda e: e.dma_start(out=wst, in_=wst_d), writes=[wst_t], dma=True, sem_tile=wst_t)
        P.add("sp", lambda e: e.dma_start(out=bsp, in_=bsp_d), writes=[bsp_t], dma=True, sem_tile=bsp_t)
        msk, msk_t = self.const_load("mask_tri", [128, 128])
        self.wsb = P.sbuf("wsb", [128, 8, 128], BF16)
        self.wsb_t = P.tile("wsb")
        P.add("dve", lambda e: e.tensor_tensor(self.wsb[:], wst, msk[:].unsqueeze(1).to_broadcast([128, 8, 128]), ALU.mult),
              reads=[wst_t, msk_t], writes=[self.wsb_t])
        for half in range(2):
            bk, bt = P.bank()

            def mm(e, bk=bk, half=half):
                ins = None
                for hh in range(4):
                    ins = e.matmul(bk[:, hh * 128:(hh + 1) * 128], self.ones[:], self.wsb[:, half * 4 + hh, :], start=True, stop=True)
                return ins
            P.add("pe", mm, reads=[self.ones_t, self.wsb_t], writes=[bt])
            P.add("act", lambda e, bk=bk, half=half: e.activation(Rsb[:, half * 4:half * 4 + 4, :], bk[:].rearrange("p (h t) -> p h t", h=4), AF.Copy),
                  reads=[bt], partial=[R_t])
        self.Bt = P.sbuf("Bt", [128, NC16, 128], F32)
        self.Bt_t = P.tile("Bt")
        vnb = self.vec[pre + "sgu_v_norm_b"]
        for fc in range(NC16):
            hd = fc // 2
            P.add("dve", lambda e, fc=fc, hd=hd: e.scalar_tensor_tensor(self.Bt[:, fc, :], Rsb[:, hd, :], vnb[:, fc:fc + 1], bsp[:, hd, :], ALU.mult, ALU.add),
                  reads=[R_t, bsp_t, self.vec_t], partial=[self.Bt_t])
        self.sgu_t = dict(u=P.tiles("su", NC16), vraw=[P.tiles("svr%d_" % b, 4) for b in range(4)], vn=P.tiles("svn", 4),
                          st=P.tiles("sst", 4))
        self.bnst = P.sbuf("bnst", [128, 4, 4 * 6], F32)
        self.mv = P.sbuf("mv", [128, 4, 4], F32)

    def sgu_mixer(self, l):
        P = self.P
        pre = "l%d_" % l
        st_ = self.sgu_t
        P.barrier()
        self.rmsnorm(self.vec[pre + "norm1"], self.h, self.ht)
        W_in = self.dram[pre + "sgu_w_in"]
        u = self.arena[:, 0:NC16 * T].rearrange("p (c t) -> p c t", c=NC16)
        vn = self.arena[:, NC16 * T:NC16 * T + 4 * D].rearrange("p (b f) -> p b f", b=4)
        vraw = self.arena2[:, 0:4 * D].rearrange("p (b f) -> p b f", b=4)
        for m, bk, bt in self.proj_fm(W_in, 0, D, self.h, self.ht):
            P.add("act", lambda e, m=m, bk=bk: e.activation(u[:, m, :], bk[:], AF.Gelu), reads=[bt], writes=[st_["u"][m]])
        for n in range(4):
            for b, bk, bt in self.proj_tm(W_in, D + n * 512, 512, self.h, self.ht):
                P.add("act", lambda e, b=b, n=n, bk=bk: e.activation(vraw[:, b, n * 512:(n + 1) * 512], bk[:], AF.Gelu),
                      reads=[bt], writes=[st_["vraw"][b][n]])
        for b in range(4):
            vt = st_["vraw"][b]
            stt = st_["st"][b]
            for n in range(4):
                P.add("dve", lambda e, b=b, n=n: e.bn_stats(self.bnst[:, b, n * 6:(n + 1) * 6], vraw[:, b, n * 512:(n + 1) * 512]),
                      reads=[vt[n]], partial=[stt] if n else (), writes=() if n else [stt])
            P.add("dve", lambda e, b=b: e.bn_aggr(self.mv[:, b, 0:2], self.bnst[:, b, :]), reads=[stt], writes=[stt])
            P.add("act", lambda e, b=b: e.activation(self.mv[:, b, 2:3], self.mv[:, b, 1:2], AF.Sqrt, bias=EPS), reads=[stt], writes=[stt])
            P.add("dve", lambda e, b=b: e.reciprocal(self.mv[:, b, 3:4], self.mv[:, b, 2:3]), reads=[stt], writes=[stt])
            P.add("dve", lambda e, b=b: e.tensor_scalar(vn[:, b, :], vraw[:, b, :], self.mv[:, b, 0:1], self.mv[:, b, 3:4], ALU.subtract, ALU.mult),
                  reads=[stt] + vt, writes=[st_["vn"][b]])
        ug = self.h
        gt_ = self.sg
        vnw = self.vec[pre + "sgu_v_norm_w"]
        for fc in range(NC16):
            hd = fc // 2
            bk, bt = P.bank()

            def mm(e, fc=fc, hd=hd, bk=bk):
                ins = None
                for b in range(4):
                    ins = e.matmul(bk[:, b * 128:(b + 1) * 128], vn[:, b, fc * 128:(fc + 1) * 128], self.wsb[:, hd, :], start=True, stop=True)
                return ins
            P.add("pe", mm, reads=st_["vn"] + [self.wsb_t], writes=[bt])
            i = fc % 2

            def gate(e, i=i, fc=fc, bk=bk):
                return e.scalar_tensor_tensor(gt_[i][:].rearrange("p (b t) -> p b t", b=4), bk[:].rearrange("p (b t) -> p b t", b=4),
                                              vnw[:, fc:fc + 1], self.Bt[:, fc, :].unsqueeze(1).to_broadcast([128, 4, 128]), ALU.mult, ALU.add)
            P.add("dve", gate, reads=[bt, self.Bt_t, self.vec_t], writes=[self.sg_t[i]])
            P.add("dve", lambda e, i=i, fc=fc: e.tensor_tensor(ug[:, fc, :], gt_[i][:], u[:, fc, :], ALU.mult),
                  reads=[self.sg_t[i], st_["u"][fc]], writes=[self.ht[fc]])
        self.out_proj_residual(self.dram[pre + "sgu_w_out"], ug, self.ht)

    def setup_gla(self):
        P = self.P
        ls = [l for l in self.layers if LAYER_KIND[l] == "gla"]
        if not ls:
            return
        l = ls[0]
        pre = "l%d_" % l
        self.wgk = P.sbuf("wgk", [128, NC16, 16], BF16)
        self.wgk_t = P.tile("wgk")
        W_in = self.dram[pre + "gla_w_in"]
        P.add("pool", lambda e: e.dma_start(out=self.wgk[:], in_=W_in.rearrange("(k p) n -> p k n", p=128)[:, :, 6144:6160]),
              writes=[self.wgk_t], dma=True, sem_tile=self.wgk_t)
        self.wup = P.sbuf("wup", [16, 1024], BF16)
        self.wup_t = P.tile("wup")
        wup_d = self.din(pre + "gla_w_gk_up", [16, 1024])
        P.add("pool", lambda e: e.dma_start(out=self.wup[:], in_=wup_d), writes=[self.wup_t], dma=True, sem_tile=self.wup_t)
        self.bgk, self.bgk_t = self.const_load(pre + "bgk_b", [128, 1024])
        ml, ml_t = self.const_load("mask_L", [128, 128])
        mu, mu_t = self.const_load("mask_U", [128, 128])
        self.m01, self.m01_t = ml, ml_t
        self.Lm = P.sbuf("Lm", [128, 128], BF16)
        self.Um = P.sbuf("Um", [128, 128], BF16)
        self.Lm_t, self.Um_t = P.tile("Lm"), P.tile("Um")
        P.add("dve", lambda e: e.tensor_scalar(self.Lm[:], ml[:], -1.0 / 16.0, None, ALU.mult), reads=[ml_t], writes=[self.Lm_t])
        P.add("dve", lambda e: e.tensor_scalar(self.Um[:], mu[:], -1.0 / 16.0, None, ALU.mult), reads=[mu_t], writes=[self.Um_t])
        if not self.pair:
            self.S = P.sbuf("S", [128, 8, 512], F32)
            self.S_t = P.tiles("S", 8)
            for i in range(8):
                P.add("dve", lambda e, i=i: e.memset(self.S[:, i, :], 0.0), writes=[self.S_t[i]])
        self.G1, self.G2 = {}, {}
        self.sw_dma_t = P.tile("swdma")
        self.gl = P.sbuf("gl", [16, T], BF16)
        self.gl_t = P.tile("gl")
        self.gla_t = dict(G=P.tiles("gG", 8), eb=P.tiles("geb", 2), em=P.tiles("gem", 2), edl=P.tiles("gedl", 4), Q=P.tiles("gQ", 2),
                          K=P.tiles("gK", 2), kd=P.tiles("gkd", 4), V=P.tiles("gV", 4), SR=P.tiles("gSR", 4), sc=P.tiles("gsc", 2),
                          oraw=P.tiles("goraw", 4), OG=P.tiles("gOG", 4), xg=P.tiles("gxg", 2), ot=P.tiles("got", 2),
                          QA=P.tile("gQA"), QB=P.tile("gQB"), Sb=P.tiles("gSb", 2), Sw=P.tiles("gSw", 2))

    def gla_mixer(self, l, full, ti=0):
        P = self.P
        nc = self.nc
        pre = "l%d_" % l
        gt = self.gla_t
        P.barrier()
        self.rmsnorm(self.vec[pre + "norm1"], self.h, self.ht)
        W_in = self.dram[pre + "gla_w_in"]
        ar = self.arena
        o = 0

        def carve(n):
            nonlocal o
            v = ar[:, o:o + n]
            o += n
            return v
        Ghi = carve(4 * 1024).rearrange("p (b d) -> p b d", b=4)
        Glo = carve(4 * 1024).rearrange("p (b d) -> p b d", b=4)
        QK = carve(4 * T)
        Q = QK[:, 0:2 * T].rearrange("p (c t) -> p c t", c=2)
        Kin = QK[:, 2 * T:4 * T].rearrange("p (c t) -> p c t", c=2)
        osq = QK.rearrange("p (c t) -> p c t", c=4)
        kdec = carve(4 * 256).rearrange("p (b d) -> p b d", b=4)
        V = carve(4 * 512).rearrange("p (b v) -> p b v", b=4)
        SR = carve(4 * T).rearrange("p (c t) -> p c t", c=4)
        scT = [carve(128) for _ in range(2)]
        OG = carve(4 * T).rearrange("p (c t) -> p c t", c=4)
        QA = carve(2 * T).rearrange("p (c t) -> p c t", c=2)
        QB = carve(2 * T).rearrange("p (c t) -> p c t", c=2)
        Sb = carve(2 * 512).rearrange("p (c v) -> p c v", c=2)
        assert o <= 44 * T, o
        a2 = self.arena2
        eb = a2[:, 0:2 * T].rearrange("p (c t) -> p c t", c=2)
        em = a2[:, 2 * T:4 * T].rearrange("p (c t) -> p c t", c=2)
        edl = a2[:, 4 * T:4 * T + 1024].rearrange("p (b d) -> p b d", b=4)
        oraw = a2[:, 3072:3072 + 4 * T].rearrange("p (c t) -> p c t", c=4)
        xg = [a2[:, 5120 + i * 512:5120 + (i + 1) * 512] for i in range(2)]
        otmp = [a2[:, 6144 + i * 512:6144 + (i + 1) * 512] for i in range(2)]
        Sw = a2[:, 7168:8192].rearrange("p (c v) -> p c v", c=2)
        bk, bt = P.bank()

        def mmgl(e, bk=bk):
            ins = None
            for k in range(NC16):
                ins = e.matmul(bk[0:16, :], self.wgk[:, k, :], self.h[:, k, :], start=(k == 0), stop=(k == NC16 - 1))
            return ins
        P.add("pe", mmgl, reads=[self.wgk_t] + self.ht, writes=[bt])
        P.add("act", lambda e, bk=bk: e.activation(self.gl[:], bk[0:16, :], AF.Copy), reads=[bt], writes=[self.gl_t])
        for b in range(4):
            for n in range(2):
                bk, bt = P.bank()
                P.add("pe", lambda e, b=b, n=n, bk=bk: e.matmul(bk[:], self.gl[:, b * 128:(b + 1) * 128], self.wup[:, n * 512:(n + 1) * 512], start=True, stop=True),
                      reads=[self.gl_t, self.wup_t], writes=[bt])
                i = n
                P.add("dve", lambda e, i=i, n=n, bk=bk: e.tensor_tensor(xg[i], bk[:], self.bgk[:, n * 512:(n + 1) * 512], ALU.add),
                      reads=[bt, self.bgk_t], writes=[gt["xg"][i]])
                P.add("act", lambda e, i=i: e.activation(xg[i], xg[i], AF.Exp, scale=-1.0), reads=[gt["xg"][i]], writes=[gt["xg"][i]])
                P.add("act", lambda e, i=i: e.activation(xg[i], xg[i], AF.Ln, bias=1.0), reads=[gt["xg"][i]], writes=[gt["xg"][i]])
                Gt = gt["G"][b * 2 + n]
                P.add("dve", lambda e, i=i, b=b, n=n: e.tensor_copy(Ghi[:, b, n * 512:(n + 1) * 512], xg[i]), reads=[gt["xg"][i]], writes=[Gt])
                P.add("dve", lambda e, i=i, b=b, n=n: e.tensor_tensor(Glo[:, b, n * 512:(n + 1) * 512], xg[i], Ghi[:, b, n * 512:(n + 1) * 512], ALU.subtract),
                      reads=[gt["xg"][i], Gt], writes=[Gt])
        Wout_v = self.dram[pre + "gla_w_out"].rearrange("(k p) n -> p k n", p=128)
        for hd in range(4):
            Gtl = [gt["G"][b * 2 + hd // 2] for b in range(4)]
            for dc in range(2):
                bk, bt = P.bank()
                c0 = hd * 256 + dc * 128

                def mmb(e, bk=bk, c0=c0):
                    ins = None
                    for b in range(4):
                        e.matmul(bk[:, b * 128:(b + 1) * 128], Ghi[:, b, c0:c0 + 128], self.Lm[:], start=True, stop=False)
                        ins = e.matmul(bk[:, b * 128:(b + 1) * 128], Glo[:, b, c0:c0 + 128], self.Lm[:], start=False, stop=True)
                    return ins
                P.add("pe", mmb, reads=[self.Lm_t] + Gtl, writes=[bt])
                P.add("act", lambda e, dc=dc, bk=bk: e.activation(eb[:, dc, :], bk[:], AF.Exp), reads=[bt], writes=[gt["eb"][dc]])
                if full:
                    P.add("act", lambda e, dc=dc, bk=bk: e.activation(em[:, dc, :], bk[:], AF.Exp, scale=-1.0), reads=[bt], writes=[gt["em"][dc]])
            for half in range(2):
                bk, bt = P.bank()

                def mmu(e, bk=bk, half=half, hd=hd):
                    ins = None
                    for bb in range(2):
                        b = half * 2 + bb
                        e.matmul(bk[:, bb * 256:(bb + 1) * 256], self.Um[:], Ghi[:, b, hd * 256:(hd + 1) * 256], start=True, stop=False)
                        ins = e.matmul(bk[:, bb * 256:(bb + 1) * 256], self.Um[:], Glo[:, b, hd * 256:(hd + 1) * 256], start=False, stop=True)
                    return ins
                P.add("pe", mmu, reads=[self.Um_t] + Gtl[half * 2:half * 2 + 2], writes=[bt])
                P.add("act", lambda e, bk=bk, half=half: e.activation(edl[:, half * 2:half * 2 + 2, :], bk[:].rearrange("p (b d) -> p b d", b=2), AF.Exp),
                      reads=[bt], writes=gt["edl"][half * 2:half * 2 + 2])
            for dc in range(2 if (full and not self.pair) else 0):
                P.add("act", lambda e, dc=dc, hd=hd: e.activation(Sb[:, dc, :], self.S[:, hd * 2 + dc, :], AF.Copy),
                      reads=[self.S_t[hd * 2 + dc]], writes=[gt["Sb"][dc]])
            if full:
                for m, bk, bt in self.proj_fm(W_in, hd * 256, 256, self.h, self.ht):
                    P.add("dve", lambda e, m=m, bk=bk: e.scalar_tensor_tensor(Q[:, m, :], bk[:], 0.0625, eb[:, m, :], ALU.mult, ALU.mult),
                          reads=[bt, gt["eb"][m]], writes=[gt["Q"][m]])
                Qv = lambda buf, par: buf.rearrange("p c (b two t) -> p c b two t", two=2, t=64)[:, :, :, par, :]
                P.add("dve", lambda e: e.memset(QA, 0.0), writes=[gt["QA"]])
                P.add("dve", lambda e: e.memset(QB, 0.0), writes=[gt["QB"]])
                P.add("act", lambda e: e.activation(Qv(QA, 0), Qv(Q, 0), AF.Copy), reads=gt["Q"] + [gt["QA"]], writes=[gt["QA"]])
                P.add("act", lambda e: e.activation(Qv(QB, 1), Qv(Q, 1), AF.Copy), reads=gt["Q"] + [gt["QB"]], writes=[gt["QB"]])
                for m, bk, bt in self.proj_fm(W_in, 1024 + hd * 256, 256, self.h, self.ht):
                    P.add("dve", lambda e, m=m, bk=bk: e.tensor_tensor(Kin[:, m, :], bk[:], em[:, m, :], ALU.mult),
                          reads=[bt, gt["em"][m]], writes=[gt["K"][m]])
            for b, bk, bt in self.proj_tm(W_in, 1024 + hd * 256, 256, self.h, self.ht):
                P.add("dve", lambda e, b=b, bk=bk: e.tensor_tensor(kdec[:, b, :], bk[:, 0:256], edl[:, b, :], ALU.mult),
                      reads=[bt, gt["edl"][b]], writes=[gt["kd"][b]])
            for b, bk, bt in self.proj_tm(W_in, 2048 + hd * 512, 512, self.h, self.ht):
                P.add("act", lambda e, b=b, bk=bk: e.activation(V[:, b, :], bk[:], AF.Copy), reads=[bt], writes=[gt["V"][b]])
            if full:
                for m, bk, bt in self.proj_fm(W_in, 4096 + hd * 512, 512, self.h, self.ht):
                    P.add("act", lambda e, m=m, bk=bk: e.activation(SR[:, m, :], bk[:], AF.Silu), reads=[bt], writes=[gt["SR"][m]])
            if self.pair:
                self.gla_pair_head(l, ti, hd, dict(Q=Q, Kin=Kin, kdec=kdec, V=V, SR=SR, scT=scT, OG=OG, QA=QA, QB=QB, Sb=Sb, Sw=Sw,
                                                   eb=eb, oraw=oraw, otmp=otmp, osq=osq, Wout_v=Wout_v))
                continue
            obanks = P.reserve(4) if full else None
            for b in range(4):
                blk = slice(b * 128, (b + 1) * 128)
                if full:
                    bks, bts = P.bank()

                    def mms(e, bks=bks, blk=blk):
                        e.matmul(bks[:, 0:128], Kin[:, 0, blk], Q[:, 0, blk], start=True, stop=False)
                        return e.matmul(bks[:, 0:128], Kin[:, 1, blk], Q[:, 1, blk], start=False, stop=True)
                    P.add("pe", mms, reads=gt["K"] + gt["Q"], writes=[bts])
                    si = b % 2
                    P.add("dve", lambda e, si=si, bks=bks: e.tensor_tensor(scT[si], bks[:, 0:128], self.m01[:], ALU.mult),
                          reads=[bts, self.m01_t], writes=[gt["sc"][si]])
                for cc in range(2):
                    c = 2 * b + cc
                    prt = slice(cc * 64, cc * 64 + 64)
                    if full:
                        Qx, Qxt = (QA, gt["QA"]) if cc == 0 else (QB, gt["QB"])

                        def mmo(e, b=b, cc=cc, blk=blk, Qx=Qx, si=b % 2, obanks=obanks):
                            ins = None
                            for vc in range(4):
                                ob = P.banks[obanks[vc]]
                                if cc == 0:
                                    e.matmul(ob[:, blk], V[:, b, vc * 128:(vc + 1) * 128], scT[si], start=True, stop=False)
                                for dc in range(2):
                                    ins = e.matmul(ob[:, blk], Sb[:, dc, vc * 128:(vc + 1) * 128], Qx[:, dc, blk],
                                                   start=False, stop=(cc == 1 and dc == 1))
                            return ins
                        obt = [P.bank_tiles[i] for i in obanks]
                        rds = [Qxt, gt["V"][b], gt["sc"][b % 2]] + gt["Sb"]
                        first = (b == 0 and cc == 0)
                        P.add("pe", mmo, reads=rds, writes=obt if first else (), partial=() if first else obt)
                    for dc in range(2):
                        si_ = hd * 2 + dc
                        bku, btu = P.bank()
                        P.add("pe", lambda e, bku=bku, prt=prt, b=b, dc=dc: e.matmul(bku[:], kdec[prt, b, dc * 128:(dc + 1) * 128], V[prt, b, :], start=True, stop=True),
                              reads=[gt["kd"][b], gt["V"][b]], writes=[btu])
                        col = 64 * c + 63
                        P.add("dve", lambda e, si_=si_, dc=dc, col=col, bku=bku: e.scalar_tensor_tensor(self.S[:, si_, :], self.S[:, si_, :], eb[:, dc, col:col + 1], bku[:], ALU.mult, ALU.add),
                              reads=[btu, gt["eb"][dc], self.S_t[si_]], writes=[self.S_t[si_]])
                        if full and not (b == 3 and cc == 1):
                            P.add("act", lambda e, si_=si_, dc=dc: e.activation(Sb[:, dc, :], self.S[:, si_, :], AF.Copy),
                                  reads=[self.S_t[si_]], writes=[gt["Sb"][dc]])
            if not full:
                continue
            qk_t = gt["Q"] + gt["K"]
            for vc in range(4):
                ob, obt = P.banks[obanks[vc]], P.bank_tiles[obanks[vc]]
                P.add("act", lambda e, vc=vc, ob=ob: e.activation(oraw[:, vc, :], ob[:], AF.Copy), reads=[obt], writes=[gt["oraw"][vc]])
                P.add("act", lambda e, vc=vc, ob=ob: e.activation(osq[:, vc, :], ob[:], AF.Square), reads=[obt],
                      partial=qk_t if vc else (), writes=() if vc else qk_t)
            P.release(obanks)
            bkn, btn = P.bank()

            def mmn(e, bkn=bkn):
                ins = None
                for vc in range(4):
                    ins = e.matmul(bkn[:], self.ones[:], osq[:, vc, :], start=(vc == 0), stop=(vc == 3))
                return ins
            P.add("pe", mmn, reads=qk_t + [self.ones_t], writes=[btn])
            P.add("act", lambda e, bkn=bkn: e.activation(self.rs_a[:], bkn[:], AF.Sqrt, scale=1.0 / 512.0, bias=EPS), reads=[btn], writes=[self.rs_at])
            P.add("dve", lambda e: e.reciprocal(self.rstd[:], self.rs_a[:]), reads=[self.rs_at], writes=[self.rstd_t])
            onw = self.vec[pre + "gla_out_norm_w"]
            for vc in range(4):
                m = hd * 4 + vc
                i = vc % 2
                P.add("dve", lambda e, vc=vc, m=m, i=i: e.scalar_tensor_tensor(otmp[i], oraw[:, vc, :], onw[:, m:m + 1], self.rstd[:], ALU.mult, ALU.mult),
                      reads=[gt["oraw"][vc], self.rstd_t, self.vec_t], writes=[gt["ot"][i]])
                P.add("dve", lambda e, vc=vc, i=i: e.tensor_tensor(OG[:, vc, :], otmp[i], SR[:, vc, :], ALU.mult),
                      reads=[gt["ot"][i], gt["SR"][vc]], writes=[gt["OG"][vc]])
            for s0 in (0, 1024):
                wv, wt = self.wslot(Wout_v[:, hd * 4:hd * 4 + 4, s0:s0 + 1024], [128, 4, 1024])
                for mi in range(8):
                    m = s0 // 128 + mi
                    bk, bt = P.bank()

                    def mmw(e, wv=wv, mi=mi, bk=bk):
                        ins = None
                        for k in range(4):
                            ins = e.matmul(bk[:], wv[:, k, mi * 128:(mi + 1) * 128], OG[:, k, :], start=(k == 0), stop=(k == 3))
                        return ins
                    P.add("pe", mmw, reads=[wt] + gt["OG"], writes=[bt])
                    P.add("dve", lambda e, m=m, bk=bk: e.tensor_tensor(self.x[:, m, :], self.x[:, m, :], bk[:], ALU.add),
                          reads=[bt, self.xt[m]], writes=[self.xt[m]])

    def gla_pair_head(self, l, ti, hd, B_):
        P = self.P
        nc = self.nc
        gt = self.gla_t
        pre = "l%d_" % l
        Q, Kin, kdec, V, SR, scT, OG, QA, QB, Sb, Sw = (B_[k] for k in ("Q", "Kin", "kdec", "V", "SR", "scT", "OG", "QA", "QB", "Sb", "Sw"))
        eb, oraw, otmp, osq, Wout_v = (B_[k] for k in ("eb", "oraw", "otmp", "osq", "Wout_v"))
        fX, fY = self.fl[:, 0:1], self.fl[:, 1:2]
        wouts = [self.wslot(Wout_v[:, hd * 4:hd * 4 + 4, s0:s0 + 1024], [128, 4, 1024]) for s0 in (0, 1024)]
        for slot in range(2):
            if slot == 0 and ti == 0:
                for dc in range(2):
                    P.add("dve", lambda e, dc=dc: e.memset(Sw[:, dc, :], 0.0), writes=[gt["Sw"][dc]])
            else:
                if slot == 0:
                    gap, gtile = self.G2[(ti - 1, hd)]
                    gsrc = gap[128:256, :]
                else:
                    gap, gtile = self.G1[(ti, hd)]
                    gsrc = gap[0:128, :]
                P.add("sp", lambda e, gsrc=gsrc: e.dma_start(out=Sw, in_=gsrc.rearrange("p (c v) -> p c v", c=2)),
                      reads=[gtile, self.gl_t], writes=gt["Sw"], dma=True, sem_tile=self.sw_dma_t)
            for dc in range(2):
                P.add("act", lambda e, dc=dc: e.activation(Sb[:, dc, :], Sw[:, dc, :], AF.Copy), reads=[gt["Sw"][dc]], writes=[gt["Sb"][dc]])
            obanks = P.reserve(4)
            obt = [P.bank_tiles[i] for i in obanks]
            for b in range(4):
                blk = slice(b * 128, (b + 1) * 128)
                bks, bts = P.bank()

                def mms(e, bks=bks, blk=blk):
                    e.matmul(bks[:, 0:128], Kin[:, 0, blk], Q[:, 0, blk], start=True, stop=False)
                    return e.matmul(bks[:, 0:128], Kin[:, 1, blk], Q[:, 1, blk], start=False, stop=True)
                P.add("pe", mms, reads=gt["K"] + gt["Q"], writes=[bts])
                si = b % 2
                P.add("dve", lambda e, si=si, bks=bks: e.tensor_tensor(scT[si], bks[:, 0:128], self.m01[:], ALU.mult),
                      reads=[bts, self.m01_t], writes=[gt["sc"][si]])
                for cc in range(2):
                    c = 2 * b + cc
                    prt = slice(cc * 64, cc * 64 + 64)
                    Qx, Qxt = (QA, gt["QA"]) if cc == 0 else (QB, gt["QB"])

                    def mmo(e, b=b, cc=cc, blk=blk, Qx=Qx, si=si, obanks=obanks):
                        ins = None
                        for vc in range(4):
                            ob = P.banks[obanks[vc]]
                            if cc == 0:
                                e.matmul(ob[:, blk], V[:, b, vc * 128:(vc + 1) * 128], scT[si], start=True, stop=False)
                            for dc in range(2):
                                ins = e.matmul(ob[:, blk], Sb[:, dc, vc * 128:(vc + 1) * 128], Qx[:, dc, blk],
                                               start=False, stop=(cc == 1 and dc == 1))
                        return ins
                    rds = [Qxt, gt["V"][b], gt["sc"][si]] + gt["Sb"]
                    first = (b == 0 and cc == 0)
                    P.add("pe", mmo, reads=rds, writes=obt if first else (), partial=() if first else obt)
                    for dc in range(2):
                        bku, btu = P.bank()
                        P.add("pe", lambda e, bku=bku, prt=prt, b=b, dc=dc: e.matmul(bku[:], kdec[prt, b, dc * 128:(dc + 1) * 128], V[prt, b, :], start=True, stop=True),
                              reads=[gt["kd"][b], gt["V"][b]], writes=[btu])
                        col = 64 * c + 63
                        P.add("dve", lambda e, dc=dc, col=col, bku=bku: e.scalar_tensor_tensor(Sw[:, dc, :], Sw[:, dc, :], eb[:, dc, col:col + 1], bku[:], ALU.mult, ALU.add),
                              reads=[btu, gt["eb"][dc], gt["Sw"][dc]], writes=[gt["Sw"][dc]])
                        if not (b == 3 and cc == 1):
                            P.add("act", lambda e, dc=dc: e.activation(Sb[:, dc, :], Sw[:, dc, :], AF.Copy), reads=[gt["Sw"][dc]], writes=[gt["Sb"][dc]])
            key = ("g", hd, slot)
            if key not in self.xbuf:
                nm = "%d_%d" % (hd, slot)
                self.xbuf[key] = (nc.dram_tensor("gs_src_" + nm, [128, 1024], F32).ap(), nc.dram_tensor("gs_dst_" + nm, [256, 1024], F32).ap(),
                                  P.tile("gss" + nm), P.tile("gsd" + nm))
            src_d, dst_d, t_src, t_dst = self.xbuf[key]
            hs, hc = self.xs_src[self.xs_n % 8], self.xs_cc[self.xs_n % 8]
            self.xs_n += 1
            P.add("sp", lambda e, src_d=src_d: e.dma_start(out=src_d.rearrange("p (c v) -> p c v", c=2), in_=Sw), reads=gt["Sw"], writes=[t_src], dma=True, sem_tile=hs)
            P.add("pool", lambda e, src_d=src_d, dst_d=dst_d: e.collective_compute("AllGather", ALU.bypass, replica_groups=PAIR_GROUPS, ins=[src_d], outs=[dst_d]),
                  reads=[t_src], writes=[t_dst], dma=True, sem_tile=hc, inc=1)
            (self.G1 if slot == 0 else self.G2)[(ti, hd)] = (dst_d, t_dst)
            for vc in range(4):
                ob = P.banks[obanks[vc]]
                if slot == 0:
                    P.add("act", lambda e, vc=vc, ob=ob: e.activation(oraw[:, vc, :], ob[:], AF.Copy, scale=fX),
                          reads=[obt[vc], self.fl_t], writes=[gt["oraw"][vc]])
                else:
                    P.add("dve", lambda e, vc=vc, ob=ob: e.scalar_tensor_tensor(oraw[:, vc, :], ob[:], fY, oraw[:, vc, :], ALU.mult, ALU.add),
                          reads=[obt[vc], self.fl_t, gt["oraw"][vc]], writes=[gt["oraw"][vc]])
            P.release(obanks)
        qk_t = gt["Q"] + gt["K"]
        for vc in range(4):
            P.add("act", lambda e, vc=vc: e.activation(osq[:, vc, :], oraw[:, vc, :], AF.Square), reads=[gt["oraw"][vc]],
                  partial=qk_t if vc else (), writes=() if vc else qk_t)
        bkn, btn = P.bank()

        def mmn(e, bkn=bkn):
            ins = None
            for vc in range(4):
                ins = e.matmul(bkn[:], self.ones[:], osq[:, vc, :], start=(vc == 0), stop=(vc == 3))
            return ins
        P.add("pe", mmn, reads=qk_t + [self.ones_t], writes=[btn])
        P.add("act", lambda e, bkn=bkn: e.activation(self.rs_a[:], bkn[:], AF.Sqrt, scale=1.0 / 512.0, bias=EPS), reads=[btn], writes=[self.rs_at])
        P.add("dve", lambda e: e.reciprocal(self.rstd[:], self.rs_a[:]), reads=[self.rs_at], writes=[self.rstd_t])
        onw = self.vec[pre + "gla_out_norm_w"]
        for vc in range(4):
            m = hd * 4 + vc
            i = vc % 2
            P.add("dve", lambda e, vc=vc, m=m, i=i: e.scalar_tensor_tensor(otmp[i], oraw[:, vc, :], onw[:, m:m + 1], self.rstd[:], ALU.mult, ALU.mult),
                  reads=[gt["oraw"][vc], self.rstd_t, self.vec_t], writes=[gt["ot"][i]])
            P.add("dve", lambda e, vc=vc, i=i: e.tensor_tensor(OG[:, vc, :], otmp[i], SR[:, vc, :], ALU.mult),
                  reads=[gt["ot"][i], gt["SR"][vc]], writes=[gt["OG"][vc]])
        for (wv, wt), s0 in zip(wouts, (0, 1024)):
            for mi in range(8):
                m = s0 // 128 + mi
                bk, bt = P.bank()

                def mmw(e, wv=wv, mi=mi, bk=bk):
                    ins = None
                    for k in range(4):
                        ins = e.matmul(bk[:], wv[:, k, mi * 128:(mi + 1) * 128], OG[:, k, :], start=(k == 0), stop=(k == 3))
                    return ins
                P.add("pe", mmw, reads=[wt] + gt["OG"], writes=[bt])
                P.add("dve", lambda e, m=m, bk=bk: e.tensor_tensor(self.x[:, m, :], self.x[:, m, :], bk[:], ALU.add),
                      reads=[bt, self.xt[m]], writes=[self.xt[m]])

    def build(self):
        P = self.P
        self.setup()
        n_out = 0
        for ti in range(self.n_tiles):
            modes = self.plan[ti]
            if not modes:
                continue
            self.fill_id = 0
            P.add("sp", lambda e, ti=ti: e.dma_start(out=self.x[:], in_=self.x_in[ti]), writes=self.xt, dma=True, sem_tile=self.t_x_dma)
            for l in self.layers:
                mode = modes.get(l)
                if mode is None:
                    continue
                kind = LAYER_KIND[l]
                if kind == "pool":
                    self.pool_mixer(l, ti, mode)
                elif kind == "sgu":
                    self.sgu_mixer(l)
                else:
                    self.gla_mixer(l, full=(mode == "full"), ti=ti)
                if mode == "full":
                    self.ffn(l)
            if ti in self.store_tiles:
                if self.final_norm:
                    self.rmsnorm(self.vec["final_norm_w"], self.x, self.xt)
                oi = n_out
                n_out += 1
                P.add("sp", lambda e, oi=oi: e.dma_start(out=self.y_out[oi], in_=self.x[:]), reads=self.xt, writes=[P.tile("yo%d" % oi)],
                      dma=True, sem_tile=self.t_x_dma)
        P.emit(final_wait_tiles=[self.t_x_dma])
        return self.nc


def fm_vec(v):
    return np.ascontiguousarray(np.asarray(v, np.float32).reshape(NC16, 128).T)


def tiles_fm(xs):
    n = xs.shape[0] // T
    return np.ascontiguousarray(xs.reshape(n, T, NC16, 128).transpose(0, 3, 2, 1))


def tiles_fm_inv(y):
    n = y.shape[0]
    return np.ascontiguousarray(y.transpose(0, 3, 2, 1).reshape(n * T, D))


def chunk_masks():
    s = np.arange(128)[:, None]
    t = np.arange(128)[None, :]
    same = (s // 64) == (t // 64)
    return ((same & (s <= t)).astype(np.float32), (same & (s > t)).astype(np.float32), (s <= t).astype(np.float32))


def invc_table(is_start):
    out = np.zeros((128, 4, 16), np.float32)
    for g in range(4):
        w = 2 ** (g + 1)
        tt = np.arange(16)
        cnt = np.minimum(tt + 1, w) if is_start else np.full(16, w)
        out[:, g, :] = (1.0 / cnt.astype(np.float32))[None, :]
    return out


def common_inputs(inputs, layers, final_norm=True):
    m = {}
    mL, mU, mT = chunk_masks()
    for l in layers:
        pre = "l%d_" % l
        k = LAYER_KIND[l]
        m["v_" + pre + "norm1"] = fm_vec(inputs[pre + "norm1"])
        m["v_" + pre + "norm2"] = fm_vec(inputs[pre + "norm2"])
        for nm in FFN_PARAMS[1:]:
            m[pre + nm] = np.ascontiguousarray(inputs[pre + nm], dtype=np.float32)
        if k == "pool":
            m["v_" + pre + "pool_scale"] = fm_vec(inputs[pre + "pool_scale"])
            for nm in ("pool_w_in", "pool_w_group", "pool_w_out"):
                m[pre + nm] = np.ascontiguousarray(inputs[pre + nm], dtype=np.float32)
        elif k == "sgu":
            for nm in ("sgu_w_in", "sgu_w_out"):
                m[pre + nm] = np.ascontiguousarray(inputs[pre + nm], dtype=np.float32)
            m["v_" + pre + "sgu_v_norm_w"] = fm_vec(inputs[pre + "sgu_v_norm_w"])
            m["v_" + pre + "sgu_v_norm_b"] = fm_vec(inputs[pre + "sgu_v_norm_b"])
            m[pre + "bsp_b"] = np.ascontiguousarray(np.broadcast_to(np.asarray(inputs[pre + "sgu_b_spatial"], np.float32)[None, :, :], (128, 8, 128)))
            m[pre + "ws_t"] = np.ascontiguousarray(np.asarray(inputs[pre + "sgu_w_spatial"], np.float32).transpose(2, 0, 1))
            m["mask_tri"] = mT
        elif k == "gla":
            m["v_" + pre + "gla_out_norm_w"] = fm_vec(inputs[pre + "gla_out_norm_w"])
            for nm in ("gla_w_in", "gla_w_out", "gla_w_gk_up"):
                m[pre + nm] = np.ascontiguousarray(inputs[pre + nm], dtype=np.float32)
            m[pre + "bgk_b"] = np.ascontiguousarray(np.broadcast_to(np.asarray(inputs[pre + "gla_b_gk"], np.float32)[None, :], (128, 1024)))
            m["mask_L"] = mL
            m["mask_U"] = mU
    if final_norm:
        m["v_final_norm_w"] = fm_vec(inputs["final_norm_w"])
    return m


N_PRE = 8
N_MAIN = 8
PAIR_GROUPS = [[0, 1], [2, 3], [4, 5], [6, 7]]


def full_plan():
    plan = []
    for ti in range(N_PRE + N_MAIN):
        if ti < N_PRE - 1:
            plan.append({0: "full", 1: "full", 2: "state"})
        elif ti == N_PRE - 1:
            plan.append({0: "full", 1: "full", 2: "full", 3: "z"})
        else:
            plan.append({0: "full", 1: "full", 2: "full", 3: "full"})
    return plan


ALL_INPUTS = (
    "x",
    "l0_norm1", "l0_pool_w_in", "l0_pool_w_group", "l0_pool_scale", "l0_pool_w_out",
    "l0_norm2", "l0_ffn_w_gate", "l0_ffn_w_up", "l0_ffn_w_down",
    "l1_norm1", "l1_sgu_w_in", "l1_sgu_v_norm_w", "l1_sgu_v_norm_b", "l1_sgu_w_spatial", "l1_sgu_b_spatial", "l1_sgu_w_out",
    "l1_norm2", "l1_ffn_w_gate", "l1_ffn_w_up", "l1_ffn_w_down",
    "l2_norm1", "l2_gla_w_in", "l2_gla_w_gk_up", "l2_gla_b_gk", "l2_gla_out_norm_w", "l2_gla_w_out",
    "l2_norm2", "l2_ffn_w_gate", "l2_ffn_w_up", "l2_ffn_w_down",
    "l3_norm1", "l3_pool_w_in", "l3_pool_w_group", "l3_pool_scale", "l3_pool_w_out",
    "l3_norm2", "l3_ffn_w_gate", "l3_ffn_w_up", "l3_ffn_w_down",
    "final_norm_w",
)


def kernel(**inputs):
    missing = [n for n in ALL_INPUTS if n not in inputs]
    assert not missing, missing
    x = np.asarray(inputs["x"], np.float32)
    B, S, _ = x.shape
    layers = [0, 1, 2, 3]
    NT = 8
    plan = [{l: "full" for l in layers} for _ in range(NT)]
    b = Builder(layers, NT, plan, list(range(NT)), [0], pair=True)
    nc = b.build()
    common = common_inputs(inputs, layers)
    in_maps = []
    for core in range(8):
        bi, r = core // 2, core % 2
        m = dict(common)
        xt_all = tiles_fm(x[bi])
        m["x_in"] = np.ascontiguousarray(xt_all[r::2])
        m["invc"] = np.ascontiguousarray(np.stack([invc_table(r == 0)], axis=1))
        fl = np.zeros((128, 2), np.float32)
        fl[:, r] = 1.0
        m["flags"] = fl
        in_maps.append(m)
    res = run_bass_kernel_spmd(nc, in_maps, core_ids=list(range(8)))
    out = np.zeros((B, S, D), np.float32)
    for core in range(8):
        bi, r = core // 2, core % 2
        y = res.results[core]["y_out"]
        o4 = out[bi].reshape(S // T, T, D)
        for k in range(NT):
            o4[2 * k + r] = tiles_fm_inv(y[k:k + 1])
    return out
```
